# Optimizing a Trainium2 kernel written in Bass

```python
import jax, jax.numpy as jnp
from jax import lax
import numpy as np

D_MODEL = 2048
BATCH = 2
SEQ = 4096
DEPTH = 1

CHUNK = 64
Q_BLOCK = 128
EPS = 1e-6
MLA_HEADS = 8
MLA_Q_RANK = 512
MLA_KV_RANK = 512
MLA_NOPE = 128
MLA_ROPE = 64
MLA_V = 128
ROPE_THETA = 10000.0
MLA_WIDTH = MLA_HEADS * MLA_V
GLA_HEADS = 4
GLA_DK = 128
GLA_DV = 256
GLA_GATE_RANK = 16
GLA_GATE_NORMALIZER = 16.0
GLA_KEY = GLA_HEADS * GLA_DK
GLA_VAL = GLA_HEADS * GLA_DV
D_FF = 5632
N_MOD = 9
IN_SPLITS = (MLA_Q_RANK, MLA_KV_RANK, MLA_ROPE,
             GLA_KEY, GLA_KEY, GLA_VAL,
             GLA_GATE_RANK, GLA_VAL,
             D_MODEL, D_MODEL)
D_IN = sum(IN_SPLITS)

kernel_name = 'hybrid_mla_gla_macaron_adaln'


def _rms_norm(t, g):
    tf = t.astype(jnp.float32)
    y = tf * lax.rsqrt(jnp.mean(tf * tf, axis=-1, keepdims=True) + EPS)
    return (y * g.astype(jnp.float32)).astype(t.dtype)


def _split_cols(t, sizes):
    out, start = [], 0
    for s in sizes:
        out.append(t[..., start:start + s])
        start += s
    return out


def _rope(t, cos, sin):
    half = t.shape[-1] // 2
    tf = t.astype(jnp.float32)
    t1, t2 = tf[..., :half], tf[..., half:]
    return jnp.concatenate([t1 * cos - t2 * sin, t2 * cos + t1 * sin], axis=-1).astype(t.dtype)


def _swiglu(h, w1, w3, w2):
    return (jax.nn.silu(h @ w1) * (h @ w3)) @ w2


def _mla_branch(q_lat, kv_lat, k_rope_raw, cos, sin, g_q_lat, w_uq, g_qn, g_qr,
                g_kv_lat, w_ukv, g_kn, g_kr):
    B, S, _ = q_lat.shape
    H = MLA_HEADS
    cq = _rms_norm(q_lat, g_q_lat)
    q = (cq @ w_uq).reshape(B, S, H, MLA_NOPE + MLA_ROPE)
    q_nope = _rms_norm(q[..., :MLA_NOPE], g_qn)
    q_pe = _rope(_rms_norm(q[..., MLA_NOPE:], g_qr), cos[:, :, None, :], sin[:, :, None, :])
    ckv = _rms_norm(kv_lat, g_kv_lat)
    kv = (ckv @ w_ukv).reshape(B, S, H, MLA_NOPE + MLA_V)
    k_nope = _rms_norm(kv[..., :MLA_NOPE], g_kn)
    v = kv[..., MLA_NOPE:]
    k_pe = _rope(_rms_norm(k_rope_raw, g_kr), cos, sin)
    k_pe = jnp.broadcast_to(k_pe[:, :, None, :], (B, S, H, MLA_ROPE))
    qf = jnp.concatenate([q_nope, q_pe], axis=-1).transpose(0, 2, 1, 3)
    kf = jnp.concatenate([k_nope, k_pe], axis=-1).transpose(0, 2, 1, 3)
    vf = v.transpose(0, 2, 1, 3)
    scale = (MLA_NOPE + MLA_ROPE) ** -0.5
    n_blocks = S // Q_BLOCK
    q_blocks = qf.reshape(B, H, n_blocks, Q_BLOCK, -1).transpose(2, 0, 1, 3, 4)
    key_chunk = jnp.arange(S) // CHUNK

    def attend(args):
        qb, start = args
        s = jnp.einsum('bhqd,bhkd->bhqk', qb, kf).astype(jnp.float32) * scale
        q_chunk = (start + jnp.arange(Q_BLOCK)) // CHUNK
        mask = key_chunk[None, :] <= q_chunk[:, None]
        s = jnp.where(mask, s, -jnp.inf)
        p = jax.nn.softmax(s, axis=-1).astype(vf.dtype)
        return jnp.einsum('bhqk,bhkd->bhqd', p, vf)

    out = lax.map(attend, (q_blocks, jnp.arange(n_blocks) * Q_BLOCK))
    return out.transpose(1, 0, 3, 2, 4).reshape(B, S, MLA_WIDTH)


def _gla_branch(q, k, v, g_lr, g_out, w_gk_up, b_gk, g_gla):
    B, S, _ = q.shape
    H, N = GLA_HEADS, S // CHUNK
    f32 = jnp.float32
    log_a = jax.nn.log_sigmoid((g_lr @ w_gk_up + b_gk).astype(f32)) / GLA_GATE_NORMALIZER

    def heads(t, d):
        return t.astype(f32).reshape(B, N, CHUNK, H, d).transpose(0, 3, 1, 2, 4)

    qh = heads(q, GLA_DK) * (GLA_DK ** -0.5)
    kh = heads(k, GLA_DK)
    vh = heads(v, GLA_DV)
    b = jnp.cumsum(heads(log_a, GLA_DK), axis=3)
    q_dec = qh * jnp.exp(b)
    k_dec = kh * jnp.exp(-b)
    causal = jnp.tril(jnp.ones((CHUNK, CHUNK), dtype=bool))
    attn = jnp.where(causal, jnp.einsum('bhncd,bhnsd->bhncs', q_dec, k_dec), 0.0)
    o_intra = jnp.einsum('bhncs,bhnsv->bhncv', attn, vh)
    b_last = b[:, :, :, -1:, :]
    chunk_kv = jnp.einsum('bhncd,bhncv->bhndv', kh * jnp.exp(b_last - b), vh)
    decay = jnp.exp(b_last[:, :, :, 0, :])

    def step(state, inp):
        dec, kv = inp
        return state * dec[..., None] + kv, state

    init = jnp.zeros((B, H, GLA_DK, GLA_DV), f32)
    _, states = lax.scan(step, init, (decay.transpose(2, 0, 1, 3), chunk_kv.transpose(2, 0, 1, 3, 4)))
    states = states.transpose(1, 2, 0, 3, 4)
    o = o_intra + jnp.einsum('bhncd,bhndv->bhncv', q_dec, states)
    o = o.transpose(0, 2, 3, 1, 4).reshape(B, S, H, GLA_DV)
    o = _rms_norm(o, g_gla) * jax.nn.silu(g_out.astype(f32).reshape(B, S, H, GLA_DV))
    return o.reshape(B, S, GLA_VAL).astype(q.dtype)


def _dense(k, shape, fan_in, scale=1.0):
    return jax.random.normal(k, shape, jnp.float32) * (scale * fan_in ** -0.5)


def _gain(k, shape):
    return 1.0 + 0.05 * jax.random.normal(k, shape, jnp.float32)


def setup_inputs(seed: int = 0) -> dict:
    key = jax.random.key(seed)
    ks = iter(jax.random.split(key, 32))
    L = DEPTH
    x = jax.random.normal(next(ks), (BATCH, SEQ, D_MODEL), jnp.float32)
    c = jax.random.normal(next(ks), (BATCH, D_MODEL), jnp.float32)
    offsets = jax.random.randint(next(ks), (BATCH, 1), 0, 64, dtype=jnp.int32) * CHUNK
    positions = (offsets + jnp.arange(SEQ, dtype=jnp.int32)[None, :]).astype(jnp.int32)
    return {
        'x': x, 'c': c, 'positions': positions,
        'w_ada': _dense(next(ks), (L, D_MODEL, N_MOD * D_MODEL), D_MODEL, 0.5),
        'b_ada': 0.02 * jax.random.normal(next(ks), (L, N_MOD * D_MODEL), jnp.float32),
        'g_ffn1': _gain(next(ks), (L, D_MODEL)),
        'w1_a': _dense(next(ks), (L, D_MODEL, D_FF), D_MODEL),
        'w3_a': _dense(next(ks), (L, D_MODEL, D_FF), D_MODEL),
        'w2_a': _dense(next(ks), (L, D_FF, D_MODEL), D_FF),
        'g_mix': _gain(next(ks), (L, D_MODEL)),
        'w_in': _dense(next(ks), (L, D_MODEL, D_IN), D_MODEL),
        'g_q_lat': _gain(next(ks), (L, MLA_Q_RANK)),
        'w_uq': _dense(next(ks), (L, MLA_Q_RANK, MLA_HEADS * (MLA_NOPE + MLA_ROPE)), MLA_Q_RANK),
        'g_qn': _gain(next(ks), (L, MLA_NOPE)),
        'g_qr': _gain(next(ks), (L, MLA_ROPE)),
        'g_kv_lat': _gain(next(ks), (L, MLA_KV_RANK)),
        'w_ukv': _dense(next(ks), (L, MLA_KV_RANK, MLA_HEADS * (MLA_NOPE + MLA_V)), MLA_KV_RANK),
        'g_kn': _gain(next(ks), (L, MLA_NOPE)),
        'g_kr': _gain(next(ks), (L, MLA_ROPE)),
        'w_gk_up': _dense(next(ks), (L, GLA_GATE_RANK, GLA_KEY), GLA_GATE_RANK),
        'b_gk': 0.1 * jax.random.normal(next(ks), (L, GLA_KEY), jnp.float32),
        'g_gla': _gain(next(ks), (L, GLA_DV)),
        'w_proj_a': _dense(next(ks), (L, MLA_WIDTH, D_MODEL), MLA_WIDTH),
        'w_proj_b': _dense(next(ks), (L, GLA_VAL, D_MODEL), GLA_VAL),
        'w_out': _dense(next(ks), (L, D_MODEL, D_MODEL), D_MODEL),
        'g_ffn2': _gain(next(ks), (L, D_MODEL)),
        'w1_b': _dense(next(ks), (L, D_MODEL, D_FF), D_MODEL),
        'w3_b': _dense(next(ks), (L, D_MODEL, D_FF), D_MODEL),
        'w2_b': _dense(next(ks), (L, D_FF, D_MODEL), D_FF),
        'g_final': _gain(next(ks), (L, D_MODEL)),
    }


def reference(x, c, positions, w_ada, b_ada, g_ffn1, w1_a, w3_a, w2_a, g_mix, w_in,
              g_q_lat, w_uq, g_qn, g_qr, g_kv_lat, w_ukv, g_kn, g_kr, w_gk_up, b_gk, g_gla,
              w_proj_a, w_proj_b, w_out, g_ffn2, w1_b, w3_b, w2_b, g_final):
    B, S, D = x.shape
    inv_freq = ROPE_THETA ** (-jnp.arange(0, MLA_ROPE, 2, dtype=jnp.float32) / MLA_ROPE)
    ang = positions.astype(jnp.float32)[..., None] * inv_freq
    cos, sin = jnp.cos(ang), jnp.sin(ang)
    for i in range(DEPTH):
        mod = (jax.nn.silu(c) @ w_ada[i] + b_ada[i]).reshape(B, N_MOD, D)
        sh1, sc1, ga1, sh2, sc2, ga2, sh3, sc3, ga3 = [mod[:, j, None, :] for j in range(N_MOD)]
        h = _rms_norm(x, g_ffn1[i]) * (1.0 + sc1) + sh1
        x = x + 0.5 * ga1 * _swiglu(h, w1_a[i], w3_a[i], w2_a[i])
        h = _rms_norm(x, g_mix[i]) * (1.0 + sc2) + sh2
        (q_lat, kv_lat, k_rope_raw, gq, gk, gv, g_lr, g_out, gate_a, gate_b) = _split_cols(h @ w_in[i], IN_SPLITS)
        ya = _mla_branch(q_lat, kv_lat, k_rope_raw, cos, sin, g_q_lat[i], w_uq[i], g_qn[i], g_qr[i],
                         g_kv_lat[i], w_ukv[i], g_kn[i], g_kr[i]) @ w_proj_a[i]
        yb = _gla_branch(gq, gk, gv, g_lr, g_out, w_gk_up[i], b_gk[i], g_gla[i]) @ w_proj_b[i]
        merged = jax.nn.sigmoid(gate_a) * ya + jax.nn.sigmoid(gate_b) * yb
        x = x + ga2 * (merged @ w_out[i])
        h = _rms_norm(x, g_ffn2[i]) * (1.0 + sc3) + sh3
        x = x + 0.5 * ga3 * _swiglu(h, w1_b[i], w3_b[i], w2_b[i])
        x = _rms_norm(x, g_final[i])
    return x
```

```python
import numpy as np
import concourse.bass as bass
import concourse.mybir as mybir
from concourse.bass_utils import run_bass_kernel_spmd
from contextlib import ExitStack

F32 = mybir.dt.float32
BF16 = mybir.dt.bfloat16
I32 = mybir.dt.int32
AF = mybir.ActivationFunctionType
ALU = mybir.AluOpType

PE, ACT, DVE, POOL, SP = "tensor", "scalar", "vector", "gpsimd", "sync"
ENGS = (PE, ACT, DVE, POOL, SP)

D_MODEL = 2048
T = 1024
NKC = 16
D_FF = 5632
NM = 44
EPS = 1e-6
NS = 5
SLOT_E = 4096
KV_ROWS = 2112
GROUPS = [[0, 1, 2, 3], [4, 5, 6, 7]]
DEBUG = False
STOP = 99


class Dep:
    __slots__ = ("w", "r")

    def __init__(self):
        self.w = None
        self.r = {}


class Prog:
    def __init__(self):
        self.q = {e: [] for e in ENGS}
        self.cnt = {e: 0 for e in ENGS}
        self.sem = {}
        self.seen = {e: {} for e in ENGS}
        self.dsem = {}
        self.pending = []
        self.stopped = False

    def _wait(self, eng, tok):
        key, handle, val = tok
        if key == PE and eng == PE:
            return
        if self.seen[eng].get(key, 0) >= val:
            return
        self.seen[eng][key] = val
        self.q[eng].append(lambda h, handle=handle, val=val: h.wait_ge(handle, val))

    def _collect(self, reads, writes, extra):
        toks = []
        for d in reads:
            if d.w is not None:
                toks.append(d.w)
        for d in writes:
            if d.w is not None:
                toks.append(d.w)
            toks.extend(d.r.values())
        toks.extend(extra)
        return toks

    def _mark(self, tok, reads, writes):
        for d in reads:
            old = d.r.get(tok[0])
            if old is None or old[2] < tok[2]:
                d.r[tok[0]] = tok
        for d in writes:
            d.w = tok
            d.r = {}

    def op(self, eng, fn, reads=(), writes=(), extra=()):
        if self.stopped:
            return None
        for tok in self._collect(reads, writes, extra):
            self._wait(eng, tok)
        self.cnt[eng] += 1
        sem = self.sem[eng]
        self.q[eng].append(lambda h, fn=fn, sem=sem: fn(h).then_inc(sem, 1))
        tok = (eng, sem, self.cnt[eng])
        self._mark(tok, reads, writes)
        return tok

    def dma(self, eng, semname, fn, reads=(), writes=(), extra=(), inc=16, track=True):
        if self.stopped:
            return None
        handle, issued = self.dsem[semname]
        key = "dma_" + semname
        toks = self._collect(reads, writes, extra)
        if issued > 0:
            toks.append((key, handle, issued))
        for tok in toks:
            self._wait(eng, tok)
        issued += inc
        self.dsem[semname][1] = issued
        self.q[eng].append(lambda h, fn=fn, handle=handle, inc=inc: fn(h).then_inc(handle, inc))
        tok = (key, handle, issued)
        self._mark(tok, reads, writes)
        if track:
            self.pending.append(tok)
        return tok

    def barrier(self):
        if self.stopped:
            return
        toks = [(e, self.sem[e], self.cnt[e]) for e in (PE, ACT, DVE) if self.cnt[e] > 0]
        toks += self.pending
        self.pending = []
        for e in (PE, ACT, DVE, SP):
            for tok in toks:
                if tok[0] != e:
                    self._wait(e, tok)

    def emit(self, block):
        def run(eng):
            def body(h):
                for fn in self.q[eng]:
                    fn(h)
            return body
        block.tensor(run(PE))
        block.scalar(run(ACT))
        block.vector(run(DVE))
        block.gpsimd(run(POOL))
        block.sync(run(SP))


class Bump:
    def __init__(self, tensor, nbytes):
        self.t = tensor
        self.n = nbytes
        self.off = 0

    def reset(self, off=0):
        self.off = off

    def alloc(self, shape, dt):
        esz = 4 if dt in (F32, I32) else 2
        n = int(np.prod(shape)) * esz
        n = (n + 31) // 32 * 32
        assert self.off + n <= self.n, ("arena overflow", self.off, n, self.n)
        w0 = self.off // 4
        ap = self.t[:, w0:w0 + n // 4]
        if dt != F32:
            ap = ap.bitcast(dt)
        ap = ap[:, 0:int(np.prod(shape))]
        self.off += n
        if len(shape) == 2:
            return ap.rearrange("p (a b) -> p a b", b=shape[1])
        if len(shape) == 3:
            return ap.rearrange("p (a b c) -> p a b c", b=shape[1], c=shape[2])
        return ap


def build_program():
    nc = bass.Bass("TRN2", target_bir_lowering=False)
    P = Prog()
    DI = {}

    SHAPES = {
        "xT": ([D_MODEL, T], F32), "cT": ([128, 16], F32), "pos": ([1, T], I32), "bada": ([128, 36], F32),
        "gains": ([128, 96], F32), "bgk": ([1, 512], F32), "wgk": ([16, 512], F32), "amask": ([128, 8], F32),
        "umask": ([128, 128], F32), "m2mask": ([128, 128], F32), "wada": ([18, 128, 4096], F32),
        "w13a": ([NM, 128, 4096], F32), "w13b": ([NM, 128, 4096], F32), "w2a": ([22, 128, 4096], F32), "w2b": ([22, 128, 4096], F32),
        "win_lat": ([4, 128, 4096], F32), "win_kr": ([1, 128, 2048], F32), "win_gqk": ([4, 128, 4096], F32),
        "win_gv": ([4, 128, 4096], F32), "win_glr": ([1, 128, 2048], F32), "win_gout": ([4, 128, 4096], F32),
        "win_gab": ([16, 128, 4096], F32), "wuq": ([2, 128, 4096], F32), "wukv": ([2, 128, 4096], F32),
        "wpa": ([4, 128, 4096], F32), "wpb": ([4, 128, 4096], F32), "wout": ([8, 128, 4096], F32),
    }

    class _Lazy:
        def __init__(self, name):
            self.name = name

        def _ap(self):
            if self.name not in DI:
                shape, dt = SHAPES[self.name]
                DI[self.name] = nc.dram_tensor(self.name, list(shape), dt, kind="ExternalInput").ap()
            return DI[self.name]

        def __getitem__(self, k):
            if P.stopped:
                return None
            return self._ap()[k]

        def rearrange(self, *a, **k):
            if P.stopped:
                return None
            return self._ap().rearrange(*a, **k)

        def partition_broadcast(self, n):
            if P.stopped:
                return None
            return self._ap().partition_broadcast(n)

    xT_d, cT_d, pos_d, bada_d, gains_d = _Lazy("xT"), _Lazy("cT"), _Lazy("pos"), _Lazy("bada"), _Lazy("gains")
    bgk_d, wgk_d, amask_d, umask_d, m2mask_d = _Lazy("bgk"), _Lazy("wgk"), _Lazy("amask"), _Lazy("umask"), _Lazy("m2mask")
    wada_d = _Lazy("wada")
    w13_d = [_Lazy("w13a"), _Lazy("w13b")]
    w2_d = [_Lazy("w2a"), _Lazy("w2b")]
    win_lat_d, win_kr_d, win_gqk_d, win_gv_d = _Lazy("win_lat"), _Lazy("win_kr"), _Lazy("win_gqk"), _Lazy("win_gv")
    win_glr_d, win_gout_d, win_gab_d = _Lazy("win_glr"), _Lazy("win_gout"), _Lazy("win_gab")
    wuq_d, wukv_d, wpa_d, wpb_d, wout_d = _Lazy("wuq"), _Lazy("wukv"), _Lazy("wpa"), _Lazy("wpb"), _Lazy("wout")
    outT_d = nc.dram_tensor("outT", [D_MODEL, T], F32, kind="ExternalOutput").ap()
    xs_d = nc.dram_tensor("xs", [D_MODEL, T], F32).ap()
    modb_d = nc.dram_tensor("modb", [128, 36], F32).ap()
    modg_d = nc.dram_tensor("modg", [512, 36], F32).ap()
    kb_k = [nc.dram_tensor(f"kb_k{i}", [512, T], BF16).ap() for i in range(2)]
    kg_k = [nc.dram_tensor(f"kg_k{i}", [2048, T], BF16).ap() for i in range(2)]
    kb_v = [nc.dram_tensor(f"kb_v{i}", [512, T], BF16).ap() for i in range(2)]
    kg_v = [nc.dram_tensor(f"kg_v{i}", [2048, T], BF16).ap() for i in range(2)]
    kb_pe = nc.dram_tensor("kb_pe", [64, T], BF16).ap()
    kg_pe = nc.dram_tensor("kg_pe", [256, T], BF16).ap()
    glb_d = nc.dram_tensor("glb", [128, 1028], F32).ap()
    glg_d = nc.dram_tensor("glg", [512, 1028], F32).ap()
    dbg_list = []

    with ExitStack() as es:
        def sbt(name, shape, dt):
            return es.enter_context(nc.sbuf_tensor(name, shape, dt))
        slots = sbt("slots", [128, NS, SLOT_E], BF16)
        xb = sbt("xb", [128, NKC, T], F32)
        hb = sbt("hb", [128, NKC, T], BF16)
        cst = sbt("cst", [128, 2048], F32)
        ARENA_B = 52 * 1024
        arena_t = sbt("arena", [128, ARENA_B // 4], F32)
        PS = [es.enter_context(nc.psum_tensor(f"ps{i}", [128, 512], F32)) for i in range(8)]
        for e in ENGS:
            P.sem[e] = es.enter_context(nc.semaphore("s_" + e))
        dnames = [f"w{i}" for i in range(NS)] + ["ld0", "ld1", "ld2", "ld3", "st0", "st1", "st2", "kv0", "kv1", "cc0", "cc1", "cc2", "cc3", "cc4", "cc5", "cc6", "xio"]
        for nm in dnames:
            P.dsem[nm] = [es.enter_context(nc.semaphore("d_" + nm)), 0]
        block = es.enter_context(nc.Block())

        AR = Bump(arena_t, ARENA_B)
        XA = Bump(xb[:].rearrange("p a b -> p (a b)"), NKC * T * 4)
        CS = Bump(cst, 2048 * 4)

        sd = [Dep() for _ in range(NS)]
        xd = [[Dep(), Dep()] for _ in range(NKC)]
        hd = [[Dep(), Dep()] for _ in range(NKC)]
        psd = [Dep() for _ in range(8)]
        xall = [d for r in xd for d in r]
        hall = [d for r in hd for d in r]

        def act(out, in_, func, reads, writes, **kw):
            return P.op(ACT, lambda h: h.activation(out=out, in_=in_, func=func, **kw), reads=reads, writes=writes)

        def vtt(out, in0, in1, op, reads, writes, eng=DVE):
            return P.op(eng, lambda h: h.tensor_tensor(out=out, in0=in0, in1=in1, op=op), reads=reads, writes=writes)

        def vts(out, in0, s1, s2, op0, op1, reads, writes, eng=DVE):
            if s2 is None:
                return P.op(eng, lambda h: h.tensor_scalar(out=out, in0=in0, scalar1=s1, scalar2=None, op0=op0), reads=reads, writes=writes)
            return P.op(eng, lambda h: h.tensor_scalar(out=out, in0=in0, scalar1=s1, scalar2=s2, op0=op0, op1=op1), reads=reads, writes=writes)

        def vstt(out, in0, scalar, in1, op0, op1, reads, writes, eng=DVE):
            return P.op(eng, lambda h: h.scalar_tensor_tensor(out=out, in0=in0, scalar=scalar, in1=in1, op0=op0, op1=op1), reads=reads, writes=writes)

        def vcopy(out, in_, reads, writes, eng=DVE):
            return P.op(eng, lambda h: h.tensor_copy(out=out, in_=in_), reads=reads, writes=writes)

        def vmemset(ap, val, writes, eng=DVE):
            return P.op(eng, lambda h: h.memset(ap, val), writes=writes)

        def vrecip(out, in_, reads, writes):
            return P.op(DVE, lambda h: h.reciprocal(out=out, in_=in_), reads=reads, writes=writes)

        def mmg(out_ap, pairs, reads, writes):
            n = len(pairs)

            def fn(h):
                ins = None
                for i, (l, r) in enumerate(pairs):
                    ins = h.matmul(out_ap, lhsT=l, rhs=r, start=(i == 0), stop=(i == n - 1))
                return ins
            return P.op(PE, fn, reads=reads, writes=writes)

        def mm1(out_ap, l, r, start, stop, reads, writes):
            return P.op(PE, lambda h: h.matmul(out_ap, lhsT=l, rhs=r, start=start, stop=stop), reads=reads, writes=writes)

        def ld(semname, out, in_, writes, reads=(), eng=SP):
            return P.dma(eng, semname, lambda h: h.dma_start(out=out, in_=in_), reads=reads, writes=writes)

        wstate = {"u": 0}

        def wload(src, E):
            s = wstate["u"] % NS
            wstate["u"] += 1
            P.dma(POOL, f"w{s}", lambda h: h.dma_start(out=slots[:, s, 0:E], in_=src), writes=[sd[s]], track=False)
            return s

        def coll(semname, in_ap, out_ap, reads, writes):
            return P.dma(POOL, semname, lambda h: h.collective_compute(
                "AllGather", ALU.bypass, replica_groups=GROUPS, ins=[in_ap], outs=[out_ap]),
                reads=reads, writes=writes, inc=1)

        def checkpoint(k):
            if STOP == k and not P.stopped:
                P.barrier()
                P.stopped = True

        psrr = {"i": 0}

        def nextps(pool):
            b = pool[psrr["i"] % len(pool)]
            psrr["i"] += 1
            return b

        def dbg(name, ap, shape, reads):
            if not DEBUG or P.stopped:
                return
            d = nc.dram_tensor("dbg_" + name, list(shape), ap.dtype, kind="ExternalOutput").ap()
            dbg_list.append("dbg_" + name)
            P.dma(SP, "xio", lambda h: h.dma_start(out=d, in_=ap), reads=reads)

        GT = CS.alloc([96], F32)
        MOD = CS.alloc([144], F32)
        AB = CS.alloc([3, 16], F32)
        GG = CS.alloc([3, 16], F32)
        BADA = CS.alloc([36], F32)
        MODS = CS.alloc([36], F32)
        AMASK = CS.alloc([8], F32)
        CTT = CS.alloc([16], F32)
        SCB = CS.alloc([16], BF16)
        ONES = CS.alloc([128], BF16)
        ON2048 = CS.alloc([128], BF16)
        ON512 = CS.alloc([128], BF16)
        ON256 = CS.alloc([128], BF16)
        ON128 = CS.alloc([128], BF16)
        ON64 = CS.alloc([128], BF16)
        U01 = CS.alloc([128], F32)
        US = CS.alloc([128], F32)
        M2S = CS.alloc([128], F32)
        DECAY = CS.alloc([4, 16], F32)
        GSC = CS.alloc([4], F32)
        dconst = Dep()
        ddec = Dep()
        dmod = Dep()
        dgt = Dep()
        ld("ld0", GT, gains_d[:, :], [dgt])
        ld("ld1", CTT, cT_d[:, :], [dconst])
        ld("ld2", BADA, bada_d[:, :], [dconst])
        ld("ld3", AMASK, amask_d[:, :], [dconst])
        ld("ld0", U01, umask_d[:, :], [dconst])
        ld("ld1", M2S, m2mask_d[:, :], [dconst])
        for ap_, v in ((ONES, 1.0), (ON2048, 1.0 / 2048), (ON512, 1.0 / 512), (ON256, 1.0 / 256), (ON128, 1.0 / 128), (ON64, 1.0 / 64)):
            vmemset(ap_, v, [dconst])
        vts(US, U01, -1.0 / 16, None, ALU.mult, None, [dconst], [dconst])
        vts(M2S, M2S, -1.0 / 16, None, ALU.mult, None, [dconst], [dconst])

        xv = xT_d.rearrange("(k p) t -> p k t", p=128)
        for kc in range(0, NKC, 4):
            ld("xio", xb[:, kc:kc + 4, :], xv[:, kc:kc + 4, :], [d for k in range(kc, kc + 4) for d in xd[k]])

        act(SCB, CTT, AF.Silu, [dconst], [dconst])
        modps = PS[0]
        for j in range(18):
            s = wload(wada_d[j], 4096)
            for c in range(2):
                m = 2 * j + c
                mmg(modps[:, m:m + 1], [(slots[:, s, k * 256 + c * 128:k * 256 + c * 128 + 128], SCB[:, k:k + 1]) for k in range(16)],
                    [sd[s], dconst], [psd[0]])
        vtt(MODS, modps[:, 0:36], BADA, ALU.add, [psd[0], dconst], [dmod])
        dmodb, dmodg = Dep(), Dep()
        P.dma(SP, "st0", lambda h: h.dma_start(out=modb_d[:, :], in_=MODS), reads=[dmod], writes=[dmodb])
        coll("cc0", modb_d, modg_d, [dmodb], [dmodg])
        ld("ld2", MOD.rearrange("p (r m) -> p r m", r=4), modg_d.rearrange("(r p) m -> p r m", p=128), [dmod], reads=[dmodg])
        gain_col = [0, 16, 32]
        for i in range(3):
            vstt(AB[:, i, :], MOD[:, (3 * i + 1) * 16:(3 * i + 2) * 16], 1.0, GT[:, gain_col[i]:gain_col[i] + 16], ALU.add, ALU.mult,
                 [dmod, dgt], [dmod])
            vts(GG[:, i, :], MOD[:, (3 * i + 2) * 16:(3 * i + 3) * 16], 0.5 if i != 1 else 1.0, None, ALU.mult, None, [dmod], [dmod])

        checkpoint(0)

        def rstd_from(ss_ps, out_ap, ssdep, outdep, npart=128):
            act(out_ap, ss_ps, AF.Sqrt, [ssdep], [outdep], bias=EPS)
            vrecip(out_ap, out_ap, [outdep], [outdep])

        def modnorm(Acol, Bcol, ar, out_fp32_inplace=False):
            sq = ar.alloc([2, 512], BF16)
            tmp = ar.alloc([2, 512], F32)
            rstd = ar.alloc([2, 512], F32)
            dsq = [Dep(), Dep()]
            dtmp = [Dep(), Dep()]
            drs = [Dep(), Dep()]
            for tt in range(2):
                ts_ = slice(tt * 512, (tt + 1) * 512)
                ssb = 6 + tt
                for kc in range(NKC):
                    b = kc % 2
                    act(sq[:, b, :], xb[:, kc, ts_], AF.Square, [xd[kc][tt]], [dsq[b]])
                    mm1(PS[ssb][:], ON2048, sq[:, b, :], kc == 0, kc == NKC - 1, [dsq[b], dconst], [psd[ssb]])
                rstd_from(PS[ssb][:], rstd[:, tt, :], psd[ssb], drs[tt])
                for kc in range(NKC):
                    b = kc % 2
                    if out_fp32_inplace:
                        vstt(xb[:, kc, ts_], xb[:, kc, ts_], Acol(kc), rstd[:, tt, :], ALU.mult, ALU.mult,
                             [xd[kc][tt], drs[tt], dmod, dgt], [xd[kc][tt]])
                    else:
                        vstt(tmp[:, b, :], xb[:, kc, ts_], Acol(kc), rstd[:, tt, :], ALU.mult, ALU.mult,
                             [xd[kc][tt], drs[tt], dmod], [dtmp[b]])
                        act(hb[:, kc, ts_], tmp[:, b, :], AF.Identity, [dtmp[b], dmod], [hd[kc][tt]], bias=Bcol(kc))

        def ffn(i, li):
            AR.reset()
            modnorm(lambda kc: AB[:, li, kc:kc + 1], lambda kc: MOD[:, 3 * li * 16 + kc:3 * li * 16 + kc + 1], AR)
            g = AR.alloc([4, T], BF16)
            sil = AR.alloc([2, 512], F32)
            dg = [[Dep(), Dep()] for _ in range(4)]
            dsil = [Dep(), Dep()]
            up_pool = [0, 1, 2, 3]
            dn_pool = [4, 5, 6, 7]
            cnt = 0
            for G in range(11):
                for mi in range(4):
                    m = 4 * G + mi
                    s = wload(w13_d[i][m], 4096)
                    for tt in range(2):
                        ts_ = slice(tt * 512, (tt + 1) * 512)
                        b1 = up_pool[(cnt * 2) % 4]
                        b3 = up_pool[(cnt * 2 + 1) % 4]
                        cnt += 1
                        mmg(PS[b1][:], [(slots[:, s, k * 128:(k + 1) * 128], hb[:, k, ts_]) for k in range(NKC)],
                            [sd[s]] + [hd[k][tt] for k in range(NKC)], [psd[b1]])
                        mmg(PS[b3][:], [(slots[:, s, 2048 + k * 128:2048 + (k + 1) * 128], hb[:, k, ts_]) for k in range(NKC)],
                            [sd[s]] + [hd[k][tt] for k in range(NKC)], [psd[b3]])
                        sb_ = cnt % 2
                        act(sil[:, sb_, :], PS[b1][:], AF.Silu, [psd[b1]], [dsil[sb_]])
                        vtt(g[:, mi, ts_], sil[:, sb_, :], PS[b3][:], ALU.mult, [dsil[sb_], psd[b3]], [dg[mi][tt]])
                s2 = [wload(w2_d[i][2 * G + u], 4096) for u in range(2)]
                for n in range(NKC):
                    for tt in range(2):
                        ts_ = slice(tt * 512, (tt + 1) * 512)
                        b = nextps(dn_pool)
                        mmg(PS[b][:], [(slots[:, s2[mi // 2], (mi % 2) * 2048 + n * 128:(mi % 2) * 2048 + (n + 1) * 128], g[:, mi, ts_]) for mi in range(4)],
                            [sd[s2[0]], sd[s2[1]]] + [dg[mi][tt] for mi in range(4)], [psd[b]])
                        vstt(xb[:, n, ts_], PS[b][:], GG[:, li, n:n + 1], xb[:, n, ts_], ALU.mult, ALU.add,
                             [psd[b], xd[n][tt], dmod], [xd[n][tt]])
            P.barrier()

        ffn(0, 0)
        checkpoint(1)

        AR.reset()
        modnorm(lambda kc: AB[:, 1, kc:kc + 1], lambda kc: MOD[:, 3 * 16 + kc:3 * 16 + kc + 1], AR)
        P.barrier()
        dxs = Dep()
        for kc in range(0, NKC, 4):
            P.dma(SP, "xio", lambda h, kc=kc: h.dma_start(out=xs_d.rearrange("(k p) t -> p k t", p=128)[:, kc:kc + 4, :], in_=xb[:, kc:kc + 4, :]),
                  reads=[d for k in range(kc, kc + 4) for d in xd[k]], writes=[dxs])
        P.barrier()
        checkpoint(2)

        AR.reset()
        XA.reset()
        TCq = AR.alloc([1, T], F32)[:, 0, :]
        TSq = AR.alloc([1, T], F32)[:, 0, :]
        cq = AR.alloc([4, T], BF16)
        a_mark = AR.off
        TCk = XA.alloc([1, T], F32)[:, 0, :]
        TSk = XA.alloc([1, T], F32)[:, 0, :]
        posi = AR.alloc([1, T], I32)[:, 0, :]
        ang = AR.alloc([1, T], F32)[:, 0, :]
        kf = AR.alloc([1, T], F32)[:, 0, :]
        ki = AR.alloc([1, T], I32)[:, 0, :]
        r1 = AR.alloc([1, T], F32)[:, 0, :]
        r2 = AR.alloc([1, T], F32)[:, 0, :]
        sinv = AR.alloc([1, T], F32)[:, 0, :]
        cosv = AR.alloc([1, T], F32)[:, 0, :]
        drope = Dep()
        R = slice(0, 64)
        ld("ld3", posi[R], pos_d.partition_broadcast(64), [drope])
        vcopy(ang[R], posi[R], [drope], [drope])
        vts(ang[R], ang[R], GT[R, 81:82], None, ALU.mult, None, [drope, dgt], [drope])
        C1 = 6.28125
        C2 = float(2 * np.pi - 6.28125)
        vts(kf[R], ang[R], float(1.0 / (2 * np.pi)), None, ALU.mult, None, [drope], [drope])
        vcopy(ki[R], kf[R], [drope], [drope])
        vcopy(kf[R], ki[R], [drope], [drope])
        vstt(r1[R], kf[R], -C1, ang[R], ALU.mult, ALU.add, [drope], [drope])
        vstt(r1[R], kf[R], -C2, r1[R], ALU.mult, ALU.add, [drope], [drope])
        PI_SAFE = 3.1415925
        vts(r1[R], r1[R], PI_SAFE, -PI_SAFE, ALU.min, ALU.max, [drope], [drope])
        act(sinv[R], r1[R], AF.Sin, [drope], [drope])
        vts(r2[R], r1[R], float(np.pi / 2), None, ALU.add, None, [drope], [drope])
        vts(kf[R], r2[R], float(np.pi), None, ALU.is_gt, None, [drope], [drope])
        vstt(r2[R], kf[R], float(-2 * np.pi), r2[R], ALU.mult, ALU.add, [drope], [drope])
        vts(r2[R], r2[R], PI_SAFE, -PI_SAFE, ALU.min, ALU.max, [drope], [drope])
        act(cosv[R], r2[R], AF.Sin, [drope], [drope])
        vtt(GSC[R, 0:1], GT[R, 77:78], GT[R, 80:81], ALU.mult, [dgt], [drope])
        vtt(GSC[R, 1:2], GT[R, 79:80], GT[R, 80:81], ALU.mult, [dgt], [drope])
        vts(TCq[R], cosv[R], GT[R, 76:77], None, ALU.mult, None, [drope, dgt], [drope])
        vts(TSq[R], sinv[R], GSC[R, 0:1], None, ALU.mult, None, [drope], [drope])
        vts(TCk[R], cosv[R], GT[R, 78:79], None, ALU.mult, None, [drope, dgt], [drope])
        vts(TSk[R], sinv[R], GSC[R, 1:2], None, ALU.mult, None, [drope], [drope])

        P.barrier()
        AR.reset(a_mark)
        lat = AR.alloc([4, T], F32)
        kst = AR.alloc([8, T], BF16)
        ckv = XA.alloc([4, T], BF16)
        sq = XA.alloc([2, 512], BF16)
        rstd = XA.alloc([2, 512], F32)
        vst = XA.alloc([8, T], BF16)
        kraw = XA.alloc([1, T], F32)[:, 0, :]
        ksw = XA.alloc([1, T], F32)[:, 0, :]
        kpe = XA.alloc([1, T], BF16)[:, 0, :]
        t1 = XA.alloc([2, 512], F32)
        dlat = [[Dep(), Dep()] for _ in range(4)]
        dsq = [Dep(), Dep()]
        drs = [Dep(), Dep()]
        dcq = [[Dep(), Dep()] for _ in range(4)]
        dckv = [[Dep(), Dep()] for _ in range(4)]
        a_pool = [0, 1, 2, 3, 4, 5]
        sqc = 0
        for fam, (dst, ddst, gcol) in enumerate(((cq, dcq, 64), (ckv, dckv, 68))):
            for u in range(2):
                s = wload(win_lat_d[2 * fam + u], 4096)
                for c in range(2):
                    m = 2 * u + c
                    for tt in range(2):
                        ts_ = slice(tt * 512, (tt + 1) * 512)
                        b = nextps(a_pool)
                        mmg(PS[b][:], [(slots[:, s, k * 256 + c * 128:k * 256 + c * 128 + 128], hb[:, k, ts_]) for k in range(NKC)],
                            [sd[s]] + [hd[k][tt] for k in range(NKC)], [psd[b]])
                        act(lat[:, m, ts_], PS[b][:], AF.Copy, [psd[b]], [dlat[m][tt]])
                        sb_ = sqc % 2
                        sqc += 1
                        act(sq[:, sb_, :], PS[b][:], AF.Square, [psd[b]], [dsq[sb_]])
                        mm1(PS[6 + tt][:], ON512, sq[:, sb_, :], m == 0, m == 3, [dsq[sb_], dconst], [psd[6 + tt]])
            for tt in range(2):
                ts_ = slice(tt * 512, (tt + 1) * 512)
                rstd_from(PS[6 + tt][:], rstd[:, tt, :], psd[6 + tt], drs[tt])
                for m in range(4):
                    vstt(dst[:, m, ts_], lat[:, m, ts_], GT[:, gcol + m:gcol + m + 1], rstd[:, tt, :], ALU.mult, ALU.mult,
                         [dlat[m][tt], drs[tt], dgt], [ddst[m][tt]])
        dkr = Dep()
        s = wload(win_kr_d[0], 2048)
        for tt in range(2):
            ts_ = slice(tt * 512, (tt + 1) * 512)
            b0 = nextps(a_pool)
            mmg(PS[b0][0:64, :], [(slots[:, s, k * 128:k * 128 + 64], hb[:, k, ts_]) for k in range(NKC)],
                [sd[s]] + [hd[k][tt] for k in range(NKC)], [psd[b0]])
            b1 = nextps(a_pool)
            mmg(PS[b1][0:64, :], [(slots[:, s, k * 128 + 64:k * 128 + 128], hb[:, k, ts_]) for k in range(NKC)],
                [sd[s]] + [hd[k][tt] for k in range(NKC)], [psd[b1]])
            act(kraw[R, ts_], PS[b0][0:64, :], AF.Copy, [psd[b0]], [dkr])
            act(ksw[R, ts_], PS[b1][0:64, :], AF.Copy, [psd[b1]], [dkr])
            act(sq[R, 0, :], PS[b0][0:64, :], AF.Square, [psd[b0]], [dsq[0]])
            mm1(PS[6][0:64, :], ON64[R, 0:64], sq[R, 0, :], True, True, [dsq[0], dconst], [psd[6]])
            rstd_from(PS[6][0:64, :], rstd[R, 0, :], psd[6], drs[0])
            vtt(t1[R, 0, :], kraw[R, ts_], TCk[R, ts_], ALU.mult, [dkr, drope], [dkr])
            vtt(t1[R, 1, :], ksw[R, ts_], TSk[R, ts_], ALU.mult, [dkr, drope], [dkr])
            vtt(t1[R, 0, :], t1[R, 0, :], t1[R, 1, :], ALU.add, [dkr], [dkr])
            vtt(kpe[R, ts_], t1[R, 0, :], rstd[R, 0, :], ALU.mult, [dkr, drs[0]], [dkr])
        dkvb = Dep()
        P.dma(SP, "st0", lambda h: h.dma_start(out=kb_pe[:, :], in_=kpe[R, :]), reads=[dkr], writes=[dkvb])
        dkst = Dep()
        s = wload(wukv_d[0], 4096)
        for hh in range(8):
            for tt in range(2):
                ts_ = slice(tt * 512, (tt + 1) * 512)
                b = nextps(a_pool)
                mmg(PS[b][:], [(slots[:, s, k * 1024 + hh * 128:k * 1024 + hh * 128 + 128], ckv[:, k, ts_]) for k in range(4)],
                    [sd[s]] + [dckv[k][tt] for k in range(4)], [psd[b]])
                sb_ = sqc % 2
                sqc += 1
                act(sq[:, sb_, :], PS[b][:], AF.Square, [psd[b]], [dsq[sb_]])
                mm1(PS[6 + sb_][:], ON128, sq[:, sb_, :], True, True, [dsq[sb_], dconst], [psd[6 + sb_]])
                rstd_from(PS[6 + sb_][:], rstd[:, sb_, :], psd[6 + sb_], drs[sb_])
                vstt(kst[:, hh, ts_], PS[b][:], GT[:, 73:74], rstd[:, sb_, :], ALU.mult, ALU.mult, [psd[b], drs[sb_], dgt], [dkst])
        for i in range(2):
            P.dma(SP, "st1", lambda h, i=i: h.dma_start(out=kb_k[i].rearrange("(h p) t -> p h t", p=128), in_=kst[:, 4 * i:4 * i + 4, :]), reads=[dkst], writes=[dkvb])
        dvst = Dep()
        s = wload(wukv_d[1], 4096)
        vst4 = vst.rearrange("p h (tb d) -> p h tb d", d=128)
        for tb in range(8):
            for half in range(2):
                b = nextps(a_pool)
                mmg(PS[b][:], [(ckv[:, k, tb * 128:(tb + 1) * 128], slots[:, s, k * 1024 + half * 512:k * 1024 + half * 512 + 512]) for k in range(4)],
                    [sd[s]] + [dckv[k][tb // 4] for k in range(4)], [psd[b]])
                vcopy(vst4[:, half * 4:half * 4 + 4, tb, :], PS[b][:].rearrange("p (h d) -> p h d", d=128), [psd[b]], [dvst])
        for i in range(2):
            P.dma(SP, "st2", lambda h, i=i: h.dma_start(out=kb_v[i].rearrange("(h p) c -> p h c", p=128), in_=vst[:, 4 * i:4 * i + 4, :]), reads=[dvst], writes=[dkvb])
        dbg("cq", cq, [128, 4, T], [d for r_ in dcq for d in r_])
        dbg("kst", kst, [128, 8, T], [dkst])
        dbg("kpe", kpe[R, :], [64, T], [dkr])
        dbg("vst", vst, [128, 8, T], [dvst])
        dkvg = Dep()
        coll("cc1", kb_k[0], kg_k[0], [dkvb], [dkvg])
        coll("cc3", kb_k[1], kg_k[1], [dkvb], [dkvg])
        coll("cc4", kb_v[0], kg_v[0], [dkvb], [dkvg])
        coll("cc5", kb_v[1], kg_v[1], [dkvb], [dkvg])
        coll("cc6", kb_pe, kg_pe, [dkvb], [dkvg])
        P.barrier()
        checkpoint(3)

        XA.reset()
        AR.reset(a_mark)
        lsp = AR.alloc([8, 512], F32)
        bT = AR.alloc([4, T], F32)
        glrT = AR.alloc([1, T], F32)[:, 0, :]
        kstate = XA.alloc([8, 512], BF16)
        vtm = XA.alloc([8, T], BF16)
        qdec = XA.alloc([4, T], BF16)
        kdec = XA.alloc([4, T], BF16)
        b_mark = XA.off
        wgkp = XA.alloc([1, 512], F32)[:, 0, :]
        bgkb = XA.alloc([1, 512], F32)[:, 0, :]
        zt = XA.alloc([2, 512], F32)
        Sst = XA.alloc([4, 2, 256], F32)
        glst = XA.alloc([1, 1028], F32)[:, 0, :]
        dglr, dwgk, dlsp, dbT = Dep(), Dep(), [Dep() for _ in range(8)], [[Dep(), Dep()] for _ in range(4)]
        dks = [Dep() for _ in range(8)]
        dvt = [Dep() for _ in range(8)]
        dzt = [Dep(), Dep()]
        vmemset(wgkp, 0.0, [dwgk])
        ld("ld0", wgkp[0:16, :], wgk_d[:, :], [dwgk])
        ld("ld1", bgkb, bgk_d.partition_broadcast(128), [dwgk])
        b_pool = [0, 1, 2, 3]
        s = wload(win_glr_d[0], 2048)
        for tt in range(2):
            ts_ = slice(tt * 512, (tt + 1) * 512)
            b = nextps(b_pool)
            mmg(PS[b][:], [(slots[:, s, k * 128:(k + 1) * 128], hb[:, k, ts_]) for k in range(NKC)],
                [sd[s]] + [hd[k][tt] for k in range(NKC)], [psd[b]])
            act(glrT[:, ts_], PS[b][:], AF.Copy, [psd[b]], [dglr])
        for tb in range(8):
            b = nextps(b_pool)
            mm1(PS[b][:], glrT[:, tb * 128:(tb + 1) * 128], wgkp, True, True, [dglr, dwgk], [psd[b]])
            vtt(zt[:, 0, :], PS[b][:], bgkb, ALU.add, [psd[b], dwgk], [dzt[0]])
            act(zt[:, 1, :], zt[:, 0, :], AF.Exp, [dzt[0]], [dzt[1]], scale=-1.0)
            act(lsp[:, tb, :], zt[:, 1, :], AF.Ln, [dzt[1]], [dlsp[tb]], bias=1.0)
            for hh in range(4):
                bb = 4 + hh
                tt = tb // 4
                mm1(PS[bb][:, (tb % 4) * 128:(tb % 4 + 1) * 128], lsp[:, tb, hh * 128:(hh + 1) * 128], US, True, True,
                    [dlsp[tb], dconst], [psd[bb]])
            if tb % 4 == 3:
                tt = tb // 4
                for hh in range(4):
                    vcopy(bT[:, hh, tt * 512:(tt + 1) * 512], PS[4 + hh][:], [psd[4 + hh]], [dbT[hh][tt]])
        for tb in range(8):
            b = nextps(b_pool)
            mm1(PS[b][:], M2S, lsp[:, tb, :], True, True, [dlsp[tb], dconst], [psd[b]])
            act(lsp[:, tb, :], PS[b][:], AF.Exp, [psd[b]], [dlsp[tb]])
        for hh in range(4):
            bl = bT[:, hh, :].rearrange("p (n s) -> p n s", s=64)[:, :, 63]
            act(DECAY[:, hh, :], bl, AF.Exp, [dbT[hh][0], dbT[hh][1]], [ddec])
        for u in range(4):
            s = wload(win_gv_d[u], 4096)
            for tb in range(8):
                b = nextps(b_pool)
                mmg(PS[b][:, 0:256], [(hb[:, k, tb * 128:(tb + 1) * 128], slots[:, s, k * 256:(k + 1) * 256]) for k in range(NKC)],
                    [sd[s]] + [hd[k][tb // 4] for k in range(NKC)], [psd[b]])
                act(vtm[:, tb, u * 256:(u + 1) * 256], PS[b][:, 0:256], AF.Copy, [psd[b]], [dvt[tb]])
        dqd = [[Dep(), Dep()] for _ in range(4)]
        dkd = [[Dep(), Dep()] for _ in range(4)]
        for u in range(4):
            s = wload(win_gqk_d[u], 4096)
            isk = u >= 2
            if isk:
                half = u - 2
                for tb in range(8):
                    b = nextps(b_pool)
                    mmg(PS[b][:, 0:256], [(hb[:, k, tb * 128:(tb + 1) * 128], slots[:, s, k * 256:(k + 1) * 256]) for k in range(NKC)],
                        [sd[s]] + [hd[k][tb // 4] for k in range(NKC)], [psd[b]])
                    vtt(kstate[:, tb, half * 256:(half + 1) * 256], PS[b][:, 0:256], lsp[:, tb, half * 256:(half + 1) * 256], ALU.mult,
                        [psd[b], dlsp[tb]], [dks[tb]])
            for c in range(2):
                hh = 2 * (u % 2) + c
                for tt in range(2):
                    ts_ = slice(tt * 512, (tt + 1) * 512)
                    b = nextps(b_pool)
                    mmg(PS[b][:], [(slots[:, s, k * 256 + c * 128:k * 256 + c * 128 + 128], hb[:, k, ts_]) for k in range(NKC)],
                        [sd[s]] + [hd[k][tt] for k in range(NKC)], [psd[b]])
                    zb = (hh * 2 + tt) % 2
                    if not isk:
                        act(zt[:, zb, :], bT[:, hh, ts_], AF.Exp, [dbT[hh][tt]], [dzt[zb]])
                        vstt(qdec[:, hh, ts_], PS[b][:], float(128 ** -0.5), zt[:, zb, :], ALU.mult, ALU.mult, [psd[b], dzt[zb]], [dqd[hh][tt]])
                    else:
                        act(zt[:, zb, :], bT[:, hh, ts_], AF.Exp, [dbT[hh][tt]], [dzt[zb]], scale=-1.0)
                        vtt(kdec[:, hh, ts_], PS[b][:], zt[:, zb, :], ALU.mult, [psd[b], dzt[zb]], [dkd[hh][tt]])
        dS = [[Dep(), Dep()] for _ in range(4)]
        dgl = Dep()
        for hh in range(4):
            vmemset(Sst[:, hh, 0, :], 0.0, [dS[hh][0]])
        for n in range(16):
            tb, j = n // 2, n % 2
            for hh in range(4):
                b = nextps(b_pool)
                mm1(PS[b][:, 0:256], kstate[64 * j:64 * j + 64, tb, hh * 128:(hh + 1) * 128], vtm[64 * j:64 * j + 64, tb, hh * 256:(hh + 1) * 256],
                    True, True, [dks[tb], dvt[tb]], [psd[b]])
                src, dst = n % 2, (n + 1) % 2
                out_ap = Sst[:, hh, dst, :] if n < 15 else glst[:, hh * 257:hh * 257 + 256]
                vstt(out_ap, Sst[:, hh, src, :], DECAY[:, hh, n:n + 1], PS[b][:, 0:256], ALU.mult, ALU.add,
                     [dS[hh][src], psd[b], ddec], [dS[hh][dst]] if n < 15 else [dgl])
        for hh in range(4):
            bl = bT[:, hh, :].rearrange("p (n s) -> p n s", s=64)[:, :, 63]
            P.op(DVE, lambda h, bl=bl, hh=hh: h.reduce_sum(out=glst[:, hh * 257 + 256:hh * 257 + 257], in_=bl, axis=mybir.AxisListType.X),
                 reads=[dbT[hh][0], dbT[hh][1]], writes=[dgl])
            act(glst[:, hh * 257 + 256:hh * 257 + 257], glst[:, hh * 257 + 256:hh * 257 + 257], AF.Exp, [dgl], [dgl])
        dglb, dglg = Dep(), Dep()
        P.dma(SP, "st0", lambda h: h.dma_start(out=glb_d[:, :], in_=glst), reads=[dgl], writes=[dglb])
        coll("cc2", glb_d, glg_d, [dglb], [dglg])
        dbg("kstate", kstate, [128, 8, 512], dks)
        dbg("qdec", qdec, [128, 4, T], [d for r_ in dqd for d in r_])
        dbg("kdec", kdec, [128, 4, T], [d for r_ in dkd for d in r_])
        dbg("glst", glst, [128, 1028], [dgl])
        checkpoint(4)

        P.barrier()
        AR.reset(a_mark)
        glaT = AR.alloc([8, T], BF16)
        c_mark = AR.off
        dgla = [[Dep(), Dep()] for _ in range(8)]
        dmla = [[Dep(), Dep()] for _ in range(8)]
        glgs = AR.alloc([4, 1028], F32)
        XA.reset(b_mark)
        osb = XA.alloc([2, T], F32)
        Sp = XA.alloc([2, 256], F32)
        Sbf = XA.alloc([2, 256], BF16)
        attm = XA.alloc([2, 128], BF16)
        sgt = XA.alloc([2, 512], F32)
        dmt = XA.alloc([1, 8], F32)[:, 0, :]
        sqg = XA.alloc([2, 512], BF16)
        rsg = XA.alloc([2, 512], F32)
        dglgs = Dep()
        ld("ld2", glgs, glg_d.rearrange("(r p) c -> p r c", p=128), [dglgs], reads=[dglg])
        dSp = [Dep(), Dep()]
        dSbf = [Dep(), Dep()]
        datt = [Dep(), Dep()]
        dosb = [[Dep(), Dep()] for _ in range(2)]
        dsg = [Dep(), Dep()]
        dsqg = [Dep(), Dep()]
        drsg = [Dep(), Dep()]
        ddm = Dep()
        c_pool = [0, 1, 2, 3]
        o_pool = [4, 5]
        sbc = 0
        for hh in range(4):
            vmemset(Sp[:, 0, :], 0.0, [dSp[0]])
            for r in range(4):
                Dr = glgs[:, r, hh * 257 + 256:hh * 257 + 257]
                Lr = glgs[:, r, hh * 257:hh * 257 + 256]
                ar_ = AMASK[:, 3 + r:4 + r]
                vts(dmt[:, 0:1], Dr, -1.0, None, ALU.add, None, [dglgs], [ddm])
                vts(dmt[:, 0:1], dmt[:, 0:1], ar_, 1.0, ALU.mult, ALU.add, [ddm, dconst], [ddm])
                vts(Lr, Lr, ar_, None, ALU.mult, None, [dglgs, dconst], [dglgs])
                vstt(Sp[:, 0, :], Sp[:, 0, :], dmt[:, 0:1], Lr, ALU.mult, ALU.add, [dSp[0], ddm, dglgs], [dSp[0]])
            cur = 0
            for tb in range(8):
                tt = tb // 4
                cols = slice(tb * 128, (tb + 1) * 128)
                b = nextps(c_pool)
                mm1(PS[b][:, 0:128], kdec[:, hh, cols], qdec[:, hh, cols], True, True, [dkd[hh][tt], dqd[hh][tt]], [psd[b]])
                ab = tb % 2
                vtt(attm[:, ab, :], PS[b][:, 0:128], U01, ALU.mult, [psd[b], dconst], [datt[ab]])
                sbfs = []
                for j in range(2):
                    n = 2 * tb + j
                    sb_ = sbc % 2
                    sbc += 1
                    act(Sbf[:, sb_, :], Sp[:, cur, :], AF.Copy, [dSp[cur]], [dSbf[sb_]])
                    sbfs.append(sb_)
                    b2 = nextps(c_pool)
                    mm1(PS[b2][:, 0:256], kstate[64 * j:64 * j + 64, tb, hh * 128:(hh + 1) * 128], vtm[64 * j:64 * j + 64, tb, hh * 256:(hh + 1) * 256],
                        True, True, [dks[tb], dvt[tb]], [psd[b2]])
                    nxt = 1 - cur
                    vstt(Sp[:, nxt, :], Sp[:, cur, :], DECAY[:, hh, n:n + 1], PS[b2][:, 0:256], ALU.mult, ALU.add,
                         [dSp[cur], psd[b2], ddec], [dSp[nxt]])
                    cur = nxt
                for half in range(2):
                    ob = o_pool[half]
                    c0 = (tb % 4) * 128

                    def ofn(h, ob=ob, c0=c0, half=half, tb=tb, hh=hh, ab=ab, sbfs=tuple(sbfs), cols=cols):
                        h.matmul(PS[ob][:, c0:c0 + 128], lhsT=vtm[:, tb, hh * 256 + half * 128:hh * 256 + half * 128 + 128], rhs=attm[:, ab, :], start=True, stop=False)
                        h.matmul(PS[ob][:, c0:c0 + 64], lhsT=Sbf[:, sbfs[0], half * 128:(half + 1) * 128], rhs=qdec[:, hh, tb * 128:tb * 128 + 64], start=False, stop=False)
                        return h.matmul(PS[ob][:, c0 + 64:c0 + 128], lhsT=Sbf[:, sbfs[1], half * 128:(half + 1) * 128], rhs=qdec[:, hh, tb * 128 + 64:tb * 128 + 128], start=False, stop=True)
                    P.op(PE, ofn, reads=[dvt[tb], datt[ab], dSbf[0], dSbf[1], dqd[hh][tt]], writes=[psd[ob]])
                if tb % 4 == 3:
                    for half in range(2):
                        act(osb[:, half, tt * 512:(tt + 1) * 512], PS[o_pool[half]][:], AF.Copy, [psd[o_pool[half]]], [dosb[half][tt]])
            s = wload(win_gout_d[hh], 4096)
            for tt in range(2):
                ts_ = slice(tt * 512, (tt + 1) * 512)
                for half in range(2):
                    act(sqg[:, half, :], osb[:, half, ts_], AF.Square, [dosb[half][tt]], [dsqg[half]])
                    mm1(PS[6][:], ON256, sqg[:, half, :], half == 0, half == 1, [dsqg[half], dconst], [psd[6]])
                rstd_from(PS[6][:], rsg[:, tt, :], psd[6], drsg[tt])
                for half in range(2):
                    b = nextps(c_pool)
                    mmg(PS[b][:], [(slots[:, s, k * 256 + half * 128:k * 256 + half * 128 + 128], hb[:, k, ts_]) for k in range(NKC)],
                        [sd[s]] + [hd[k][tt] for k in range(NKC)], [psd[b]])
                    act(sgt[:, half, :], PS[b][:], AF.Silu, [psd[b]], [dsg[half]])
                    vstt(osb[:, half, ts_], osb[:, half, ts_], GT[:, 74 + half:75 + half], rsg[:, tt, :], ALU.mult, ALU.mult,
                         [dosb[half][tt], drsg[tt], dgt], [dosb[half][tt]])
                    vtt(glaT[:, hh * 2 + half, ts_], osb[:, half, ts_], sgt[:, half, :], ALU.mult, [dosb[half][tt], dsg[half]], [dgla[hh * 2 + half][tt]])
        P.barrier()

        dbg("glaT", glaT, [128, 8, T], [d for r_ in dgla for d in r_])
        checkpoint(5)
        XA.reset()
        AR.reset(c_mark)
        mlaT = AR.alloc([8, T], BF16)
        qn = XA.alloc([2, T], BF16)
        qpe = XA.alloc([2, T], BF16)
        qraw = XA.alloc([2, 512], F32)
        t2 = XA.alloc([2, 512], F32)
        sqd = XA.alloc([2, 512], BF16)
        rsd = XA.alloc([2, 512], F32)
        kTb = XA.alloc([2, T], BF16)
        kpb = XA.alloc([2, T], BF16)
        vb = XA.alloc([2, T], BF16)
        PT = XA.alloc([3, 512], BF16)
        rden = XA.alloc([1, 512], F32)[:, 0, :]
        dqn = [[Dep(), Dep()] for _ in range(2)]
        dqp = [[Dep(), Dep()] for _ in range(2)]
        dqraw, dt2 = [Dep(), Dep()], Dep()
        dsqd, drsd = [Dep(), Dep()], [Dep(), Dep()]
        dkvbuf = [Dep(), Dep()]
        dPT = [Dep(), Dep(), Dep()]
        drden = Dep()
        q_pool = [0, 1]
        s_pool = [2, 3]
        SCALE = float(192 ** -0.5)
        kvc = 0
        ptc = 0
        sq_c = 0
        wq_slot = None
        for hh in range(8):
            if hh % 4 == 0:
                wq_slot = wload(wuq_d[hh // 4], 4096)
            hq = hh % 2
            base = (hh % 4) * 256
            for tt in range(2):
                ts_ = slice(tt * 512, (tt + 1) * 512)
                b = nextps(q_pool)
                mmg(PS[b][:], [(slots[:, wq_slot, k * 1024 + base:k * 1024 + base + 128], cq[:, k, ts_]) for k in range(4)],
                    [sd[wq_slot]] + [dcq[k][tt] for k in range(4)], [psd[b]])
                sb_ = sq_c % 2
                sq_c += 1
                act(sqd[:, sb_, :], PS[b][:], AF.Square, [psd[b]], [dsqd[sb_]])
                mm1(PS[6][:], ON128, sqd[:, sb_, :], True, True, [dsqd[sb_], dconst], [psd[6]])
                rstd_from(PS[6][:], rsd[:, sb_, :], psd[6], drsd[sb_])
                vts(rsd[:, sb_, :], rsd[:, sb_, :], SCALE, None, ALU.mult, None, [drsd[sb_]], [drsd[sb_]])
                vstt(qn[:, hq, ts_], PS[b][:], GT[:, 72:73], rsd[:, sb_, :], ALU.mult, ALU.mult, [psd[b], drsd[sb_], dgt], [dqn[hq][tt]])
                b0 = nextps(q_pool)
                mmg(PS[b0][0:64, :], [(slots[:, wq_slot, k * 1024 + base + 128:k * 1024 + base + 192], cq[:, k, ts_]) for k in range(4)],
                    [sd[wq_slot]] + [dcq[k][tt] for k in range(4)], [psd[b0]])
                act(qraw[R, 0, :], PS[b0][0:64, :], AF.Copy, [psd[b0]], [dqraw[0]])
                sb_ = sq_c % 2
                sq_c += 1
                act(sqd[R, sb_, :], PS[b0][0:64, :], AF.Square, [psd[b0]], [dsqd[sb_]])
                mm1(PS[6][0:64, :], ON64[R, 0:64], sqd[R, sb_, :], True, True, [dsqd[sb_], dconst], [psd[6]])
                rstd_from(PS[6][0:64, :], rsd[R, sb_, :], psd[6], drsd[sb_])
                vts(rsd[R, sb_, :], rsd[R, sb_, :], SCALE, None, ALU.mult, None, [drsd[sb_]], [drsd[sb_]])
                b1 = nextps(q_pool)
                mmg(PS[b1][0:64, :], [(slots[:, wq_slot, k * 1024 + base + 192:k * 1024 + base + 256], cq[:, k, ts_]) for k in range(4)],
                    [sd[wq_slot]] + [dcq[k][tt] for k in range(4)], [psd[b1]])
                vtt(t2[R, 0, :], qraw[R, 0, :], TCq[R, ts_], ALU.mult, [dqraw[0], drope], [dt2])
                vtt(t2[R, 1, :], PS[b1][0:64, :], TSq[R, ts_], ALU.mult, [psd[b1], drope], [dt2])
                vtt(t2[R, 0, :], t2[R, 0, :], t2[R, 1, :], ALU.add, [dt2], [dt2])
                vtt(qpe[R, hq, ts_], t2[R, 0, :], rsd[R, sb_, :], ALU.mult, [dt2, drsd[sb_]], [dqp[hq][tt]])
            first = [True, True]
            for slot in range(4):
                kb_ = kvc % 2
                kvc += 1
                hi, hl = hh // 4, hh % 4
                if slot < 3:
                    ksrc, vsrc, psrc, rdeps = kg_k[hi], kg_v[hi], kg_pe, [dkvg]
                    kr0, pr0 = slot * 512 + hl * 128, slot * 64
                else:
                    ksrc, vsrc, psrc, rdeps = kb_k[hi], kb_v[hi], kb_pe, [dkvb]
                    kr0, pr0 = hl * 128, 0
                P.dma(SP, f"kv{kb_}", lambda h, ksrc=ksrc, kr0=kr0, kb_=kb_: h.dma_start(out=kTb[:, kb_, :], in_=ksrc[kr0:kr0 + 128, :]),
                      reads=rdeps, writes=[dkvbuf[kb_]])
                P.dma(SP, f"kv{kb_}", lambda h, psrc=psrc, pr0=pr0, kb_=kb_: h.dma_start(out=kpb[R, kb_, :], in_=psrc[pr0:pr0 + 64, :]),
                      reads=rdeps, writes=[dkvbuf[kb_]])
                P.dma(SP, f"kv{kb_}", lambda h, vsrc=vsrc, kr0=kr0, kb_=kb_: h.dma_start(out=vb[:, kb_, :], in_=vsrc[kr0:kr0 + 128, :]),
                      reads=rdeps, writes=[dkvbuf[kb_]])
                for tt in range(2):
                    ob, db = 4 + tt, 6 + tt
                    for kb in range(8):
                        q0 = 0
                        diag = False
                        if slot == 3:
                            if kb * 128 >= (tt + 1) * 512:
                                continue
                            if kb * 128 >= tt * 512:
                                q0 = kb * 128 - tt * 512
                                diag = True
                        last = (slot == 3) and (kb == 4 * tt + 3)
                        sbk = nextps(s_pool)
                        qs = slice(tt * 512 + q0, (tt + 1) * 512)
                        mmg(PS[sbk][:, q0:512],
                            [(kTb[:, kb_, kb * 128:(kb + 1) * 128], qn[:, hq, qs]), (kpb[R, kb_, kb * 128:(kb + 1) * 128], qpe[R, hq, qs])],
                            [dkvbuf[kb_], dqn[hq][tt], dqp[hq][tt]], [psd[sbk]])
                        pb = ptc % 3
                        ptc += 1
                        if slot < 3:
                            act(PT[:, pb, q0:512], PS[sbk][:, q0:512], AF.Exp, [psd[sbk], dconst], [dPT[pb]], bias=AMASK[:, slot:slot + 1])
                        else:
                            act(PT[:, pb, q0:512], PS[sbk][:, q0:512], AF.Exp, [psd[sbk]], [dPT[pb]])
                        if diag:
                            vmemset(PT[64:128, pb, q0:q0 + 64], 0.0, [dPT[pb]])
                        st = first[tt]
                        first[tt] = False
                        mm1(PS[ob][:, q0:512], vb[:, kb_, kb * 128:(kb + 1) * 128], PT[:, pb, q0:512], st, last, [dkvbuf[kb_], dPT[pb]], [psd[ob]])
                        mm1(PS[db][:, q0:512], ONES, PT[:, pb, q0:512], st, last, [dPT[pb], dconst], [psd[db]])
            for tt in range(2):
                ts_ = slice(tt * 512, (tt + 1) * 512)
                vrecip(rden, PS[6 + tt][:], [psd[6 + tt]], [drden])
                vtt(mlaT[:, hh, ts_], PS[4 + tt][:], rden, ALU.mult, [psd[4 + tt], drden], [dmla[hh][tt]])
        P.barrier()

        dbg("mlaT", mlaT, [128, 8, T], [d for r_ in dmla for d in r_])
        checkpoint(6)
        for kc in range(0, NKC, 4):
            ld("xio", xb[:, kc:kc + 4, :], xs_d.rearrange("(k p) t -> p k t", p=128)[:, kc:kc + 4, :],
               [d for k in range(kc, kc + 4) for d in xd[k]], reads=[dxs])
        AR.reset(0)
        mg = AR.alloc([4, T], BF16)
        sga = AR.alloc([1, 512], F32)[:, 0, :]
        sgb = AR.alloc([1, 512], F32)[:, 0, :]
        ta = AR.alloc([1, 512], F32)[:, 0, :]
        tbb = AR.alloc([1, 512], F32)[:, 0, :]
        assert AR.off <= a_mark
        dmg = [[Dep(), Dep()] for _ in range(4)]
        dsga, dsgb, dta, dtb = Dep(), Dep(), Dep(), Dep()
        e_pool = [0, 1, 2, 3]
        e2_pool = [4, 5, 6, 7]
        for G in range(4):
            for u in range(2):
                spa = wload(wpa_d[G], 4096)
                spb = wload(wpb_d[G], 4096)
                sga_s = wload(win_gab_d[2 * G + u], 4096)
                sgb_s = wload(win_gab_d[8 + 2 * G + u], 4096)
                for c in range(2):
                    ni = 2 * u + c
                    for tt in range(2):
                        ts_ = slice(tt * 512, (tt + 1) * 512)
                        bya, byb, bga, bgb = 0, 1, 2, 3
                        mmg(PS[bya][:], [(slots[:, spa, k * 512 + ni * 128:k * 512 + ni * 128 + 128], mlaT[:, k, ts_]) for k in range(8)],
                            [sd[spa]] + [dmla[k][tt] for k in range(8)], [psd[bya]])
                        mmg(PS[byb][:], [(slots[:, spb, k * 512 + ni * 128:k * 512 + ni * 128 + 128], glaT[:, k, ts_]) for k in range(8)],
                            [sd[spb]] + [dgla[k][tt] for k in range(8)], [psd[byb]])
                        mmg(PS[bga][:], [(slots[:, sga_s, k * 256 + c * 128:k * 256 + c * 128 + 128], hb[:, k, ts_]) for k in range(NKC)],
                            [sd[sga_s]] + [hd[k][tt] for k in range(NKC)], [psd[bga]])
                        mmg(PS[bgb][:], [(slots[:, sgb_s, k * 256 + c * 128:k * 256 + c * 128 + 128], hb[:, k, ts_]) for k in range(NKC)],
                            [sd[sgb_s]] + [hd[k][tt] for k in range(NKC)], [psd[bgb]])
                        act(sga, PS[bga][:], AF.Sigmoid, [psd[bga]], [dsga])
                        act(sgb, PS[bgb][:], AF.Sigmoid, [psd[bgb]], [dsgb])
                        vtt(ta, PS[bya][:], sga, ALU.mult, [psd[bya], dsga], [dta])
                        vtt(tbb, PS[byb][:], sgb, ALU.mult, [psd[byb], dsgb], [dtb])
                        vtt(mg[:, ni, ts_], ta, tbb, ALU.add, [dta, dtb], [dmg[ni][tt]])
            so = [wload(wout_d[2 * G + u], 4096) for u in range(2)]
            for n in range(NKC):
                for tt in range(2):
                    ts_ = slice(tt * 512, (tt + 1) * 512)
                    b = nextps(e2_pool)
                    mmg(PS[b][:], [(slots[:, so[ni // 2], (ni % 2) * 2048 + n * 128:(ni % 2) * 2048 + (n + 1) * 128], mg[:, ni, ts_]) for ni in range(4)],
                        [sd[so[0]], sd[so[1]]] + [dmg[ni][tt] for ni in range(4)], [psd[b]])
                    vstt(xb[:, n, ts_], PS[b][:], GG[:, 1, n:n + 1], xb[:, n, ts_], ALU.mult, ALU.add, [psd[b], xd[n][tt], dmod], [xd[n][tt]])
        P.barrier()

        dbg("x2", xb[:], [128, NKC, T], xall)
        checkpoint(7)
        ffn(1, 2)
        checkpoint(8)

        AR.reset()
        modnorm(lambda kc: GT[:, 48 + kc:49 + kc], None, AR, out_fp32_inplace=True)
        P.stopped = False
        ov = outT_d.rearrange("(k p) t -> p k t", p=128)
        for kc in range(0, NKC, 4):
            tok = P.dma(SP, "xio", lambda h, kc=kc: h.dma_start(out=ov[:, kc:kc + 4, :], in_=xb[:, kc:kc + 4, :]),
                        reads=[d for k in range(kc, kc + 4) for d in xd[k]])
        P._wait(SP, tok)
        P.emit(block)
    _CACHE['used'] = list(DI.keys())
    return nc, dbg_list


def _colunits(W, col_lists, kch):
    out = []
    for cols in col_lists:
        sub = W[:, cols]
        n = sub.shape[1]
        out.append(sub.reshape(kch, 128, n).transpose(1, 0, 2).reshape(128, kch * n))
    return np.ascontiguousarray(np.stack(out)).astype(np.float32, copy=False)


def _rowunits(W, nrow_chunks_per_unit):
    K, N = W.shape
    nu = K // (128 * nrow_chunks_per_unit)
    a = W.reshape(nu, nrow_chunks_per_unit, 128, N).transpose(0, 2, 1, 3).reshape(nu, 128, nrow_chunks_per_unit * N)
    return np.ascontiguousarray(a).astype(np.float32, copy=False)


def _percol(v, n):
    return np.ascontiguousarray(np.asarray(v, np.float32).reshape(n, 128).T)


_CACHE = {}


def kernel(x, c, positions, w_ada, b_ada, g_ffn1, w1_a, w3_a, w2_a, g_mix, w_in,
           g_q_lat, w_uq, g_qn, g_qr, g_kv_lat, w_ukv, g_kn, g_kr, w_gk_up, b_gk, g_gla,
           w_proj_a, w_proj_b, w_out, g_ffn2, w1_b, w3_b, w2_b, g_final):
    f = lambda a: np.asarray(a)
    x, c, positions = f(x), f(c), f(positions)
    w_ada, b_ada = f(w_ada)[0], f(b_ada)[0]
    w_in = f(w_in)[0]
    w_uq, w_ukv = f(w_uq)[0], f(w_ukv)[0]
    ar = np.arange
    shared = {}
    for tag, (w1, w3, w2) in (("a", (w1_a, w3_a, w2_a)), ("b", (w1_b, w3_b, w2_b))):
        w1, w3, w2 = f(w1)[0], f(w3)[0], f(w2)[0]
        u1 = _colunits(w1, [ar(m * 128, (m + 1) * 128) for m in range(NM)], 16)
        u3 = _colunits(w3, [ar(m * 128, (m + 1) * 128) for m in range(NM)], 16)
        shared["w13" + tag] = np.ascontiguousarray(np.concatenate([u1, u3], axis=2))
        shared["w2" + tag] = _rowunits(w2, 2)
    shared["win_lat"] = _colunits(w_in, [ar(i * 256, (i + 1) * 256) for i in range(4)], 16)
    shared["win_kr"] = _colunits(w_in, [np.concatenate([ar(1024, 1088), ar(1056, 1088), ar(1024, 1056)])], 16)
    shared["win_gqk"] = _colunits(w_in, [ar(1088 + i * 256, 1088 + (i + 1) * 256) for i in range(4)], 16)
    shared["win_gv"] = _colunits(w_in, [ar(2112 + i * 256, 2112 + (i + 1) * 256) for i in range(4)], 16)
    shared["win_glr"] = _colunits(w_in, [ar(3136, 3264)], 16)
    shared["win_gout"] = _colunits(w_in, [ar(3152 + i * 256, 3152 + (i + 1) * 256) for i in range(4)], 16)
    shared["win_gab"] = _colunits(w_in, [ar(4176 + i * 256, 4176 + (i + 1) * 256) for i in range(16)], 16)
    uq_cols = []
    for g in range(2):
        cols = []
        for hh in range(4 * g, 4 * g + 4):
            b0 = hh * 192
            cols += [ar(b0, b0 + 128), ar(b0 + 128, b0 + 192), ar(b0 + 160, b0 + 192), ar(b0 + 128, b0 + 160)]
        uq_cols.append(np.concatenate(cols))
    shared["wuq"] = _colunits(w_uq, uq_cols, 4)
    shared["wukv"] = _colunits(w_ukv, [np.concatenate([ar(hh * 256, hh * 256 + 128) for hh in range(8)]),
                                       np.concatenate([ar(hh * 256 + 128, hh * 256 + 256) for hh in range(8)])], 4)
    shared["wpa"] = _colunits(f(w_proj_a)[0], [ar(i * 512, (i + 1) * 512) for i in range(4)], 8)
    shared["wpb"] = _colunits(f(w_proj_b)[0], [ar(i * 512, (i + 1) * 512) for i in range(4)], 8)
    shared["wout"] = _rowunits(f(w_out)[0], 2)
    gains = np.zeros((128, 96), np.float32)
    gains[:, 0:16] = _percol(f(g_ffn1)[0], 16)
    gains[:, 16:32] = _percol(f(g_mix)[0], 16)
    gains[:, 32:48] = _percol(f(g_ffn2)[0], 16)
    gains[:, 48:64] = _percol(f(g_final)[0], 16)
    gains[:, 64:68] = _percol(f(g_q_lat)[0], 4)
    gains[:, 68:72] = _percol(f(g_kv_lat)[0], 4)
    gains[:, 72] = f(g_qn)[0]
    gains[:, 73] = f(g_kn)[0]
    gains[:, 74:76] = _percol(f(g_gla)[0], 2)
    gqr, gkr = f(g_qr)[0], f(g_kr)[0]
    gains[:64, 76] = gqr
    gains[:64, 77] = np.concatenate([gqr[32:], gqr[:32]])
    gains[:64, 78] = gkr
    gains[:64, 79] = np.concatenate([gkr[32:], gkr[:32]])
    gains[:32, 80] = -1.0
    gains[32:64, 80] = 1.0
    inv_freq = (10000.0 ** (-np.arange(0, 64, 2, dtype=np.float32) / 64)).astype(np.float32)
    gains[:64, 81] = np.concatenate([inv_freq, inv_freq])
    shared["gains"] = gains
    shared["bgk"] = np.ascontiguousarray(f(b_gk)[0].reshape(1, 512).astype(np.float32))
    shared["wgk"] = np.ascontiguousarray(f(w_gk_up)[0].astype(np.float32))
    idx = np.arange(128)
    same = (idx[:, None] // 64) == (idx[None, :] // 64)
    shared["umask"] = (same & (idx[:, None] <= idx[None, :])).astype(np.float32)
    shared["m2mask"] = (same & (idx[:, None] > idx[None, :])).astype(np.float32)

    if "nc" not in _CACHE:
        _CACHE["nc"] = build_program()
    nc, dbg_list = _CACHE["nc"]
    in_maps = []
    for core in range(8):
        b, r = core // 4, core % 4
        m = dict(shared)
        m["xT"] = np.ascontiguousarray(x[b, r * T:(r + 1) * T, :].T)
        m["cT"] = _percol(c[b], 16)
        m["pos"] = np.ascontiguousarray(positions[b, r * T:(r + 1) * T].reshape(1, T).astype(np.int32))
        m["bada"] = _percol(b_ada[r * 4608:(r + 1) * 4608], 36)
        m["wada"] = _colunits(w_ada, [ar(r * 4608 + j * 256, r * 4608 + (j + 1) * 256) for j in range(18)], 16)
        am = np.zeros((128, 8), np.float32)
        for s_ in range(3):
            am[:, s_] = 0.0 if s_ < r else -30000.0
        for s_ in range(4):
            am[:, 3 + s_] = 1.0 if s_ < r else 0.0
        m["amask"] = am
        in_maps.append(m)
    used = set(_CACHE['used'])
    in_maps = [{k: v for k, v in m.items() if k in used} for m in in_maps]
    res = run_bass_kernel_spmd(nc, in_maps, core_ids=list(range(8)))
    out = np.empty((2, 4096, D_MODEL), np.float32)
    for core in range(8):
        b, r = core // 4, core % 4
        out[b, r * T:(r + 1) * T, :] = res.results[core]["outT"].T
    if DEBUG:
        _CACHE["dbg"] = [{k: res.results[core][k] for k in dbg_list} for core in range(8)]
    return out
```

```python
import numpy as np
import concourse.bass as bass
import concourse.mybir as mybir
from concourse.bass_utils import run_bass_kernel_spmd
from contextlib import ExitStack

F32 = mybir.dt.float32
BF16 = mybir.dt.bfloat16
I32 = mybir.dt.int32
AF = mybir.ActivationFunctionType
ALU = mybir.AluOpType

PE, ACT, DVE, POOL, SP = "tensor", "scalar", "vector", "gpsimd", "sync"
ENGS = (PE, ACT, DVE, POOL, SP)

D_MODEL = 2048
T = 1024
NKC = 16
D_FF = 5632
NM = 44
EPS = 1e-6
NS = 5
SLOT_E = 4096
KV_ROWS = 2112
GROUPS = [[0, 1, 2, 3], [4, 5, 6, 7]]
DEBUG = False
STOP = 99


class Dep:
    __slots__ = ("w", "r")

    def __init__(self):
        self.w = None
        self.r = {}


class Prog:
    def __init__(self):
        self.q = {e: [] for e in ENGS}
        self.cnt = {e: 0 for e in ENGS}
        self.sem = {}
        self.seen = {e: {} for e in ENGS}
        self.dsem = {}
        self.pending = []
        self.stopped = False

    def _wait(self, eng, tok):
        key, handle, val = tok
        if key == PE and eng == PE:
            return
        if self.seen[eng].get(key, 0) >= val:
            return
        self.seen[eng][key] = val
        self.q[eng].append(lambda h, handle=handle, val=val: h.wait_ge(handle, val))

    def _collect(self, reads, writes, extra):
        toks = []
        for d in reads:
            if d.w is not None:
                toks.append(d.w)
        for d in writes:
            if d.w is not None:
                toks.append(d.w)
            toks.extend(d.r.values())
        toks.extend(extra)
        return toks

    def _mark(self, tok, reads, writes):
        for d in reads:
            old = d.r.get(tok[0])
            if old is None or old[2] < tok[2]:
                d.r[tok[0]] = tok
        for d in writes:
            d.w = tok
            d.r = {}

    def op(self, eng, fn, reads=(), writes=(), extra=()):
        if self.stopped:
            return None
        for tok in self._collect(reads, writes, extra):
            self._wait(eng, tok)
        self.cnt[eng] += 1
        sem = self.sem[eng]
        self.q[eng].append(lambda h, fn=fn, sem=sem: fn(h).then_inc(sem, 1))
        tok = (eng, sem, self.cnt[eng])
        self._mark(tok, reads, writes)
        return tok

    def dma(self, eng, semname, fn, reads=(), writes=(), extra=(), inc=16, track=True):
        if self.stopped:
            return None
        handle, issued = self.dsem[semname]
        key = "dma_" + semname
        toks = self._collect(reads, writes, extra)
        if issued > 0:
            toks.append((key, handle, issued))
        for tok in toks:
            self._wait(eng, tok)
        issued += inc
        self.dsem[semname][1] = issued
        self.q[eng].append(lambda h, fn=fn, handle=handle, inc=inc: fn(h).then_inc(handle, inc))
        tok = (key, handle, issued)
        self._mark(tok, reads, writes)
        if track:
            self.pending.append(tok)
        return tok

    def barrier(self):
        if self.stopped:
            return
        toks = [(e, self.sem[e], self.cnt[e]) for e in (PE, ACT, DVE) if self.cnt[e] > 0]
        toks += self.pending
        self.pending = []
        for e in (PE, ACT, DVE, SP):
            for tok in toks:
                if tok[0] != e:
                    self._wait(e, tok)

    def emit(self, block):
        def run(eng):
            def body(h):
                for fn in self.q[eng]:
                    fn(h)
            return body
        block.tensor(run(PE))
        block.scalar(run(ACT))
        block.vector(run(DVE))
        block.gpsimd(run(POOL))
        block.sync(run(SP))


class Bump:
    def __init__(self, tensor, nbytes):
        self.t = tensor
        self.n = nbytes
        self.off = 0

    def reset(self, off=0):
        self.off = off

    def alloc(self, shape, dt):
        esz = 4 if dt in (F32, I32) else 2
        n = int(np.prod(shape)) * esz
        n = (n + 31) // 32 * 32
        assert self.off + n <= self.n, ("arena overflow", self.off, n, self.n)
        w0 = self.off // 4
        ap = self.t[:, w0:w0 + n // 4]
        if dt != F32:
            ap = ap.bitcast(dt)
        ap = ap[:, 0:int(np.prod(shape))]
        self.off += n
        if len(shape) == 2:
            return ap.rearrange("p (a b) -> p a b", b=shape[1])
        if len(shape) == 3:
            return ap.rearrange("p (a b c) -> p a b c", b=shape[1], c=shape[2])
        return ap


def build_program():
    nc = bass.Bass("TRN2", target_bir_lowering=False)
    P = Prog()
    DI = {}

    SHAPES = {
        "xT": ([D_MODEL, T], F32), "cT": ([128, 16], F32), "pos": ([1, T], I32), "bada": ([128, 36], F32),
        "gains": ([128, 96], F32), "bgk": ([1, 512], F32), "wgk": ([16, 512], F32), "amask": ([128, 8], F32),
        "umask": ([128, 128], F32), "m2mask": ([128, 128], F32), "wada": ([18, 128, 4096], F32),
        "w13a": ([NM, 128, 4096], F32), "w13b": ([NM, 128, 4096], F32), "w2a": ([22, 128, 4096], F32), "w2b": ([22, 128, 4096], F32),
        "win_lat": ([4, 128, 4096], F32), "win_kr": ([1, 128, 2048], F32), "win_gqk": ([4, 128, 4096], F32),
        "win_gv": ([4, 128, 4096], F32), "win_glr": ([1, 128, 2048], F32), "win_gout": ([4, 128, 4096], F32),
        "win_gab": ([16, 128, 4096], F32), "wuq": ([2, 128, 4096], F32), "wukv": ([2, 128, 4096], F32),
        "wpa": ([4, 128, 4096], F32), "wpb": ([4, 128, 4096], F32), "wout": ([8, 128, 4096], F32),
    }

    class _Lazy:
        def __init__(self, name):
            self.name = name

        def _ap(self):
            if self.name not in DI:
                shape, dt = SHAPES[self.name]
                DI[self.name] = nc.dram_tensor(self.name, list(shape), dt, kind="ExternalInput").ap()
            return DI[self.name]

        def __getitem__(self, k):
            if P.stopped:
                return None
            return self._ap()[k]

        def rearrange(self, *a, **k):
            if P.stopped:
                return None
            return self._ap().rearrange(*a, **k)

        def partition_broadcast(self, n):
            if P.stopped:
                return None
            return self._ap().partition_broadcast(n)

    xT_d, cT_d, pos_d, bada_d, gains_d = _Lazy("xT"), _Lazy("cT"), _Lazy("pos"), _Lazy("bada"), _Lazy("gains")
    bgk_d, wgk_d, amask_d, umask_d, m2mask_d = _Lazy("bgk"), _Lazy("wgk"), _Lazy("amask"), _Lazy("umask"), _Lazy("m2mask")
    wada_d = _Lazy("wada")
    w13_d = [_Lazy("w13a"), _Lazy("w13b")]
    w2_d = [_Lazy("w2a"), _Lazy("w2b")]
    win_lat_d, win_kr_d, win_gqk_d, win_gv_d = _Lazy("win_lat"), _Lazy("win_kr"), _Lazy("win_gqk"), _Lazy("win_gv")
    win_glr_d, win_gout_d, win_gab_d = _Lazy("win_glr"), _Lazy("win_gout"), _Lazy("win_gab")
    wuq_d, wukv_d, wpa_d, wpb_d, wout_d = _Lazy("wuq"), _Lazy("wukv"), _Lazy("wpa"), _Lazy("wpb"), _Lazy("wout")
    outT_d = nc.dram_tensor("outT", [D_MODEL, T], F32, kind="ExternalOutput").ap()
    xs_d = nc.dram_tensor("xs", [D_MODEL, T], F32).ap()
    modb_d = nc.dram_tensor("modb", [128, 36], F32).ap()
    modg_d = nc.dram_tensor("modg", [512, 36], F32).ap()
    kb_k = [nc.dram_tensor(f"kb_k{i}", [512, T], BF16).ap() for i in range(2)]
    kg_k = [nc.dram_tensor(f"kg_k{i}", [2048, T], BF16).ap() for i in range(2)]
    kb_v = [nc.dram_tensor(f"kb_v{i}", [512, T], BF16).ap() for i in range(2)]
    kg_v = [nc.dram_tensor(f"kg_v{i}", [2048, T], BF16).ap() for i in range(2)]
    kb_pe = nc.dram_tensor("kb_pe", [64, T], BF16).ap()
    kg_pe = nc.dram_tensor("kg_pe", [256, T], BF16).ap()
    glb_d = nc.dram_tensor("glb", [128, 1028], F32).ap()
    glg_d = nc.dram_tensor("glg", [512, 1028], F32).ap()
    dbg_list = []

    with ExitStack() as es:
        def sbt(name, shape, dt):
            return es.enter_context(nc.sbuf_tensor(name, shape, dt))
        slots = sbt("slots", [128, NS, SLOT_E], BF16)
        xb = sbt("xb", [128, NKC, T], F32)
        hb = sbt("hb", [128, NKC, T], BF16)
        cst = sbt("cst", [128, 2048], F32)
        ARENA_B = 52 * 1024
        arena_t = sbt("arena", [128, ARENA_B // 4], F32)
        PS = [es.enter_context(nc.psum_tensor(f"ps{i}", [128, 512], F32)) for i in range(8)]
        for e in ENGS:
            P.sem[e] = es.enter_context(nc.semaphore("s_" + e))
        dnames = [f"w{i}" for i in range(NS)] + ["ld0", "ld1", "ld2", "ld3", "st0", "st1", "st2", "kvk0", "kvk1", "kvp0", "kvp1", "kvv0", "kvv1", "cc0", "cc1", "cc2", "cc3", "cc4", "cc5", "cc6", "xio"]
        for nm in dnames:
            P.dsem[nm] = [es.enter_context(nc.semaphore("d_" + nm)), 0]
        block = es.enter_context(nc.Block())

        AR = Bump(arena_t, ARENA_B)
        XA = Bump(xb[:].rearrange("p a b -> p (a b)"), NKC * T * 4)
        CS = Bump(cst, 2048 * 4)

        sd = [Dep() for _ in range(NS)]
        xd = [[Dep(), Dep()] for _ in range(NKC)]
        hd = [[Dep(), Dep()] for _ in range(NKC)]
        psd = [Dep() for _ in range(8)]
        xall = [d for r in xd for d in r]
        hall = [d for r in hd for d in r]

        def act(out, in_, func, reads, writes, **kw):
            return P.op(ACT, lambda h: h.activation(out=out, in_=in_, func=func, **kw), reads=reads, writes=writes)

        def vtt(out, in0, in1, op, reads, writes, eng=DVE):
            return P.op(eng, lambda h: h.tensor_tensor(out=out, in0=in0, in1=in1, op=op), reads=reads, writes=writes)

        def vts(out, in0, s1, s2, op0, op1, reads, writes, eng=DVE):
            if s2 is None:
                return P.op(eng, lambda h: h.tensor_scalar(out=out, in0=in0, scalar1=s1, scalar2=None, op0=op0), reads=reads, writes=writes)
            return P.op(eng, lambda h: h.tensor_scalar(out=out, in0=in0, scalar1=s1, scalar2=s2, op0=op0, op1=op1), reads=reads, writes=writes)

        def vstt(out, in0, scalar, in1, op0, op1, reads, writes, eng=DVE):
            return P.op(eng, lambda h: h.scalar_tensor_tensor(out=out, in0=in0, scalar=scalar, in1=in1, op0=op0, op1=op1), reads=reads, writes=writes)

        def vcopy(out, in_, reads, writes, eng=DVE):
            return P.op(eng, lambda h: h.tensor_copy(out=out, in_=in_), reads=reads, writes=writes)

        def vmemset(ap, val, writes, eng=DVE):
            return P.op(eng, lambda h: h.memset(ap, val), writes=writes)

        def vrecip(out, in_, reads, writes):
            return P.op(DVE, lambda h: h.reciprocal(out=out, in_=in_), reads=reads, writes=writes)

        def mmg(out_ap, pairs, reads, writes):
            n = len(pairs)

            def fn(h):
                ins = None
                for i, (l, r) in enumerate(pairs):
                    ins = h.matmul(out_ap, lhsT=l, rhs=r, start=(i == 0), stop=(i == n - 1))
                return ins
            return P.op(PE, fn, reads=reads, writes=writes)

        def mm1(out_ap, l, r, start, stop, reads, writes):
            return P.op(PE, lambda h: h.matmul(out_ap, lhsT=l, rhs=r, start=start, stop=stop), reads=reads, writes=writes)

        def ld(semname, out, in_, writes, reads=(), eng=SP):
            return P.dma(eng, semname, lambda h: h.dma_start(out=out, in_=in_), reads=reads, writes=writes)

        wstate = {"u": 0}

        def wload(src, E):
            s = wstate["u"] % NS
            wstate["u"] += 1
            P.dma(POOL, f"w{s}", lambda h: h.dma_start(out=slots[:, s, 0:E], in_=src), writes=[sd[s]], track=False)
            return s

        def coll(semname, in_ap, out_ap, reads, writes):
            return P.dma(POOL, semname, lambda h: h.collective_compute(
                "AllGather", ALU.bypass, replica_groups=GROUPS, ins=[in_ap], outs=[out_ap]),
                reads=reads, writes=writes, inc=1, track=False)

        def checkpoint(k):
            if STOP == k and not P.stopped:
                P.barrier()
                P.stopped = True

        psrr = {"i": 0}

        def nextps(pool):
            b = pool[psrr["i"] % len(pool)]
            psrr["i"] += 1
            return b

        def dbg(name, ap, shape, reads):
            if not DEBUG or P.stopped:
                return
            d = nc.dram_tensor("dbg_" + name, list(shape), ap.dtype, kind="ExternalOutput").ap()
            dbg_list.append("dbg_" + name)
            P.dma(SP, "xio", lambda h: h.dma_start(out=d, in_=ap), reads=reads)

        GT = CS.alloc([96], F32)
        MOD = CS.alloc([144], F32)
        AB = CS.alloc([3, 16], F32)
        GG = CS.alloc([3, 16], F32)
        BADA = CS.alloc([36], F32)
        MODS = CS.alloc([36], F32)
        AMASK = CS.alloc([8], F32)
        CTT = CS.alloc([16], F32)
        SCB = CS.alloc([16], BF16)
        ONES = CS.alloc([128], BF16)
        ON2048 = CS.alloc([128], BF16)
        ON512 = CS.alloc([128], BF16)
        ON256 = CS.alloc([128], BF16)
        ON128 = CS.alloc([128], BF16)
        ON64 = CS.alloc([128], BF16)
        U01 = CS.alloc([128], F32)
        US = CS.alloc([128], F32)
        M2S = CS.alloc([128], F32)
        DECAY = CS.alloc([4, 16], F32)
        GSC = CS.alloc([4], F32)
        dconst = Dep()
        ddec = Dep()
        dmod = Dep()
        dgt = Dep()
        ld("ld0", GT, gains_d[:, :], [dgt])
        ld("ld1", CTT, cT_d[:, :], [dconst])
        ld("ld2", BADA, bada_d[:, :], [dconst])
        ld("ld3", AMASK, amask_d[:, :], [dconst])
        ld("ld0", U01, umask_d[:, :], [dconst])
        ld("ld1", M2S, m2mask_d[:, :], [dconst])
        for ap_, v in ((ONES, 1.0), (ON2048, 1.0 / 2048), (ON512, 1.0 / 512), (ON256, 1.0 / 256), (ON128, 1.0 / 128), (ON64, 1.0 / 64)):
            vmemset(ap_, v, [dconst])
        vts(US, U01, -1.0 / 16, None, ALU.mult, None, [dconst], [dconst])
        vts(M2S, M2S, -1.0 / 16, None, ALU.mult, None, [dconst], [dconst])

        xv = xT_d.rearrange("(k p) t -> p k t", p=128)
        for kc in range(0, NKC, 4):
            ld("xio", xb[:, kc:kc + 4, :], xv[:, kc:kc + 4, :], [d for k in range(kc, kc + 4) for d in xd[k]])

        act(SCB, CTT, AF.Silu, [dconst], [dconst])
        modps = PS[0]
        for j in range(18):
            s = wload(wada_d[j], 4096)
            for c in range(2):
                m = 2 * j + c
                mmg(modps[:, m:m + 1], [(slots[:, s, k * 256 + c * 128:k * 256 + c * 128 + 128], SCB[:, k:k + 1]) for k in range(16)],
                    [sd[s], dconst], [psd[0]])
        vtt(MODS, modps[:, 0:36], BADA, ALU.add, [psd[0], dconst], [dmod])
        dmodb, dmodg = Dep(), Dep()
        P.dma(SP, "st0", lambda h: h.dma_start(out=modb_d[:, :], in_=MODS), reads=[dmod], writes=[dmodb])
        coll("cc0", modb_d, modg_d, [dmodb], [dmodg])
        ld("ld2", MOD.rearrange("p (r m) -> p r m", r=4), modg_d.rearrange("(r p) m -> p r m", p=128), [dmod], reads=[dmodg])
        gain_col = [0, 16, 32]
        for i in range(3):
            vstt(AB[:, i, :], MOD[:, (3 * i + 1) * 16:(3 * i + 2) * 16], 1.0, GT[:, gain_col[i]:gain_col[i] + 16], ALU.add, ALU.mult,
                 [dmod, dgt], [dmod])
            vts(GG[:, i, :], MOD[:, (3 * i + 2) * 16:(3 * i + 3) * 16], 0.5 if i != 1 else 1.0, None, ALU.mult, None, [dmod], [dmod])

        checkpoint(0)

        def rstd_from(ss_ps, out_ap, ssdep, outdep, npart=128):
            act(out_ap, ss_ps, AF.Sqrt, [ssdep], [outdep], bias=EPS)
            vrecip(out_ap, out_ap, [outdep], [outdep])

        def modnorm(Acol, Bcol, ar, out_fp32_inplace=False):
            sq = ar.alloc([2, 512], BF16)
            tmp = ar.alloc([2, 512], F32)
            rstd = ar.alloc([2, 512], F32)
            dsq = [Dep(), Dep()]
            dtmp = [Dep(), Dep()]
            drs = [Dep(), Dep()]
            for tt in range(2):
                ts_ = slice(tt * 512, (tt + 1) * 512)
                ssb = 6 + tt
                for kc in range(NKC):
                    b = kc % 2
                    act(sq[:, b, :], xb[:, kc, ts_], AF.Square, [xd[kc][tt]], [dsq[b]])
                    mm1(PS[ssb][:], ON2048, sq[:, b, :], kc == 0, kc == NKC - 1, [dsq[b], dconst], [psd[ssb]])
                rstd_from(PS[ssb][:], rstd[:, tt, :], psd[ssb], drs[tt])
                for kc in range(NKC):
                    b = kc % 2
                    if out_fp32_inplace:
                        vstt(xb[:, kc, ts_], xb[:, kc, ts_], Acol(kc), rstd[:, tt, :], ALU.mult, ALU.mult,
                             [xd[kc][tt], drs[tt], dmod, dgt], [xd[kc][tt]])
                    else:
                        vstt(tmp[:, b, :], xb[:, kc, ts_], Acol(kc), rstd[:, tt, :], ALU.mult, ALU.mult,
                             [xd[kc][tt], drs[tt], dmod], [dtmp[b]])
                        act(hb[:, kc, ts_], tmp[:, b, :], AF.Identity, [dtmp[b], dmod], [hd[kc][tt]], bias=Bcol(kc))

        def ffn(i, li):
            AR.reset()
            modnorm(lambda kc: AB[:, li, kc:kc + 1], lambda kc: MOD[:, 3 * li * 16 + kc:3 * li * 16 + kc + 1], AR)
            g = AR.alloc([4, T], BF16)
            sil = AR.alloc([2, 512], F32)
            dg = [[Dep(), Dep()] for _ in range(4)]
            dsil = [Dep(), Dep()]
            up_pool = [0, 1, 2, 3]
            dn_pool = [4, 5, 6, 7]
            cnt = 0
            for G in range(11):
                for mi in range(4):
                    m = 4 * G + mi
                    s = wload(w13_d[i][m], 4096)
                    for tt in range(2):
                        ts_ = slice(tt * 512, (tt + 1) * 512)
                        b1 = up_pool[(cnt * 2) % 4]
                        b3 = up_pool[(cnt * 2 + 1) % 4]
                        cnt += 1
                        mmg(PS[b1][:], [(slots[:, s, k * 128:(k + 1) * 128], hb[:, k, ts_]) for k in range(NKC)],
                            [sd[s]] + [hd[k][tt] for k in range(NKC)], [psd[b1]])
                        mmg(PS[b3][:], [(slots[:, s, 2048 + k * 128:2048 + (k + 1) * 128], hb[:, k, ts_]) for k in range(NKC)],
                            [sd[s]] + [hd[k][tt] for k in range(NKC)], [psd[b3]])
                        sb_ = cnt % 2
                        act(sil[:, sb_, :], PS[b1][:], AF.Silu, [psd[b1]], [dsil[sb_]])
                        vtt(g[:, mi, ts_], sil[:, sb_, :], PS[b3][:], ALU.mult, [dsil[sb_], psd[b3]], [dg[mi][tt]])
                s2 = [wload(w2_d[i][2 * G + u], 4096) for u in range(2)]
                for n in range(NKC):
                    for tt in range(2):
                        ts_ = slice(tt * 512, (tt + 1) * 512)
                        b = nextps(dn_pool)
                        mmg(PS[b][:], [(slots[:, s2[mi // 2], (mi % 2) * 2048 + n * 128:(mi % 2) * 2048 + (n + 1) * 128], g[:, mi, ts_]) for mi in range(4)],
                            [sd[s2[0]], sd[s2[1]]] + [dg[mi][tt] for mi in range(4)], [psd[b]])
                        vstt(xb[:, n, ts_], PS[b][:], GG[:, li, n:n + 1], xb[:, n, ts_], ALU.mult, ALU.add,
                             [psd[b], xd[n][tt], dmod], [xd[n][tt]])
            P.barrier()

        ffn(0, 0)
        checkpoint(1)

        AR.reset()
        modnorm(lambda kc: AB[:, 1, kc:kc + 1], lambda kc: MOD[:, 3 * 16 + kc:3 * 16 + kc + 1], AR)
        P.barrier()
        dxs = Dep()
        for kc in range(0, NKC, 4):
            P.dma(SP, "xio", lambda h, kc=kc: h.dma_start(out=xs_d.rearrange("(k p) t -> p k t", p=128)[:, kc:kc + 4, :], in_=xb[:, kc:kc + 4, :]),
                  reads=[d for k in range(kc, kc + 4) for d in xd[k]], writes=[dxs])
        P.barrier()
        checkpoint(2)

        AR.reset()
        XA.reset()
        TCq = AR.alloc([1, T], F32)[:, 0, :]
        TSq = AR.alloc([1, T], F32)[:, 0, :]
        cq = AR.alloc([4, T], BF16)
        a_mark = AR.off
        TCk = XA.alloc([1, T], F32)[:, 0, :]
        TSk = XA.alloc([1, T], F32)[:, 0, :]
        posi = AR.alloc([1, T], I32)[:, 0, :]
        ang = AR.alloc([1, T], F32)[:, 0, :]
        kf = AR.alloc([1, T], F32)[:, 0, :]
        ki = AR.alloc([1, T], I32)[:, 0, :]
        r1 = AR.alloc([1, T], F32)[:, 0, :]
        r2 = AR.alloc([1, T], F32)[:, 0, :]
        sinv = AR.alloc([1, T], F32)[:, 0, :]
        cosv = AR.alloc([1, T], F32)[:, 0, :]
        drope = Dep()
        R = slice(0, 64)
        ld("ld3", posi[R], pos_d.partition_broadcast(64), [drope])
        vcopy(ang[R], posi[R], [drope], [drope])
        vts(ang[R], ang[R], GT[R, 81:82], None, ALU.mult, None, [drope, dgt], [drope])
        C1 = 6.28125
        C2 = float(2 * np.pi - 6.28125)
        vts(kf[R], ang[R], float(1.0 / (2 * np.pi)), None, ALU.mult, None, [drope], [drope])
        vcopy(ki[R], kf[R], [drope], [drope])
        vcopy(kf[R], ki[R], [drope], [drope])
        vstt(r1[R], kf[R], -C1, ang[R], ALU.mult, ALU.add, [drope], [drope])
        vstt(r1[R], kf[R], -C2, r1[R], ALU.mult, ALU.add, [drope], [drope])
        PI_SAFE = 3.1415925
        vts(r1[R], r1[R], PI_SAFE, -PI_SAFE, ALU.min, ALU.max, [drope], [drope])
        act(sinv[R], r1[R], AF.Sin, [drope], [drope])
        vts(r2[R], r1[R], float(np.pi / 2), None, ALU.add, None, [drope], [drope])
        vts(kf[R], r2[R], float(np.pi), None, ALU.is_gt, None, [drope], [drope])
        vstt(r2[R], kf[R], float(-2 * np.pi), r2[R], ALU.mult, ALU.add, [drope], [drope])
        vts(r2[R], r2[R], PI_SAFE, -PI_SAFE, ALU.min, ALU.max, [drope], [drope])
        act(cosv[R], r2[R], AF.Sin, [drope], [drope])
        vtt(GSC[R, 0:1], GT[R, 77:78], GT[R, 80:81], ALU.mult, [dgt], [drope])
        vtt(GSC[R, 1:2], GT[R, 79:80], GT[R, 80:81], ALU.mult, [dgt], [drope])
        vts(TCq[R], cosv[R], GT[R, 76:77], None, ALU.mult, None, [drope, dgt], [drope])
        vts(TSq[R], sinv[R], GSC[R, 0:1], None, ALU.mult, None, [drope], [drope])
        vts(TCk[R], cosv[R], GT[R, 78:79], None, ALU.mult, None, [drope, dgt], [drope])
        vts(TSk[R], sinv[R], GSC[R, 1:2], None, ALU.mult, None, [drope], [drope])

        P.barrier()
        AR.reset(a_mark)
        lat = AR.alloc([4, T], F32)
        kst = AR.alloc([8, T], BF16)
        ckv = XA.alloc([4, T], BF16)
        sq = XA.alloc([2, 512], BF16)
        rstd = XA.alloc([2, 512], F32)
        vst = XA.alloc([8, T], BF16)
        kraw = XA.alloc([1, T], F32)[:, 0, :]
        ksw = XA.alloc([1, T], F32)[:, 0, :]
        kpe = XA.alloc([1, T], BF16)[:, 0, :]
        t1 = XA.alloc([2, 512], F32)
        dlat = [[Dep(), Dep()] for _ in range(4)]
        dsq = [Dep(), Dep()]
        drs = [Dep(), Dep()]
        dcq = [[Dep(), Dep()] for _ in range(4)]
        dckv = [[Dep(), Dep()] for _ in range(4)]
        a_pool = [0, 1, 2, 3, 4, 5]
        sqc = 0
        for fam, (dst, ddst, gcol) in enumerate(((cq, dcq, 64), (ckv, dckv, 68))):
            for u in range(2):
                s = wload(win_lat_d[2 * fam + u], 4096)
                for c in range(2):
                    m = 2 * u + c
                    for tt in range(2):
                        ts_ = slice(tt * 512, (tt + 1) * 512)
                        b = nextps(a_pool)
                        mmg(PS[b][:], [(slots[:, s, k * 256 + c * 128:k * 256 + c * 128 + 128], hb[:, k, ts_]) for k in range(NKC)],
                            [sd[s]] + [hd[k][tt] for k in range(NKC)], [psd[b]])
                        act(lat[:, m, ts_], PS[b][:], AF.Copy, [psd[b]], [dlat[m][tt]])
                        sb_ = sqc % 2
                        sqc += 1
                        act(sq[:, sb_, :], PS[b][:], AF.Square, [psd[b]], [dsq[sb_]])
                        mm1(PS[6 + tt][:], ON512, sq[:, sb_, :], m == 0, m == 3, [dsq[sb_], dconst], [psd[6 + tt]])
            for tt in range(2):
                ts_ = slice(tt * 512, (tt + 1) * 512)
                rstd_from(PS[6 + tt][:], rstd[:, tt, :], psd[6 + tt], drs[tt])
                for m in range(4):
                    vstt(dst[:, m, ts_], lat[:, m, ts_], GT[:, gcol + m:gcol + m + 1], rstd[:, tt, :], ALU.mult, ALU.mult,
                         [dlat[m][tt], drs[tt], dgt], [ddst[m][tt]])
        dkr = Dep()
        s = wload(win_kr_d[0], 2048)
        for tt in range(2):
            ts_ = slice(tt * 512, (tt + 1) * 512)
            b0 = nextps(a_pool)
            mmg(PS[b0][0:64, :], [(slots[:, s, k * 128:k * 128 + 64], hb[:, k, ts_]) for k in range(NKC)],
                [sd[s]] + [hd[k][tt] for k in range(NKC)], [psd[b0]])
            b1 = nextps(a_pool)
            mmg(PS[b1][0:64, :], [(slots[:, s, k * 128 + 64:k * 128 + 128], hb[:, k, ts_]) for k in range(NKC)],
                [sd[s]] + [hd[k][tt] for k in range(NKC)], [psd[b1]])
            act(kraw[R, ts_], PS[b0][0:64, :], AF.Copy, [psd[b0]], [dkr])
            act(ksw[R, ts_], PS[b1][0:64, :], AF.Copy, [psd[b1]], [dkr])
            act(sq[R, 0, :], PS[b0][0:64, :], AF.Square, [psd[b0]], [dsq[0]])
            mm1(PS[6][0:64, :], ON64[R, 0:64], sq[R, 0, :], True, True, [dsq[0], dconst], [psd[6]])
            rstd_from(PS[6][0:64, :], rstd[R, 0, :], psd[6], drs[0])
            vtt(t1[R, 0, :], kraw[R, ts_], TCk[R, ts_], ALU.mult, [dkr, drope], [dkr])
            vtt(t1[R, 1, :], ksw[R, ts_], TSk[R, ts_], ALU.mult, [dkr, drope], [dkr])
            vtt(t1[R, 0, :], t1[R, 0, :], t1[R, 1, :], ALU.add, [dkr], [dkr])
            vtt(kpe[R, ts_], t1[R, 0, :], rstd[R, 0, :], ALU.mult, [dkr, drs[0]], [dkr])
        dkvb = Dep()
        P.dma(SP, "st0", lambda h: h.dma_start(out=kb_pe[:, :], in_=kpe[R, :]), reads=[dkr], writes=[dkvb])
        dkst = Dep()
        s = wload(wukv_d[0], 4096)
        for hh in range(8):
            for tt in range(2):
                ts_ = slice(tt * 512, (tt + 1) * 512)
                b = nextps(a_pool)
                mmg(PS[b][:], [(slots[:, s, k * 1024 + hh * 128:k * 1024 + hh * 128 + 128], ckv[:, k, ts_]) for k in range(4)],
                    [sd[s]] + [dckv[k][tt] for k in range(4)], [psd[b]])
                sb_ = sqc % 2
                sqc += 1
                act(sq[:, sb_, :], PS[b][:], AF.Square, [psd[b]], [dsq[sb_]])
                mm1(PS[6 + sb_][:], ON128, sq[:, sb_, :], True, True, [dsq[sb_], dconst], [psd[6 + sb_]])
                rstd_from(PS[6 + sb_][:], rstd[:, sb_, :], psd[6 + sb_], drs[sb_])
                vstt(kst[:, hh, ts_], PS[b][:], GT[:, 73:74], rstd[:, sb_, :], ALU.mult, ALU.mult, [psd[b], drs[sb_], dgt], [dkst])
        for i in range(2):
            P.dma(SP, "st1", lambda h, i=i: h.dma_start(out=kb_k[i].rearrange("(h p) t -> p h t", p=128), in_=kst[:, 4 * i:4 * i + 4, :]), reads=[dkst], writes=[dkvb])
        dvst = Dep()
        s = wload(wukv_d[1], 4096)
        vst4 = vst.rearrange("p h (tb d) -> p h tb d", d=128)
        for tb in range(8):
            for half in range(2):
                b = nextps(a_pool)
                mmg(PS[b][:], [(ckv[:, k, tb * 128:(tb + 1) * 128], slots[:, s, k * 1024 + half * 512:k * 1024 + half * 512 + 512]) for k in range(4)],
                    [sd[s]] + [dckv[k][tb // 4] for k in range(4)], [psd[b]])
                vcopy(vst4[:, half * 4:half * 4 + 4, tb, :], PS[b][:].rearrange("p (h d) -> p h d", d=128), [psd[b]], [dvst])
        for i in range(2):
            P.dma(SP, "st2", lambda h, i=i: h.dma_start(out=kb_v[i].rearrange("(h p) c -> p h c", p=128), in_=vst[:, 4 * i:4 * i + 4, :]), reads=[dvst], writes=[dkvb])
        dbg("cq", cq, [128, 4, T], [d for r_ in dcq for d in r_])
        dbg("kst", kst, [128, 8, T], [dkst])
        dbg("kpe", kpe[R, :], [64, T], [dkr])
        dbg("vst", vst, [128, 8, T], [dvst])
        dkvg = Dep()
        coll("cc1", kb_k[0], kg_k[0], [dkvb], [dkvg])
        coll("cc3", kb_k[1], kg_k[1], [dkvb], [dkvg])
        coll("cc4", kb_v[0], kg_v[0], [dkvb], [dkvg])
        coll("cc5", kb_v[1], kg_v[1], [dkvb], [dkvg])
        coll("cc6", kb_pe, kg_pe, [dkvb], [dkvg])
        P.barrier()
        checkpoint(3)

        XA.reset()
        AR.reset(a_mark)
        lsp = AR.alloc([8, 512], F32)
        bT = AR.alloc([4, T], F32)
        glrT = AR.alloc([1, T], F32)[:, 0, :]
        kstate = XA.alloc([8, 512], BF16)
        vtm = XA.alloc([8, T], BF16)
        qdec = XA.alloc([4, T], BF16)
        kdec = XA.alloc([4, T], BF16)
        b_mark = XA.off
        wgkp = XA.alloc([1, 512], F32)[:, 0, :]
        bgkb = XA.alloc([1, 512], F32)[:, 0, :]
        zt = XA.alloc([2, 512], F32)
        Sst = XA.alloc([4, 2, 256], F32)
        glst = XA.alloc([1, 1028], F32)[:, 0, :]
        dglr, dwgk, dlsp, dbT = Dep(), Dep(), [Dep() for _ in range(8)], [[Dep(), Dep()] for _ in range(4)]
        dks = [Dep() for _ in range(8)]
        dvt = [Dep() for _ in range(8)]
        dzt = [Dep(), Dep()]
        vmemset(wgkp, 0.0, [dwgk])
        ld("ld0", wgkp[0:16, :], wgk_d[:, :], [dwgk])
        ld("ld1", bgkb, bgk_d.partition_broadcast(128), [dwgk])
        b_pool = [0, 1, 2, 3]
        s = wload(win_glr_d[0], 2048)
        for tt in range(2):
            ts_ = slice(tt * 512, (tt + 1) * 512)
            b = nextps(b_pool)
            mmg(PS[b][:], [(slots[:, s, k * 128:(k + 1) * 128], hb[:, k, ts_]) for k in range(NKC)],
                [sd[s]] + [hd[k][tt] for k in range(NKC)], [psd[b]])
            act(glrT[:, ts_], PS[b][:], AF.Copy, [psd[b]], [dglr])
        for tb in range(8):
            b = nextps(b_pool)
            mm1(PS[b][:], glrT[:, tb * 128:(tb + 1) * 128], wgkp, True, True, [dglr, dwgk], [psd[b]])
            vtt(zt[:, 0, :], PS[b][:], bgkb, ALU.add, [psd[b], dwgk], [dzt[0]])
            act(zt[:, 1, :], zt[:, 0, :], AF.Exp, [dzt[0]], [dzt[1]], scale=-1.0)
            act(lsp[:, tb, :], zt[:, 1, :], AF.Ln, [dzt[1]], [dlsp[tb]], bias=1.0)
            for hh in range(4):
                bb = 4 + hh
                tt = tb // 4
                mm1(PS[bb][:, (tb % 4) * 128:(tb % 4 + 1) * 128], lsp[:, tb, hh * 128:(hh + 1) * 128], US, True, True,
                    [dlsp[tb], dconst], [psd[bb]])
            if tb % 4 == 3:
                tt = tb // 4
                for hh in range(4):
                    vcopy(bT[:, hh, tt * 512:(tt + 1) * 512], PS[4 + hh][:], [psd[4 + hh]], [dbT[hh][tt]])
        for tb in range(8):
            b = nextps(b_pool)
            mm1(PS[b][:], M2S, lsp[:, tb, :], True, True, [dlsp[tb], dconst], [psd[b]])
            act(lsp[:, tb, :], PS[b][:], AF.Exp, [psd[b]], [dlsp[tb]])
        for hh in range(4):
            bl = bT[:, hh, :].rearrange("p (n s) -> p n s", s=64)[:, :, 63]
            act(DECAY[:, hh, :], bl, AF.Exp, [dbT[hh][0], dbT[hh][1]], [ddec])
        for u in range(4):
            s = wload(win_gv_d[u], 4096)
            for tb in range(8):
                b = nextps(b_pool)
                mmg(PS[b][:, 0:256], [(hb[:, k, tb * 128:(tb + 1) * 128], slots[:, s, k * 256:(k + 1) * 256]) for k in range(NKC)],
                    [sd[s]] + [hd[k][tb // 4] for k in range(NKC)], [psd[b]])
                act(vtm[:, tb, u * 256:(u + 1) * 256], PS[b][:, 0:256], AF.Copy, [psd[b]], [dvt[tb]])
        dqd = [[Dep(), Dep()] for _ in range(4)]
        dkd = [[Dep(), Dep()] for _ in range(4)]
        for u in range(4):
            s = wload(win_gqk_d[u], 4096)
            isk = u >= 2
            if isk:
                half = u - 2
                for tb in range(8):
                    b = nextps(b_pool)
                    mmg(PS[b][:, 0:256], [(hb[:, k, tb * 128:(tb + 1) * 128], slots[:, s, k * 256:(k + 1) * 256]) for k in range(NKC)],
                        [sd[s]] + [hd[k][tb // 4] for k in range(NKC)], [psd[b]])
                    vtt(kstate[:, tb, half * 256:(half + 1) * 256], PS[b][:, 0:256], lsp[:, tb, half * 256:(half + 1) * 256], ALU.mult,
                        [psd[b], dlsp[tb]], [dks[tb]])
            for c in range(2):
                hh = 2 * (u % 2) + c
                for tt in range(2):
                    ts_ = slice(tt * 512, (tt + 1) * 512)
                    b = nextps(b_pool)
                    mmg(PS[b][:], [(slots[:, s, k * 256 + c * 128:k * 256 + c * 128 + 128], hb[:, k, ts_]) for k in range(NKC)],
                        [sd[s]] + [hd[k][tt] for k in range(NKC)], [psd[b]])
                    zb = (hh * 2 + tt) % 2
                    if not isk:
                        act(zt[:, zb, :], bT[:, hh, ts_], AF.Exp, [dbT[hh][tt]], [dzt[zb]])
                        vstt(qdec[:, hh, ts_], PS[b][:], float(128 ** -0.5), zt[:, zb, :], ALU.mult, ALU.mult, [psd[b], dzt[zb]], [dqd[hh][tt]])
                    else:
                        act(zt[:, zb, :], bT[:, hh, ts_], AF.Exp, [dbT[hh][tt]], [dzt[zb]], scale=-1.0)
                        vtt(kdec[:, hh, ts_], PS[b][:], zt[:, zb, :], ALU.mult, [psd[b], dzt[zb]], [dkd[hh][tt]])
        dS = [[Dep(), Dep()] for _ in range(4)]
        dgl = Dep()
        for hh in range(4):
            vmemset(Sst[:, hh, 0, :], 0.0, [dS[hh][0]])
        for n in range(16):
            tb, j = n // 2, n % 2
            for hh in range(4):
                b = nextps(b_pool)
                mm1(PS[b][:, 0:256], kstate[64 * j:64 * j + 64, tb, hh * 128:(hh + 1) * 128], vtm[64 * j:64 * j + 64, tb, hh * 256:(hh + 1) * 256],
                    True, True, [dks[tb], dvt[tb]], [psd[b]])
                src, dst = n % 2, (n + 1) % 2
                out_ap = Sst[:, hh, dst, :] if n < 15 else glst[:, hh * 257:hh * 257 + 256]
                vstt(out_ap, Sst[:, hh, src, :], DECAY[:, hh, n:n + 1], PS[b][:, 0:256], ALU.mult, ALU.add,
                     [dS[hh][src], psd[b], ddec], [dS[hh][dst]] if n < 15 else [dgl])
        for hh in range(4):
            bl = bT[:, hh, :].rearrange("p (n s) -> p n s", s=64)[:, :, 63]
            P.op(DVE, lambda h, bl=bl, hh=hh: h.reduce_sum(out=glst[:, hh * 257 + 256:hh * 257 + 257], in_=bl, axis=mybir.AxisListType.X),
                 reads=[dbT[hh][0], dbT[hh][1]], writes=[dgl])
            act(glst[:, hh * 257 + 256:hh * 257 + 257], glst[:, hh * 257 + 256:hh * 257 + 257], AF.Exp, [dgl], [dgl])
        dglb, dglg = Dep(), Dep()
        P.dma(SP, "st0", lambda h: h.dma_start(out=glb_d[:, :], in_=glst), reads=[dgl], writes=[dglb])
        coll("cc2", glb_d, glg_d, [dglb], [dglg])
        dbg("kstate", kstate, [128, 8, 512], dks)
        dbg("qdec", qdec, [128, 4, T], [d for r_ in dqd for d in r_])
        dbg("kdec", kdec, [128, 4, T], [d for r_ in dkd for d in r_])
        dbg("glst", glst, [128, 1028], [dgl])
        checkpoint(4)

        P.barrier()
        AR.reset(a_mark)
        glaT = AR.alloc([8, T], BF16)
        c_mark = AR.off
        dgla = [[Dep(), Dep()] for _ in range(8)]
        dmla = [[Dep(), Dep()] for _ in range(8)]
        glgs = AR.alloc([4, 1028], F32)
        XA.reset(b_mark)
        osb = XA.alloc([2, T], F32)
        Sp = XA.alloc([2, 256], F32)
        Sbf = XA.alloc([2, 256], BF16)
        attm = XA.alloc([2, 128], BF16)
        sgt = XA.alloc([2, 512], F32)
        dmt = XA.alloc([1, 8], F32)[:, 0, :]
        sqg = XA.alloc([2, 512], BF16)
        rsg = XA.alloc([2, 512], F32)
        dglgs = Dep()
        ld("ld2", glgs, glg_d.rearrange("(r p) c -> p r c", p=128), [dglgs], reads=[dglg])
        dSp = [Dep(), Dep()]
        dSbf = [Dep(), Dep()]
        datt = [Dep(), Dep()]
        dosb = [[Dep(), Dep()] for _ in range(2)]
        dsg = [Dep(), Dep()]
        dsqg = [Dep(), Dep()]
        drsg = [Dep(), Dep()]
        ddm = Dep()
        c_pool = [0, 1, 2, 3]
        o_pool = [4, 5]
        sbc = 0
        for hh in range(4):
            vmemset(Sp[:, 0, :], 0.0, [dSp[0]])
            for r in range(4):
                Dr = glgs[:, r, hh * 257 + 256:hh * 257 + 257]
                Lr = glgs[:, r, hh * 257:hh * 257 + 256]
                ar_ = AMASK[:, 3 + r:4 + r]
                vts(dmt[:, 0:1], Dr, -1.0, None, ALU.add, None, [dglgs], [ddm])
                vts(dmt[:, 0:1], dmt[:, 0:1], ar_, 1.0, ALU.mult, ALU.add, [ddm, dconst], [ddm])
                vts(Lr, Lr, ar_, None, ALU.mult, None, [dglgs, dconst], [dglgs])
                vstt(Sp[:, 0, :], Sp[:, 0, :], dmt[:, 0:1], Lr, ALU.mult, ALU.add, [dSp[0], ddm, dglgs], [dSp[0]])
            cur = 0
            for tb in range(8):
                tt = tb // 4
                cols = slice(tb * 128, (tb + 1) * 128)
                b = nextps(c_pool)
                mm1(PS[b][:, 0:128], kdec[:, hh, cols], qdec[:, hh, cols], True, True, [dkd[hh][tt], dqd[hh][tt]], [psd[b]])
                ab = tb % 2
                vtt(attm[:, ab, :], PS[b][:, 0:128], U01, ALU.mult, [psd[b], dconst], [datt[ab]])
                sbfs = []
                for j in range(2):
                    n = 2 * tb + j
                    sb_ = sbc % 2
                    sbc += 1
                    act(Sbf[:, sb_, :], Sp[:, cur, :], AF.Copy, [dSp[cur]], [dSbf[sb_]])
                    sbfs.append(sb_)
                    b2 = nextps(c_pool)
                    mm1(PS[b2][:, 0:256], kstate[64 * j:64 * j + 64, tb, hh * 128:(hh + 1) * 128], vtm[64 * j:64 * j + 64, tb, hh * 256:(hh + 1) * 256],
                        True, True, [dks[tb], dvt[tb]], [psd[b2]])
                    nxt = 1 - cur
                    vstt(Sp[:, nxt, :], Sp[:, cur, :], DECAY[:, hh, n:n + 1], PS[b2][:, 0:256], ALU.mult, ALU.add,
                         [dSp[cur], psd[b2], ddec], [dSp[nxt]])
                    cur = nxt
                for half in range(2):
                    ob = o_pool[half]
                    c0 = (tb % 4) * 128

                    def ofn(h, ob=ob, c0=c0, half=half, tb=tb, hh=hh, ab=ab, sbfs=tuple(sbfs), cols=cols):
                        h.matmul(PS[ob][:, c0:c0 + 128], lhsT=vtm[:, tb, hh * 256 + half * 128:hh * 256 + half * 128 + 128], rhs=attm[:, ab, :], start=True, stop=False)
                        h.matmul(PS[ob][:, c0:c0 + 64], lhsT=Sbf[:, sbfs[0], half * 128:(half + 1) * 128], rhs=qdec[:, hh, tb * 128:tb * 128 + 64], start=False, stop=False)
                        return h.matmul(PS[ob][:, c0 + 64:c0 + 128], lhsT=Sbf[:, sbfs[1], half * 128:(half + 1) * 128], rhs=qdec[:, hh, tb * 128 + 64:tb * 128 + 128], start=False, stop=True)
                    P.op(PE, ofn, reads=[dvt[tb], datt[ab], dSbf[0], dSbf[1], dqd[hh][tt]], writes=[psd[ob]])
                if tb % 4 == 3:
                    for half in range(2):
                        act(osb[:, half, tt * 512:(tt + 1) * 512], PS[o_pool[half]][:], AF.Copy, [psd[o_pool[half]]], [dosb[half][tt]])
            s = wload(win_gout_d[hh], 4096)
            for tt in range(2):
                ts_ = slice(tt * 512, (tt + 1) * 512)
                for half in range(2):
                    act(sqg[:, half, :], osb[:, half, ts_], AF.Square, [dosb[half][tt]], [dsqg[half]])
                    mm1(PS[6][:], ON256, sqg[:, half, :], half == 0, half == 1, [dsqg[half], dconst], [psd[6]])
                rstd_from(PS[6][:], rsg[:, tt, :], psd[6], drsg[tt])
                for half in range(2):
                    b = nextps(c_pool)
                    mmg(PS[b][:], [(slots[:, s, k * 256 + half * 128:k * 256 + half * 128 + 128], hb[:, k, ts_]) for k in range(NKC)],
                        [sd[s]] + [hd[k][tt] for k in range(NKC)], [psd[b]])
                    act(sgt[:, half, :], PS[b][:], AF.Silu, [psd[b]], [dsg[half]])
                    vstt(osb[:, half, ts_], osb[:, half, ts_], GT[:, 74 + half:75 + half], rsg[:, tt, :], ALU.mult, ALU.mult,
                         [dosb[half][tt], drsg[tt], dgt], [dosb[half][tt]])
                    vtt(glaT[:, hh * 2 + half, ts_], osb[:, half, ts_], sgt[:, half, :], ALU.mult, [dosb[half][tt], dsg[half]], [dgla[hh * 2 + half][tt]])
        P.barrier()

        dbg("glaT", glaT, [128, 8, T], [d for r_ in dgla for d in r_])
        checkpoint(5)
        XA.reset()
        AR.reset(c_mark)
        mlaT = AR.alloc([8, T], BF16)
        qn = XA.alloc([2, T], BF16)
        qpe = XA.alloc([2, T], BF16)
        qraw = XA.alloc([2, 512], F32)
        t2 = XA.alloc([2, 512], F32)
        sqd = XA.alloc([2, 512], BF16)
        rsd = XA.alloc([2, 512], F32)
        kTb = XA.alloc([2, T], BF16)
        kpb = XA.alloc([2, T], BF16)
        vb = XA.alloc([2, T], BF16)
        PT = XA.alloc([4, 512], BF16)
        rden = XA.alloc([2, 512], F32)
        dqn = [[Dep(), Dep()] for _ in range(2)]
        dqp = [[Dep(), Dep()] for _ in range(2)]
        dqraw, dt2 = [Dep(), Dep()], Dep()
        dsqd, drsd = [Dep(), Dep()], [Dep(), Dep()]
        dkT, dkp, dvb = [Dep(), Dep()], [Dep(), Dep()], [Dep(), Dep()]
        dPT = [Dep() for _ in range(4)]
        drden = [Dep(), Dep()]
        s_pool = [0, 1]
        QB0, QB1 = 6, 7
        SCALE = float(192 ** -0.5)
        dstate = {"ptc": 0, "sq": 0, "wq": None}

        def qproj(hh):
            if hh % 4 == 0:
                dstate["wq"] = wload(wuq_d[hh // 4], 4096)
            wq_slot = dstate["wq"]
            hq = hh % 2
            base = (hh % 4) * 256
            for tt in range(2):
                ts_ = slice(tt * 512, (tt + 1) * 512)
                mmg(PS[QB0][:], [(slots[:, wq_slot, k * 1024 + base:k * 1024 + base + 128], cq[:, k, ts_]) for k in range(4)],
                    [sd[wq_slot]] + [dcq[k][tt] for k in range(4)], [psd[QB0]])
                sb_ = dstate["sq"] % 2
                dstate["sq"] += 1
                act(sqd[:, sb_, :], PS[QB0][:], AF.Square, [psd[QB0]], [dsqd[sb_]])
                mm1(PS[QB1][:], ON128, sqd[:, sb_, :], True, True, [dsqd[sb_], dconst], [psd[QB1]])
                rstd_from(PS[QB1][:], rsd[:, sb_, :], psd[QB1], drsd[sb_])
                vts(rsd[:, sb_, :], rsd[:, sb_, :], SCALE, None, ALU.mult, None, [drsd[sb_]], [drsd[sb_]])
                vstt(qn[:, hq, ts_], PS[QB0][:], GT[:, 72:73], rsd[:, sb_, :], ALU.mult, ALU.mult, [psd[QB0], drsd[sb_], dgt], [dqn[hq][tt]])
                mmg(PS[QB0][0:64, :], [(slots[:, wq_slot, k * 1024 + base + 128:k * 1024 + base + 192], cq[:, k, ts_]) for k in range(4)],
                    [sd[wq_slot]] + [dcq[k][tt] for k in range(4)], [psd[QB0]])
                act(qraw[R, 0, :], PS[QB0][0:64, :], AF.Copy, [psd[QB0]], [dqraw[0]])
                sb_ = dstate["sq"] % 2
                dstate["sq"] += 1
                act(sqd[R, sb_, :], PS[QB0][0:64, :], AF.Square, [psd[QB0]], [dsqd[sb_]])
                mm1(PS[QB1][0:64, :], ON64[R, 0:64], sqd[R, sb_, :], True, True, [dsqd[sb_], dconst], [psd[QB1]])
                rstd_from(PS[QB1][0:64, :], rsd[R, sb_, :], psd[QB1], drsd[sb_])
                vts(rsd[R, sb_, :], rsd[R, sb_, :], SCALE, None, ALU.mult, None, [drsd[sb_]], [drsd[sb_]])
                mmg(PS[QB0][0:64, :], [(slots[:, wq_slot, k * 1024 + base + 192:k * 1024 + base + 256], cq[:, k, ts_]) for k in range(4)],
                    [sd[wq_slot]] + [dcq[k][tt] for k in range(4)], [psd[QB0]])
                vtt(t2[R, 0, :], qraw[R, 0, :], TCq[R, ts_], ALU.mult, [dqraw[0], drope], [dt2])
                vtt(t2[R, 1, :], PS[QB0][0:64, :], TSq[R, ts_], ALU.mult, [psd[QB0], drope], [dt2])
                vtt(t2[R, 0, :], t2[R, 0, :], t2[R, 1, :], ALU.add, [dt2], [dt2])
                vtt(qpe[R, hq, ts_], t2[R, 0, :], rsd[R, sb_, :], ALU.mult, [dt2, drsd[sb_]], [dqp[hq][tt]])

        def kvload(idx):
            hh, slot = idx // 4, idx % 4
            kb_ = idx % 2
            hi, hl = hh // 4, hh % 4
            if slot < 3:
                ksrc, vsrc, psrc, rdeps = kg_k[hi], kg_v[hi], kg_pe, [dkvg]
                kr0, pr0 = slot * 512 + hl * 128, slot * 64
            else:
                ksrc, vsrc, psrc, rdeps = kb_k[hi], kb_v[hi], kb_pe, [dkvb]
                kr0, pr0 = hl * 128, 0
            P.dma(SP, f"kvk{kb_}", lambda h: h.dma_start(out=kTb[:, kb_, :], in_=ksrc[kr0:kr0 + 128, :]), reads=rdeps, writes=[dkT[kb_]])
            P.dma(SP, f"kvp{kb_}", lambda h: h.dma_start(out=kpb[R, kb_, :], in_=psrc[pr0:pr0 + 64, :]), reads=rdeps, writes=[dkp[kb_]])
            P.dma(SP, f"kvv{kb_}", lambda h: h.dma_start(out=vb[:, kb_, :], in_=vsrc[kr0:kr0 + 128, :]), reads=rdeps, writes=[dvb[kb_]])

        qproj(0)
        kvload(0)
        for hh in range(8):
            hq = hh % 2
            first = [True, True]
            for slot in range(4):
                idx = hh * 4 + slot
                kb_ = idx % 2
                if idx + 1 < 32:
                    kvload(idx + 1)
                if slot == 1 and hh + 1 < 8:
                    qproj(hh + 1)
                items = []
                for tt in range(2):
                    for kb in range(8):
                        q0, diag = 0, False
                        if slot == 3:
                            if kb * 128 >= (tt + 1) * 512:
                                continue
                            if kb * 128 >= tt * 512:
                                q0, diag = kb * 128 - tt * 512, True
                        last = (slot == 3) and (kb == 4 * tt + 3)
                        items.append((tt, kb, q0, diag, last))

                def emit_s(it):
                    tt, kb, q0, diag, last = it
                    sbk = nextps(s_pool)
                    qs = slice(tt * 512 + q0, (tt + 1) * 512)
                    mmg(PS[sbk][:, q0:512],
                        [(kTb[:, kb_, kb * 128:(kb + 1) * 128], qn[:, hq, qs]), (kpb[R, kb_, kb * 128:(kb + 1) * 128], qpe[R, hq, qs])],
                        [dkT[kb_], dkp[kb_], dqn[hq][tt], dqp[hq][tt]], [psd[sbk]])
                    return sbk

                def emit_rest(it, sbk):
                    tt, kb, q0, diag, last = it
                    ob, db = 2 + tt, 4 + tt
                    pb = dstate["ptc"] % 4
                    dstate["ptc"] += 1
                    if slot < 3:
                        act(PT[:, pb, q0:512], PS[sbk][:, q0:512], AF.Exp, [psd[sbk], dconst], [dPT[pb]], bias=AMASK[:, slot:slot + 1])
                    else:
                        act(PT[:, pb, q0:512], PS[sbk][:, q0:512], AF.Exp, [psd[sbk]], [dPT[pb]])
                    if diag:
                        vmemset(PT[64:128, pb, q0:q0 + 64], 0.0, [dPT[pb]])
                    st = first[tt]
                    first[tt] = False
                    mm1(PS[ob][:, q0:512], vb[:, kb_, kb * 128:(kb + 1) * 128], PT[:, pb, q0:512], st, last, [dvb[kb_], dPT[pb]], [psd[ob]])
                    mm1(PS[db][:, q0:512], ONES, PT[:, pb, q0:512], st, last, [dPT[pb], dconst], [psd[db]])

                pend = emit_s(items[0])
                for i_, it in enumerate(items):
                    cur = pend
                    if i_ + 1 < len(items):
                        pend = emit_s(items[i_ + 1])
                    emit_rest(it, cur)
            for tt in range(2):
                ts_ = slice(tt * 512, (tt + 1) * 512)
                vrecip(rden[:, tt, :], PS[4 + tt][:], [psd[4 + tt]], [drden[tt]])
                vtt(mlaT[:, hh, ts_], PS[2 + tt][:], rden[:, tt, :], ALU.mult, [psd[2 + tt], drden[tt]], [dmla[hh][tt]])
        P.barrier()

        dbg("mlaT", mlaT, [128, 8, T], [d for r_ in dmla for d in r_])
        checkpoint(6)
        for kc in range(0, NKC, 4):
            ld("xio", xb[:, kc:kc + 4, :], xs_d.rearrange("(k p) t -> p k t", p=128)[:, kc:kc + 4, :],
               [d for k in range(kc, kc + 4) for d in xd[k]], reads=[dxs])
        AR.reset(0)
        mg = AR.alloc([4, T], BF16)
        sga = AR.alloc([1, 512], F32)[:, 0, :]
        sgb = AR.alloc([1, 512], F32)[:, 0, :]
        ta = AR.alloc([1, 512], F32)[:, 0, :]
        tbb = AR.alloc([1, 512], F32)[:, 0, :]
        assert AR.off <= a_mark
        dmg = [[Dep(), Dep()] for _ in range(4)]
        dsga, dsgb, dta, dtb = Dep(), Dep(), Dep(), Dep()
        e_pool = [0, 1, 2, 3]
        e2_pool = [4, 5, 6, 7]
        for G in range(4):
            for u in range(2):
                spa = wload(wpa_d[G], 4096)
                spb = wload(wpb_d[G], 4096)
                sga_s = wload(win_gab_d[2 * G + u], 4096)
                sgb_s = wload(win_gab_d[8 + 2 * G + u], 4096)
                for c in range(2):
                    ni = 2 * u + c
                    for tt in range(2):
                        ts_ = slice(tt * 512, (tt + 1) * 512)
                        bya, byb, bga, bgb = 0, 1, 2, 3
                        mmg(PS[bya][:], [(slots[:, spa, k * 512 + ni * 128:k * 512 + ni * 128 + 128], mlaT[:, k, ts_]) for k in range(8)],
                            [sd[spa]] + [dmla[k][tt] for k in range(8)], [psd[bya]])
                        mmg(PS[byb][:], [(slots[:, spb, k * 512 + ni * 128:k * 512 + ni * 128 + 128], glaT[:, k, ts_]) for k in range(8)],
                            [sd[spb]] + [dgla[k][tt] for k in range(8)], [psd[byb]])
                        mmg(PS[bga][:], [(slots[:, sga_s, k * 256 + c * 128:k * 256 + c * 128 + 128], hb[:, k, ts_]) for k in range(NKC)],
                            [sd[sga_s]] + [hd[k][tt] for k in range(NKC)], [psd[bga]])
                        mmg(PS[bgb][:], [(slots[:, sgb_s, k * 256 + c * 128:k * 256 + c * 128 + 128], hb[:, k, ts_]) for k in range(NKC)],
                            [sd[sgb_s]] + [hd[k][tt] for k in range(NKC)], [psd[bgb]])
                        act(sga, PS[bga][:], AF.Sigmoid, [psd[bga]], [dsga])
                        act(sgb, PS[bgb][:], AF.Sigmoid, [psd[bgb]], [dsgb])
                        vtt(ta, PS[bya][:], sga, ALU.mult, [psd[bya], dsga], [dta])
                        vtt(tbb, PS[byb][:], sgb, ALU.mult, [psd[byb], dsgb], [dtb])
                        vtt(mg[:, ni, ts_], ta, tbb, ALU.add, [dta, dtb], [dmg[ni][tt]])
            so = [wload(wout_d[2 * G + u], 4096) for u in range(2)]
            for n in range(NKC):
                for tt in range(2):
                    ts_ = slice(tt * 512, (tt + 1) * 512)
                    b = nextps(e2_pool)
                    mmg(PS[b][:], [(slots[:, so[ni // 2], (ni % 2) * 2048 + n * 128:(ni % 2) * 2048 + (n + 1) * 128], mg[:, ni, ts_]) for ni in range(4)],
                        [sd[so[0]], sd[so[1]]] + [dmg[ni][tt] for ni in range(4)], [psd[b]])
                    vstt(xb[:, n, ts_], PS[b][:], GG[:, 1, n:n + 1], xb[:, n, ts_], ALU.mult, ALU.add, [psd[b], xd[n][tt], dmod], [xd[n][tt]])
        P.barrier()

        dbg("x2", xb[:], [128, NKC, T], xall)
        checkpoint(7)
        ffn(1, 2)
        checkpoint(8)

        AR.reset()
        modnorm(lambda kc: GT[:, 48 + kc:49 + kc], None, AR, out_fp32_inplace=True)
        P.stopped = False
        ov = outT_d.rearrange("(k p) t -> p k t", p=128)
        for kc in range(0, NKC, 4):
            tok = P.dma(SP, "xio", lambda h, kc=kc: h.dma_start(out=ov[:, kc:kc + 4, :], in_=xb[:, kc:kc + 4, :]),
                        reads=[d for k in range(kc, kc + 4) for d in xd[k]])
        P._wait(SP, tok)
        P.emit(block)
    _CACHE['used'] = list(DI.keys())
    return nc, dbg_list


def _colunits(W, col_lists, kch):
    out = []
    for cols in col_lists:
        sub = W[:, cols]
        n = sub.shape[1]
        out.append(sub.reshape(kch, 128, n).transpose(1, 0, 2).reshape(128, kch * n))
    return np.ascontiguousarray(np.stack(out)).astype(np.float32, copy=False)


def _rowunits(W, nrow_chunks_per_unit):
    K, N = W.shape
    nu = K // (128 * nrow_chunks_per_unit)
    a = W.reshape(nu, nrow_chunks_per_unit, 128, N).transpose(0, 2, 1, 3).reshape(nu, 128, nrow_chunks_per_unit * N)
    return np.ascontiguousarray(a).astype(np.float32, copy=False)


def _percol(v, n):
    return np.ascontiguousarray(np.asarray(v, np.float32).reshape(n, 128).T)


_CACHE = {}


def kernel(x, c, positions, w_ada, b_ada, g_ffn1, w1_a, w3_a, w2_a, g_mix, w_in,
           g_q_lat, w_uq, g_qn, g_qr, g_kv_lat, w_ukv, g_kn, g_kr, w_gk_up, b_gk, g_gla,
           w_proj_a, w_proj_b, w_out, g_ffn2, w1_b, w3_b, w2_b, g_final):
    f = lambda a: np.asarray(a)
    x, c, positions = f(x), f(c), f(positions)
    w_ada, b_ada = f(w_ada)[0], f(b_ada)[0]
    w_in = f(w_in)[0]
    w_uq, w_ukv = f(w_uq)[0], f(w_ukv)[0]
    ar = np.arange
    shared = {}
    for tag, (w1, w3, w2) in (("a", (w1_a, w3_a, w2_a)), ("b", (w1_b, w3_b, w2_b))):
        w1, w3, w2 = f(w1)[0], f(w3)[0], f(w2)[0]
        u1 = _colunits(w1, [ar(m * 128, (m + 1) * 128) for m in range(NM)], 16)
        u3 = _colunits(w3, [ar(m * 128, (m + 1) * 128) for m in range(NM)], 16)
        shared["w13" + tag] = np.ascontiguousarray(np.concatenate([u1, u3], axis=2))
        shared["w2" + tag] = _rowunits(w2, 2)
    shared["win_lat"] = _colunits(w_in, [ar(i * 256, (i + 1) * 256) for i in range(4)], 16)
    shared["win_kr"] = _colunits(w_in, [np.concatenate([ar(1024, 1088), ar(1056, 1088), ar(1024, 1056)])], 16)
    shared["win_gqk"] = _colunits(w_in, [ar(1088 + i * 256, 1088 + (i + 1) * 256) for i in range(4)], 16)
    shared["win_gv"] = _colunits(w_in, [ar(2112 + i * 256, 2112 + (i + 1) * 256) for i in range(4)], 16)
    shared["win_glr"] = _colunits(w_in, [ar(3136, 3264)], 16)
    shared["win_gout"] = _colunits(w_in, [ar(3152 + i * 256, 3152 + (i + 1) * 256) for i in range(4)], 16)
    shared["win_gab"] = _colunits(w_in, [ar(4176 + i * 256, 4176 + (i + 1) * 256) for i in range(16)], 16)
    uq_cols = []
    for g in range(2):
        cols = []
        for hh in range(4 * g, 4 * g + 4):
            b0 = hh * 192
            cols += [ar(b0, b0 + 128), ar(b0 + 128, b0 + 192), ar(b0 + 160, b0 + 192), ar(b0 + 128, b0 + 160)]
        uq_cols.append(np.concatenate(cols))
    shared["wuq"] = _colunits(w_uq, uq_cols, 4)
    shared["wukv"] = _colunits(w_ukv, [np.concatenate([ar(hh * 256, hh * 256 + 128) for hh in range(8)]),
                                       np.concatenate([ar(hh * 256 + 128, hh * 256 + 256) for hh in range(8)])], 4)
    shared["wpa"] = _colunits(f(w_proj_a)[0], [ar(i * 512, (i + 1) * 512) for i in range(4)], 8)
    shared["wpb"] = _colunits(f(w_proj_b)[0], [ar(i * 512, (i + 1) * 512) for i in range(4)], 8)
    shared["wout"] = _rowunits(f(w_out)[0], 2)
    gains = np.zeros((128, 96), np.float32)
    gains[:, 0:16] = _percol(f(g_ffn1)[0], 16)
    gains[:, 16:32] = _percol(f(g_mix)[0], 16)
    gains[:, 32:48] = _percol(f(g_ffn2)[0], 16)
    gains[:, 48:64] = _percol(f(g_final)[0], 16)
    gains[:, 64:68] = _percol(f(g_q_lat)[0], 4)
    gains[:, 68:72] = _percol(f(g_kv_lat)[0], 4)
    gains[:, 72] = f(g_qn)[0]
    gains[:, 73] = f(g_kn)[0]
    gains[:, 74:76] = _percol(f(g_gla)[0], 2)
    gqr, gkr = f(g_qr)[0], f(g_kr)[0]
    gains[:64, 76] = gqr
    gains[:64, 77] = np.concatenate([gqr[32:], gqr[:32]])
    gains[:64, 78] = gkr
    gains[:64, 79] = np.concatenate([gkr[32:], gkr[:32]])
    gains[:32, 80] = -1.0
    gains[32:64, 80] = 1.0
    inv_freq = (10000.0 ** (-np.arange(0, 64, 2, dtype=np.float32) / 64)).astype(np.float32)
    gains[:64, 81] = np.concatenate([inv_freq, inv_freq])
    shared["gains"] = gains
    shared["bgk"] = np.ascontiguousarray(f(b_gk)[0].reshape(1, 512).astype(np.float32))
    shared["wgk"] = np.ascontiguousarray(f(w_gk_up)[0].astype(np.float32))
    idx = np.arange(128)
    same = (idx[:, None] // 64) == (idx[None, :] // 64)
    shared["umask"] = (same & (idx[:, None] <= idx[None, :])).astype(np.float32)
    shared["m2mask"] = (same & (idx[:, None] > idx[None, :])).astype(np.float32)

    if "nc" not in _CACHE:
        _CACHE["nc"] = build_program()
    nc, dbg_list = _CACHE["nc"]
    in_maps = []
    for core in range(8):
        b, r = core // 4, core % 4
        m = dict(shared)
        m["xT"] = np.ascontiguousarray(x[b, r * T:(r + 1) * T, :].T)
        m["cT"] = _percol(c[b], 16)
        m["pos"] = np.ascontiguousarray(positions[b, r * T:(r + 1) * T].reshape(1, T).astype(np.int32))
        m["bada"] = _percol(b_ada[r * 4608:(r + 1) * 4608], 36)
        m["wada"] = _colunits(w_ada, [ar(r * 4608 + j * 256, r * 4608 + (j + 1) * 256) for j in range(18)], 16)
        am = np.zeros((128, 8), np.float32)
        for s_ in range(3):
            am[:, s_] = 0.0 if s_ < r else -30000.0
        for s_ in range(4):
            am[:, 3 + s_] = 1.0 if s_ < r else 0.0
        m["amask"] = am
        in_maps.append(m)
    used = set(_CACHE['used'])
    in_maps = [{k: v for k, v in m.items() if k in used} for m in in_maps]
    res = run_bass_kernel_spmd(nc, in_maps, core_ids=list(range(8)))
    out = np.empty((2, 4096, D_MODEL), np.float32)
    for core in range(8):
        b, r = core // 4, core % 4
        out[b, r * T:(r + 1) * T, :] = res.results[core]["outT"].T
    if DEBUG:
        _CACHE["dbg"] = [{k: res.results[core][k] for k in dbg_list} for core in range(8)]
    return out
```

```python
import numpy as np
import concourse.bass as bass
import concourse.mybir as mybir
from concourse.bass_utils import run_bass_kernel_spmd
from contextlib import ExitStack

F32 = mybir.dt.float32
BF16 = mybir.dt.bfloat16
I32 = mybir.dt.int32
AF = mybir.ActivationFunctionType
ALU = mybir.AluOpType

PE, ACT, DVE, POOL, SP = "tensor", "scalar", "vector", "gpsimd", "sync"
ENGS = (PE, ACT, DVE, POOL, SP)

D_MODEL = 2048
T = 1024
NKC = 16
D_FF = 5632
NM = 44
EPS = 1e-6
NS = 5
SLOT_E = 4096
KV_ROWS = 2112
GROUPS = [[0, 1, 2, 3], [4, 5, 6, 7]]
DEBUG = False
STOP = 99


class Dep:
    __slots__ = ("w", "r")

    def __init__(self):
        self.w = None
        self.r = {}


class Prog:
    def __init__(self):
        self.q = {e: [] for e in ENGS}
        self.cnt = {e: 0 for e in ENGS}
        self.sem = {}
        self.seen = {e: {} for e in ENGS}
        self.dsem = {}
        self.pending = []
        self.stopped = False

    def _wait(self, eng, tok):
        key, handle, val = tok
        if key == PE and eng == PE:
            return
        if self.seen[eng].get(key, 0) >= val:
            return
        self.seen[eng][key] = val
        self.q[eng].append(lambda h, handle=handle, val=val: h.wait_ge(handle, val))

    def _collect(self, reads, writes, extra):
        toks = []
        for d in reads:
            if d.w is not None:
                toks.append(d.w)
        for d in writes:
            if d.w is not None:
                toks.append(d.w)
            toks.extend(d.r.values())
        toks.extend(extra)
        return toks

    def _mark(self, tok, reads, writes):
        for d in reads:
            old = d.r.get(tok[0])
            if old is None or old[2] < tok[2]:
                d.r[tok[0]] = tok
        for d in writes:
            d.w = tok
            d.r = {}

    def op(self, eng, fn, reads=(), writes=(), extra=()):
        if self.stopped:
            return None
        for tok in self._collect(reads, writes, extra):
            self._wait(eng, tok)
        self.cnt[eng] += 1
        sem = self.sem[eng]
        self.q[eng].append(lambda h, fn=fn, sem=sem: fn(h).then_inc(sem, 1))
        tok = (eng, sem, self.cnt[eng])
        self._mark(tok, reads, writes)
        return tok

    def dma(self, eng, semname, fn, reads=(), writes=(), extra=(), inc=16, track=True):
        if self.stopped:
            return None
        handle, issued = self.dsem[semname]
        key = "dma_" + semname
        toks = self._collect(reads, writes, extra)
        if issued > 0:
            toks.append((key, handle, issued))
        for tok in toks:
            self._wait(eng, tok)
        issued += inc
        self.dsem[semname][1] = issued
        self.q[eng].append(lambda h, fn=fn, handle=handle, inc=inc: fn(h).then_inc(handle, inc))
        tok = (key, handle, issued)
        self._mark(tok, reads, writes)
        if track:
            self.pending.append(tok)
        return tok

    def barrier(self):
        if self.stopped:
            return
        toks = [(e, self.sem[e], self.cnt[e]) for e in (PE, ACT, DVE) if self.cnt[e] > 0]
        toks += self.pending
        self.pending = []
        for e in (PE, ACT, DVE, SP):
            for tok in toks:
                if tok[0] != e:
                    self._wait(e, tok)

    def emit(self, block):
        def run(eng):
            def body(h):
                for fn in self.q[eng]:
                    fn(h)
            return body
        block.tensor(run(PE))
        block.scalar(run(ACT))
        block.vector(run(DVE))
        block.gpsimd(run(POOL))
        block.sync(run(SP))


class Bump:
    def __init__(self, tensor, nbytes):
        self.t = tensor
        self.n = nbytes
        self.off = 0

    def reset(self, off=0):
        self.off = off

    def alloc(self, shape, dt):
        esz = 4 if dt in (F32, I32) else 2
        n = int(np.prod(shape)) * esz
        n = (n + 31) // 32 * 32
        assert self.off + n <= self.n, ("arena overflow", self.off, n, self.n)
        w0 = self.off // 4
        ap = self.t[:, w0:w0 + n // 4]
        if dt != F32:
            ap = ap.bitcast(dt)
        ap = ap[:, 0:int(np.prod(shape))]
        self.off += n
        if len(shape) == 2:
            return ap.rearrange("p (a b) -> p a b", b=shape[1])
        if len(shape) == 3:
            return ap.rearrange("p (a b c) -> p a b c", b=shape[1], c=shape[2])
        return ap


def build_program():
    nc = bass.Bass("TRN2", target_bir_lowering=False)
    P = Prog()
    DI = {}

    SHAPES = {
        "xT": ([D_MODEL, T], F32), "cT": ([128, 16], F32), "pos": ([1, T], I32), "bada": ([128, 36], F32),
        "gains": ([128, 96], F32), "bgk": ([1, 512], F32), "wgk": ([16, 512], F32), "amask": ([128, 8], F32),
        "umask": ([128, 128], F32), "m2mask": ([128, 128], F32), "wada": ([18, 128, 4096], F32),
        "w13a": ([NM, 128, 4096], F32), "w13b": ([NM, 128, 4096], F32), "w2a": ([22, 128, 4096], F32), "w2b": ([22, 128, 4096], F32),
        "win_lat": ([4, 128, 4096], F32), "win_kr": ([1, 128, 2048], F32), "win_gqk": ([4, 128, 4096], F32),
        "win_gv": ([4, 128, 4096], F32), "win_glr": ([1, 128, 2048], F32), "win_gout": ([4, 128, 4096], F32),
        "win_gab": ([16, 128, 4096], F32), "wuq": ([2, 128, 4096], F32), "wukv": ([2, 128, 4096], F32),
        "wpa": ([4, 128, 4096], F32), "wpb": ([4, 128, 4096], F32), "wout": ([8, 128, 4096], F32),
    }

    class _Lazy:
        def __init__(self, name):
            self.name = name

        def _ap(self):
            if self.name not in DI:
                shape, dt = SHAPES[self.name]
                DI[self.name] = nc.dram_tensor(self.name, list(shape), dt, kind="ExternalInput").ap()
            return DI[self.name]

        def __getitem__(self, k):
            if P.stopped:
                return None
            return self._ap()[k]

        def rearrange(self, *a, **k):
            if P.stopped:
                return None
            return self._ap().rearrange(*a, **k)

        def partition_broadcast(self, n):
            if P.stopped:
                return None
            return self._ap().partition_broadcast(n)

    xT_d, cT_d, pos_d, bada_d, gains_d = _Lazy("xT"), _Lazy("cT"), _Lazy("pos"), _Lazy("bada"), _Lazy("gains")
    bgk_d, wgk_d, amask_d, umask_d, m2mask_d = _Lazy("bgk"), _Lazy("wgk"), _Lazy("amask"), _Lazy("umask"), _Lazy("m2mask")
    wada_d = _Lazy("wada")
    w13_d = [_Lazy("w13a"), _Lazy("w13b")]
    w2_d = [_Lazy("w2a"), _Lazy("w2b")]
    win_lat_d, win_kr_d, win_gqk_d, win_gv_d = _Lazy("win_lat"), _Lazy("win_kr"), _Lazy("win_gqk"), _Lazy("win_gv")
    win_glr_d, win_gout_d, win_gab_d = _Lazy("win_glr"), _Lazy("win_gout"), _Lazy("win_gab")
    wuq_d, wukv_d, wpa_d, wpb_d, wout_d = _Lazy("wuq"), _Lazy("wukv"), _Lazy("wpa"), _Lazy("wpb"), _Lazy("wout")
    outT_d = nc.dram_tensor("outT", [D_MODEL, T], F32, kind="ExternalOutput").ap()
    xs_d = nc.dram_tensor("xs", [D_MODEL, T], F32).ap()
    modb_d = nc.dram_tensor("modb", [128, 36], F32).ap()
    modg_d = nc.dram_tensor("modg", [512, 36], F32).ap()
    kb_k = [nc.dram_tensor(f"kb_k{i}", [512, T], BF16).ap() for i in range(2)]
    kg_k = [nc.dram_tensor(f"kg_k{i}", [2048, T], BF16).ap() for i in range(2)]
    kb_v = [nc.dram_tensor(f"kb_v{i}", [512, T], BF16).ap() for i in range(2)]
    kg_v = [nc.dram_tensor(f"kg_v{i}", [2048, T], BF16).ap() for i in range(2)]
    kb_pe = nc.dram_tensor("kb_pe", [64, T], BF16).ap()
    kg_pe = nc.dram_tensor("kg_pe", [256, T], BF16).ap()
    glb_d = nc.dram_tensor("glb", [128, 1028], F32).ap()
    glg_d = nc.dram_tensor("glg", [512, 1028], F32).ap()
    dbg_list = []

    with ExitStack() as es:
        def sbt(name, shape, dt):
            return es.enter_context(nc.sbuf_tensor(name, shape, dt))
        slots = sbt("slots", [128, NS, SLOT_E], BF16)
        xb = sbt("xb", [128, NKC, T], F32)
        hb = sbt("hb", [128, NKC, T], BF16)
        cst = sbt("cst", [128, 2048], F32)
        ARENA_B = 52 * 1024
        arena_t = sbt("arena", [128, ARENA_B // 4], F32)
        PS = [es.enter_context(nc.psum_tensor(f"ps{i}", [128, 512], F32)) for i in range(8)]
        for e in ENGS:
            P.sem[e] = es.enter_context(nc.semaphore("s_" + e))
        dnames = [f"w{i}" for i in range(NS)] + ["ld0", "ld1", "ld2", "ld3", "st0", "st1", "st2", "kvk0", "kvk1", "kvp0", "kvp1", "kvv0", "kvv1", "cc0", "cc1", "cc2", "cc3", "cc4", "cc5", "cc6", "xio"]
        for nm in dnames:
            P.dsem[nm] = [es.enter_context(nc.semaphore("d_" + nm)), 0]
        block = es.enter_context(nc.Block())

        AR = Bump(arena_t, ARENA_B)
        XA = Bump(xb[:].rearrange("p a b -> p (a b)"), NKC * T * 4)
        CS = Bump(cst, 2048 * 4)

        sd = [Dep() for _ in range(NS)]
        xd = [[Dep(), Dep()] for _ in range(NKC)]
        hd = [[Dep(), Dep()] for _ in range(NKC)]
        psd = [Dep() for _ in range(8)]
        xall = [d for r in xd for d in r]
        hall = [d for r in hd for d in r]

        def act(out, in_, func, reads, writes, **kw):
            return P.op(ACT, lambda h: h.activation(out=out, in_=in_, func=func, **kw), reads=reads, writes=writes)

        def vtt(out, in0, in1, op, reads, writes, eng=DVE):
            return P.op(eng, lambda h: h.tensor_tensor(out=out, in0=in0, in1=in1, op=op), reads=reads, writes=writes)

        def vts(out, in0, s1, s2, op0, op1, reads, writes, eng=DVE):
            if s2 is None:
                return P.op(eng, lambda h: h.tensor_scalar(out=out, in0=in0, scalar1=s1, scalar2=None, op0=op0), reads=reads, writes=writes)
            return P.op(eng, lambda h: h.tensor_scalar(out=out, in0=in0, scalar1=s1, scalar2=s2, op0=op0, op1=op1), reads=reads, writes=writes)

        def vstt(out, in0, scalar, in1, op0, op1, reads, writes, eng=DVE):
            return P.op(eng, lambda h: h.scalar_tensor_tensor(out=out, in0=in0, scalar=scalar, in1=in1, op0=op0, op1=op1), reads=reads, writes=writes)

        def vcopy(out, in_, reads, writes, eng=DVE):
            return P.op(eng, lambda h: h.tensor_copy(out=out, in_=in_), reads=reads, writes=writes)

        def vmemset(ap, val, writes, eng=DVE):
            return P.op(eng, lambda h: h.memset(ap, val), writes=writes)

        def vrecip(out, in_, reads, writes):
            return P.op(DVE, lambda h: h.reciprocal(out=out, in_=in_), reads=reads, writes=writes)

        def mmg(out_ap, pairs, reads, writes):
            n = len(pairs)

            def fn(h):
                ins = None
                for i, (l, r) in enumerate(pairs):
                    ins = h.matmul(out_ap, lhsT=l, rhs=r, start=(i == 0), stop=(i == n - 1))
                return ins
            return P.op(PE, fn, reads=reads, writes=writes)

        def mm1(out_ap, l, r, start, stop, reads, writes):
            return P.op(PE, lambda h: h.matmul(out_ap, lhsT=l, rhs=r, start=start, stop=stop), reads=reads, writes=writes)

        def ld(semname, out, in_, writes, reads=(), eng=SP):
            return P.dma(eng, semname, lambda h: h.dma_start(out=out, in_=in_), reads=reads, writes=writes)

        wstate = {"u": 0}

        def wload(src, E):
            s = wstate["u"] % NS
            wstate["u"] += 1
            P.dma(POOL, f"w{s}", lambda h: h.dma_start(out=slots[:, s, 0:E], in_=src), writes=[sd[s]], track=False)
            return s

        def coll(semname, in_ap, out_ap, reads, writes):
            return P.dma(POOL, semname, lambda h: h.collective_compute(
                "AllGather", ALU.bypass, replica_groups=GROUPS, ins=[in_ap], outs=[out_ap]),
                reads=reads, writes=writes, inc=1, track=False)

        def checkpoint(k):
            if STOP == k and not P.stopped:
                P.barrier()
                P.stopped = True

        psrr = {"i": 0}

        def nextps(pool):
            b = pool[psrr["i"] % len(pool)]
            psrr["i"] += 1
            return b

        def dbg(name, ap, shape, reads):
            if not DEBUG or P.stopped:
                return
            d = nc.dram_tensor("dbg_" + name, list(shape), ap.dtype, kind="ExternalOutput").ap()
            dbg_list.append("dbg_" + name)
            P.dma(SP, "xio", lambda h: h.dma_start(out=d, in_=ap), reads=reads)

        GT = CS.alloc([96], F32)
        MOD = CS.alloc([144], F32)
        AB = CS.alloc([3, 16], F32)
        GG = CS.alloc([3, 16], F32)
        BADA = CS.alloc([36], F32)
        MODS = CS.alloc([36], F32)
        AMASK = CS.alloc([8], F32)
        CTT = CS.alloc([16], F32)
        SCB = CS.alloc([16], BF16)
        ONES = CS.alloc([128], BF16)
        ONESF = CS.alloc([128], F32)
        ON2048 = CS.alloc([128], BF16)
        ON512 = CS.alloc([128], BF16)
        ON256 = CS.alloc([128], BF16)
        ON128 = CS.alloc([128], BF16)
        ON64 = CS.alloc([128], BF16)
        U01 = CS.alloc([128], F32)
        US = CS.alloc([128], F32)
        M2S = CS.alloc([128], F32)
        DECAY = CS.alloc([4, 16], F32)
        GSC = CS.alloc([4], F32)
        dconst = Dep()
        ddec = Dep()
        dmod = Dep()
        dgt = Dep()
        ld("ld0", GT, gains_d[:, :], [dgt])
        ld("ld1", CTT, cT_d[:, :], [dconst])
        ld("ld2", BADA, bada_d[:, :], [dconst])
        ld("ld3", AMASK, amask_d[:, :], [dconst])
        ld("ld0", U01, umask_d[:, :], [dconst])
        ld("ld1", M2S, m2mask_d[:, :], [dconst])
        for ap_, v in ((ONESF, 1.0), (ONES, 1.0), (ON2048, 1.0 / 2048), (ON512, 1.0 / 512), (ON256, 1.0 / 256), (ON128, 1.0 / 128), (ON64, 1.0 / 64)):
            vmemset(ap_, v, [dconst])
        vts(US, U01, -1.0 / 16, None, ALU.mult, None, [dconst], [dconst])
        vts(M2S, M2S, -1.0 / 16, None, ALU.mult, None, [dconst], [dconst])

        xv = xT_d.rearrange("(k p) t -> p k t", p=128)
        for kc in range(0, NKC, 4):
            ld("xio", xb[:, kc:kc + 4, :], xv[:, kc:kc + 4, :], [d for k in range(kc, kc + 4) for d in xd[k]])

        act(SCB, CTT, AF.Silu, [dconst], [dconst])
        modps = PS[0]
        for j in range(18):
            s = wload(wada_d[j], 4096)
            for c in range(2):
                m = 2 * j + c
                mmg(modps[:, m:m + 1], [(slots[:, s, k * 256 + c * 128:k * 256 + c * 128 + 128], SCB[:, k:k + 1]) for k in range(16)],
                    [sd[s], dconst], [psd[0]])
        vtt(MODS, modps[:, 0:36], BADA, ALU.add, [psd[0], dconst], [dmod])
        dmodb, dmodg = Dep(), Dep()
        P.dma(SP, "st0", lambda h: h.dma_start(out=modb_d[:, :], in_=MODS), reads=[dmod], writes=[dmodb])
        coll("cc0", modb_d, modg_d, [dmodb], [dmodg])
        ld("ld2", MOD.rearrange("p (r m) -> p r m", r=4), modg_d.rearrange("(r p) m -> p r m", p=128), [dmod], reads=[dmodg])
        gain_col = [0, 16, 32]
        for i in range(3):
            vstt(AB[:, i, :], MOD[:, (3 * i + 1) * 16:(3 * i + 2) * 16], 1.0, GT[:, gain_col[i]:gain_col[i] + 16], ALU.add, ALU.mult,
                 [dmod, dgt], [dmod])
            vts(GG[:, i, :], MOD[:, (3 * i + 2) * 16:(3 * i + 3) * 16], 0.5 if i != 1 else 1.0, None, ALU.mult, None, [dmod], [dmod])

        checkpoint(0)

        def rstd_from(ss_ps, out_ap, ssdep, outdep, npart=128):
            act(out_ap, ss_ps, AF.Sqrt, [ssdep], [outdep], bias=EPS)
            vrecip(out_ap, out_ap, [outdep], [outdep])

        def modnorm(Acol, Bcol, ar, out_fp32_inplace=False):
            sq = ar.alloc([2, 512], BF16)
            tmp = ar.alloc([2, 512], F32)
            rstd = ar.alloc([2, 512], F32)
            dsq = [Dep(), Dep()]
            dtmp = [Dep(), Dep()]
            drs = [Dep(), Dep()]
            for tt in range(2):
                ts_ = slice(tt * 512, (tt + 1) * 512)
                ssb = 6 + tt
                for kc in range(NKC):
                    b = kc % 2
                    act(sq[:, b, :], xb[:, kc, ts_], AF.Square, [xd[kc][tt]], [dsq[b]])
                    mm1(PS[ssb][:], ON2048, sq[:, b, :], kc == 0, kc == NKC - 1, [dsq[b], dconst], [psd[ssb]])
                rstd_from(PS[ssb][:], rstd[:, tt, :], psd[ssb], drs[tt])
                for kc in range(NKC):
                    b = kc % 2
                    if out_fp32_inplace:
                        vstt(xb[:, kc, ts_], xb[:, kc, ts_], Acol(kc), rstd[:, tt, :], ALU.mult, ALU.mult,
                             [xd[kc][tt], drs[tt], dmod, dgt], [xd[kc][tt]])
                    else:
                        vstt(tmp[:, b, :], xb[:, kc, ts_], Acol(kc), rstd[:, tt, :], ALU.mult, ALU.mult,
                             [xd[kc][tt], drs[tt], dmod], [dtmp[b]])
                        act(hb[:, kc, ts_], tmp[:, b, :], AF.Identity, [dtmp[b], dmod], [hd[kc][tt]], bias=Bcol(kc))

        def ffn(i, li):
            AR.reset()
            modnorm(lambda kc: AB[:, li, kc:kc + 1], lambda kc: MOD[:, 3 * li * 16 + kc:3 * li * 16 + kc + 1], AR)
            g = AR.alloc([4, T], BF16)
            sil = AR.alloc([2, 512], F32)
            dg = [[Dep(), Dep()] for _ in range(4)]
            dsil = [Dep(), Dep()]
            up_pool = [0, 1, 2, 3]
            dn_pool = [4, 5, 6, 7]
            cnt = 0
            for G in range(11):
                for mi in range(4):
                    m = 4 * G + mi
                    s = wload(w13_d[i][m], 4096)
                    for tt in range(2):
                        ts_ = slice(tt * 512, (tt + 1) * 512)
                        b1 = up_pool[(cnt * 2) % 4]
                        b3 = up_pool[(cnt * 2 + 1) % 4]
                        cnt += 1
                        mmg(PS[b1][:], [(slots[:, s, k * 128:(k + 1) * 128], hb[:, k, ts_]) for k in range(NKC)],
                            [sd[s]] + [hd[k][tt] for k in range(NKC)], [psd[b1]])
                        mmg(PS[b3][:], [(slots[:, s, 2048 + k * 128:2048 + (k + 1) * 128], hb[:, k, ts_]) for k in range(NKC)],
                            [sd[s]] + [hd[k][tt] for k in range(NKC)], [psd[b3]])
                        sb_ = cnt % 2
                        act(sil[:, sb_, :], PS[b1][:], AF.Silu, [psd[b1]], [dsil[sb_]])
                        vtt(g[:, mi, ts_], sil[:, sb_, :], PS[b3][:], ALU.mult, [dsil[sb_], psd[b3]], [dg[mi][tt]])
                s2 = [wload(w2_d[i][2 * G + u], 4096) for u in range(2)]
                for n in range(NKC):
                    for tt in range(2):
                        ts_ = slice(tt * 512, (tt + 1) * 512)
                        b = nextps(dn_pool)
                        mmg(PS[b][:], [(slots[:, s2[mi // 2], (mi % 2) * 2048 + n * 128:(mi % 2) * 2048 + (n + 1) * 128], g[:, mi, ts_]) for mi in range(4)],
                            [sd[s2[0]], sd[s2[1]]] + [dg[mi][tt] for mi in range(4)], [psd[b]])
                        vstt(xb[:, n, ts_], PS[b][:], GG[:, li, n:n + 1], xb[:, n, ts_], ALU.mult, ALU.add,
                             [psd[b], xd[n][tt], dmod], [xd[n][tt]])
            P.barrier()

        ffn(0, 0)
        checkpoint(1)

        AR.reset()
        modnorm(lambda kc: AB[:, 1, kc:kc + 1], lambda kc: MOD[:, 3 * 16 + kc:3 * 16 + kc + 1], AR)
        P.barrier()
        dxs = Dep()
        for kc in range(0, NKC, 4):
            P.dma(SP, "xio", lambda h, kc=kc: h.dma_start(out=xs_d.rearrange("(k p) t -> p k t", p=128)[:, kc:kc + 4, :], in_=xb[:, kc:kc + 4, :]),
                  reads=[d for k in range(kc, kc + 4) for d in xd[k]], writes=[dxs])
        P.barrier()
        checkpoint(2)

        AR.reset()
        XA.reset()
        TCq = AR.alloc([1, T], F32)[:, 0, :]
        TSq = AR.alloc([1, T], F32)[:, 0, :]
        cq = AR.alloc([4, T], BF16)
        a_mark = AR.off
        TCk = XA.alloc([1, T], F32)[:, 0, :]
        TSk = XA.alloc([1, T], F32)[:, 0, :]
        posi = AR.alloc([1, T], I32)[:, 0, :]
        ang = AR.alloc([1, T], F32)[:, 0, :]
        kf = AR.alloc([1, T], F32)[:, 0, :]
        ki = AR.alloc([1, T], I32)[:, 0, :]
        r1 = AR.alloc([1, T], F32)[:, 0, :]
        r2 = AR.alloc([1, T], F32)[:, 0, :]
        sinv = AR.alloc([1, T], F32)[:, 0, :]
        cosv = AR.alloc([1, T], F32)[:, 0, :]
        drope = Dep()
        R = slice(0, 64)
        ld("ld3", posi[R], pos_d.partition_broadcast(64), [drope])
        vcopy(ang[R], posi[R], [drope], [drope])
        vts(ang[R], ang[R], GT[R, 81:82], None, ALU.mult, None, [drope, dgt], [drope])
        C1 = 6.28125
        C2 = float(2 * np.pi - 6.28125)
        vts(kf[R], ang[R], float(1.0 / (2 * np.pi)), None, ALU.mult, None, [drope], [drope])
        vcopy(ki[R], kf[R], [drope], [drope])
        vcopy(kf[R], ki[R], [drope], [drope])
        vstt(r1[R], kf[R], -C1, ang[R], ALU.mult, ALU.add, [drope], [drope])
        vstt(r1[R], kf[R], -C2, r1[R], ALU.mult, ALU.add, [drope], [drope])
        PI_SAFE = 3.1415925
        vts(r1[R], r1[R], PI_SAFE, -PI_SAFE, ALU.min, ALU.max, [drope], [drope])
        act(sinv[R], r1[R], AF.Sin, [drope], [drope])
        vts(r2[R], r1[R], float(np.pi / 2), None, ALU.add, None, [drope], [drope])
        vts(kf[R], r2[R], float(np.pi), None, ALU.is_gt, None, [drope], [drope])
        vstt(r2[R], kf[R], float(-2 * np.pi), r2[R], ALU.mult, ALU.add, [drope], [drope])
        vts(r2[R], r2[R], PI_SAFE, -PI_SAFE, ALU.min, ALU.max, [drope], [drope])
        act(cosv[R], r2[R], AF.Sin, [drope], [drope])
        vtt(GSC[R, 0:1], GT[R, 77:78], GT[R, 80:81], ALU.mult, [dgt], [drope])
        vtt(GSC[R, 1:2], GT[R, 79:80], GT[R, 80:81], ALU.mult, [dgt], [drope])
        vts(TCq[R], cosv[R], GT[R, 76:77], None, ALU.mult, None, [drope, dgt], [drope])
        vts(TSq[R], sinv[R], GSC[R, 0:1], None, ALU.mult, None, [drope], [drope])
        vts(TCk[R], cosv[R], GT[R, 78:79], None, ALU.mult, None, [drope, dgt], [drope])
        vts(TSk[R], sinv[R], GSC[R, 1:2], None, ALU.mult, None, [drope], [drope])

        P.barrier()
        AR.reset(a_mark)
        lat = AR.alloc([4, T], F32)
        kst = AR.alloc([8, T], BF16)
        ckv = XA.alloc([4, T], BF16)
        sq = XA.alloc([2, 512], BF16)
        rstd = XA.alloc([2, 512], F32)
        vst = XA.alloc([8, T], BF16)
        kraw = XA.alloc([1, T], F32)[:, 0, :]
        ksw = XA.alloc([1, T], F32)[:, 0, :]
        kpe = XA.alloc([1, T], BF16)[:, 0, :]
        t1 = XA.alloc([2, 512], F32)
        dlat = [[Dep(), Dep()] for _ in range(4)]
        dsq = [Dep(), Dep()]
        drs = [Dep(), Dep()]
        dcq = [[Dep(), Dep()] for _ in range(4)]
        dckv = [[Dep(), Dep()] for _ in range(4)]
        a_pool = [0, 1, 2, 3, 4, 5]
        sqc = 0
        for fam, (dst, ddst, gcol) in enumerate(((cq, dcq, 64), (ckv, dckv, 68))):
            for u in range(2):
                s = wload(win_lat_d[2 * fam + u], 4096)
                for c in range(2):
                    m = 2 * u + c
                    for tt in range(2):
                        ts_ = slice(tt * 512, (tt + 1) * 512)
                        b = nextps(a_pool)
                        mmg(PS[b][:], [(slots[:, s, k * 256 + c * 128:k * 256 + c * 128 + 128], hb[:, k, ts_]) for k in range(NKC)],
                            [sd[s]] + [hd[k][tt] for k in range(NKC)], [psd[b]])
                        act(lat[:, m, ts_], PS[b][:], AF.Copy, [psd[b]], [dlat[m][tt]])
                        sb_ = sqc % 2
                        sqc += 1
                        act(sq[:, sb_, :], PS[b][:], AF.Square, [psd[b]], [dsq[sb_]])
                        mm1(PS[6 + tt][:], ON512, sq[:, sb_, :], m == 0, m == 3, [dsq[sb_], dconst], [psd[6 + tt]])
            for tt in range(2):
                ts_ = slice(tt * 512, (tt + 1) * 512)
                rstd_from(PS[6 + tt][:], rstd[:, tt, :], psd[6 + tt], drs[tt])
                for m in range(4):
                    vstt(dst[:, m, ts_], lat[:, m, ts_], GT[:, gcol + m:gcol + m + 1], rstd[:, tt, :], ALU.mult, ALU.mult,
                         [dlat[m][tt], drs[tt], dgt], [ddst[m][tt]])
        dkr = Dep()
        s = wload(win_kr_d[0], 2048)
        for tt in range(2):
            ts_ = slice(tt * 512, (tt + 1) * 512)
            b0 = nextps(a_pool)
            mmg(PS[b0][0:64, :], [(slots[:, s, k * 128:k * 128 + 64], hb[:, k, ts_]) for k in range(NKC)],
                [sd[s]] + [hd[k][tt] for k in range(NKC)], [psd[b0]])
            b1 = nextps(a_pool)
            mmg(PS[b1][0:64, :], [(slots[:, s, k * 128 + 64:k * 128 + 128], hb[:, k, ts_]) for k in range(NKC)],
                [sd[s]] + [hd[k][tt] for k in range(NKC)], [psd[b1]])
            act(kraw[R, ts_], PS[b0][0:64, :], AF.Copy, [psd[b0]], [dkr])
            act(ksw[R, ts_], PS[b1][0:64, :], AF.Copy, [psd[b1]], [dkr])
            act(sq[R, 0, :], PS[b0][0:64, :], AF.Square, [psd[b0]], [dsq[0]])
            mm1(PS[6][0:64, :], ON64[R, 0:64], sq[R, 0, :], True, True, [dsq[0], dconst], [psd[6]])
            rstd_from(PS[6][0:64, :], rstd[R, 0, :], psd[6], drs[0])
            vtt(t1[R, 0, :], kraw[R, ts_], TCk[R, ts_], ALU.mult, [dkr, drope], [dkr])
            vtt(t1[R, 1, :], ksw[R, ts_], TSk[R, ts_], ALU.mult, [dkr, drope], [dkr])
            vtt(t1[R, 0, :], t1[R, 0, :], t1[R, 1, :], ALU.add, [dkr], [dkr])
            vtt(kpe[R, ts_], t1[R, 0, :], rstd[R, 0, :], ALU.mult, [dkr, drs[0]], [dkr])
        dkvb = Dep()
        P.dma(SP, "st0", lambda h: h.dma_start(out=kb_pe[:, :], in_=kpe[R, :]), reads=[dkr], writes=[dkvb])
        dkst = Dep()
        s = wload(wukv_d[0], 4096)
        for hh in range(8):
            for tt in range(2):
                ts_ = slice(tt * 512, (tt + 1) * 512)
                b = nextps(a_pool)
                mmg(PS[b][:], [(slots[:, s, k * 1024 + hh * 128:k * 1024 + hh * 128 + 128], ckv[:, k, ts_]) for k in range(4)],
                    [sd[s]] + [dckv[k][tt] for k in range(4)], [psd[b]])
                sb_ = sqc % 2
                sqc += 1
                act(sq[:, sb_, :], PS[b][:], AF.Square, [psd[b]], [dsq[sb_]])
                mm1(PS[6 + sb_][:], ON128, sq[:, sb_, :], True, True, [dsq[sb_], dconst], [psd[6 + sb_]])
                rstd_from(PS[6 + sb_][:], rstd[:, sb_, :], psd[6 + sb_], drs[sb_])
                vstt(kst[:, hh, ts_], PS[b][:], GT[:, 73:74], rstd[:, sb_, :], ALU.mult, ALU.mult, [psd[b], drs[sb_], dgt], [dkst])
        for i in range(2):
            P.dma(SP, "st1", lambda h, i=i: h.dma_start(out=kb_k[i].rearrange("(h p) t -> p h t", p=128), in_=kst[:, 4 * i:4 * i + 4, :]), reads=[dkst], writes=[dkvb])
        dvst = Dep()
        s = wload(wukv_d[1], 4096)
        vst4 = vst.rearrange("p h (tb d) -> p h tb d", d=128)
        for tb in range(8):
            for half in range(2):
                b = nextps(a_pool)
                mmg(PS[b][:], [(ckv[:, k, tb * 128:(tb + 1) * 128], slots[:, s, k * 1024 + half * 512:k * 1024 + half * 512 + 512]) for k in range(4)],
                    [sd[s]] + [dckv[k][tb // 4] for k in range(4)], [psd[b]])
                vcopy(vst4[:, half * 4:half * 4 + 4, tb, :], PS[b][:].rearrange("p (h d) -> p h d", d=128), [psd[b]], [dvst])
        for i in range(2):
            P.dma(SP, "st2", lambda h, i=i: h.dma_start(out=kb_v[i].rearrange("(h p) c -> p h c", p=128), in_=vst[:, 4 * i:4 * i + 4, :]), reads=[dvst], writes=[dkvb])
        dbg("cq", cq, [128, 4, T], [d for r_ in dcq for d in r_])
        dbg("kst", kst, [128, 8, T], [dkst])
        dbg("kpe", kpe[R, :], [64, T], [dkr])
        dbg("vst", vst, [128, 8, T], [dvst])
        dkvg = Dep()
        pre_glr = wload(win_glr_d[0], 2048)
        pre_gv = [wload(win_gv_d[u], 4096) for u in range(2)]
        dkg_k, dkg_v, dkg_pe = [Dep(), Dep()], [Dep(), Dep()], Dep()
        coll("cc1", kb_k[0], kg_k[0], [dkvb], [dkg_k[0]])
        coll("cc4", kb_v[0], kg_v[0], [dkvb], [dkg_v[0]])
        coll("cc6", kb_pe, kg_pe, [dkvb], [dkg_pe])
        coll("cc3", kb_k[1], kg_k[1], [dkvb], [dkg_k[1]])
        coll("cc5", kb_v[1], kg_v[1], [dkvb], [dkg_v[1]])
        P.barrier()
        checkpoint(3)

        XA.reset()
        AR.reset(a_mark)
        lsp = AR.alloc([8, 512], F32)
        bT = AR.alloc([4, T], F32)
        glrT = AR.alloc([1, T], F32)[:, 0, :]
        kstate = XA.alloc([8, 512], BF16)
        vtm = XA.alloc([8, T], BF16)
        qdec = XA.alloc([4, T], BF16)
        kdec = XA.alloc([4, T], BF16)
        b_mark = XA.off
        wgkp = XA.alloc([1, 512], F32)[:, 0, :]
        bgkb = XA.alloc([1, 512], F32)[:, 0, :]
        zt = XA.alloc([2, 512], F32)
        Sst = XA.alloc([4, 2, 256], F32)
        glst = XA.alloc([1, 1028], F32)[:, 0, :]
        dglr, dwgk, dlsp, dbT = Dep(), Dep(), [Dep() for _ in range(8)], [[Dep(), Dep()] for _ in range(4)]
        dks = [Dep() for _ in range(8)]
        dvt = [Dep() for _ in range(8)]
        dzt = [Dep(), Dep()]
        vmemset(wgkp, 0.0, [dwgk])
        ld("ld0", wgkp[0:16, :], wgk_d[:, :], [dwgk])
        ld("ld1", bgkb, bgk_d.partition_broadcast(128), [dwgk])
        b_pool = [0, 1, 2, 3]
        s = pre_glr
        for tt in range(2):
            ts_ = slice(tt * 512, (tt + 1) * 512)
            b = nextps(b_pool)
            mmg(PS[b][:], [(slots[:, s, k * 128:(k + 1) * 128], hb[:, k, ts_]) for k in range(NKC)],
                [sd[s]] + [hd[k][tt] for k in range(NKC)], [psd[b]])
            act(glrT[:, ts_], PS[b][:], AF.Copy, [psd[b]], [dglr])
        for tb in range(8):
            b = nextps(b_pool)
            mm1(PS[b][:], glrT[:, tb * 128:(tb + 1) * 128], wgkp, True, True, [dglr, dwgk], [psd[b]])
            vtt(zt[:, 0, :], PS[b][:], bgkb, ALU.add, [psd[b], dwgk], [dzt[0]])
            act(zt[:, 1, :], zt[:, 0, :], AF.Exp, [dzt[0]], [dzt[1]], scale=-1.0)
            act(lsp[:, tb, :], zt[:, 1, :], AF.Ln, [dzt[1]], [dlsp[tb]], bias=1.0)
            for hh in range(4):
                bb = 4 + hh
                tt = tb // 4
                mm1(PS[bb][:, (tb % 4) * 128:(tb % 4 + 1) * 128], lsp[:, tb, hh * 128:(hh + 1) * 128], US, True, True,
                    [dlsp[tb], dconst], [psd[bb]])
            if tb % 4 == 3:
                tt = tb // 4
                for hh in range(4):
                    vcopy(bT[:, hh, tt * 512:(tt + 1) * 512], PS[4 + hh][:], [psd[4 + hh]], [dbT[hh][tt]])
        for tb in range(8):
            b = nextps(b_pool)
            mm1(PS[b][:], M2S, lsp[:, tb, :], True, True, [dlsp[tb], dconst], [psd[b]])
            act(lsp[:, tb, :], PS[b][:], AF.Exp, [psd[b]], [dlsp[tb]])
        for hh in range(4):
            bl = bT[:, hh, :].rearrange("p (n s) -> p n s", s=64)[:, :, 63]
            act(DECAY[:, hh, :], bl, AF.Exp, [dbT[hh][0], dbT[hh][1]], [ddec])
        for u in range(4):
            s = pre_gv[u] if u < 2 else wload(win_gv_d[u], 4096)
            for tb in range(8):
                b = nextps(b_pool)
                mmg(PS[b][:, 0:256], [(hb[:, k, tb * 128:(tb + 1) * 128], slots[:, s, k * 256:(k + 1) * 256]) for k in range(NKC)],
                    [sd[s]] + [hd[k][tb // 4] for k in range(NKC)], [psd[b]])
                act(vtm[:, tb, u * 256:(u + 1) * 256], PS[b][:, 0:256], AF.Copy, [psd[b]], [dvt[tb]])
        dqd = [[Dep(), Dep()] for _ in range(4)]
        dkd = [[Dep(), Dep()] for _ in range(4)]
        for u in range(4):
            s = wload(win_gqk_d[u], 4096)
            isk = u >= 2
            if isk:
                half = u - 2
                for tb in range(8):
                    b = nextps(b_pool)
                    mmg(PS[b][:, 0:256], [(hb[:, k, tb * 128:(tb + 1) * 128], slots[:, s, k * 256:(k + 1) * 256]) for k in range(NKC)],
                        [sd[s]] + [hd[k][tb // 4] for k in range(NKC)], [psd[b]])
                    vtt(kstate[:, tb, half * 256:(half + 1) * 256], PS[b][:, 0:256], lsp[:, tb, half * 256:(half + 1) * 256], ALU.mult,
                        [psd[b], dlsp[tb]], [dks[tb]])
            for c in range(2):
                hh = 2 * (u % 2) + c
                for tt in range(2):
                    ts_ = slice(tt * 512, (tt + 1) * 512)
                    b = nextps(b_pool)
                    mmg(PS[b][:], [(slots[:, s, k * 256 + c * 128:k * 256 + c * 128 + 128], hb[:, k, ts_]) for k in range(NKC)],
                        [sd[s]] + [hd[k][tt] for k in range(NKC)], [psd[b]])
                    zb = (hh * 2 + tt) % 2
                    if not isk:
                        act(zt[:, zb, :], bT[:, hh, ts_], AF.Exp, [dbT[hh][tt]], [dzt[zb]])
                        vstt(qdec[:, hh, ts_], PS[b][:], float(128 ** -0.5), zt[:, zb, :], ALU.mult, ALU.mult, [psd[b], dzt[zb]], [dqd[hh][tt]])
                    else:
                        act(zt[:, zb, :], bT[:, hh, ts_], AF.Exp, [dbT[hh][tt]], [dzt[zb]], scale=-1.0)
                        vtt(kdec[:, hh, ts_], PS[b][:], zt[:, zb, :], ALU.mult, [psd[b], dzt[zb]], [dkd[hh][tt]])
        dS = [[Dep(), Dep()] for _ in range(4)]
        dgl = Dep()
        for hh in range(4):
            vmemset(Sst[:, hh, 0, :], 0.0, [dS[hh][0]])
        for n in range(16):
            tb, j = n // 2, n % 2
            for hh in range(4):
                b = nextps(b_pool)
                mm1(PS[b][:, 0:256], kstate[64 * j:64 * j + 64, tb, hh * 128:(hh + 1) * 128], vtm[64 * j:64 * j + 64, tb, hh * 256:(hh + 1) * 256],
                    True, True, [dks[tb], dvt[tb]], [psd[b]])
                src, dst = n % 2, (n + 1) % 2
                out_ap = Sst[:, hh, dst, :] if n < 15 else glst[:, hh * 257:hh * 257 + 256]
                vstt(out_ap, Sst[:, hh, src, :], DECAY[:, hh, n:n + 1], PS[b][:, 0:256], ALU.mult, ALU.add,
                     [dS[hh][src], psd[b], ddec], [dS[hh][dst]] if n < 15 else [dgl])
        for hh in range(4):
            bl = bT[:, hh, :].rearrange("p (n s) -> p n s", s=64)[:, :, 63]
            P.op(DVE, lambda h, bl=bl, hh=hh: h.reduce_sum(out=glst[:, hh * 257 + 256:hh * 257 + 257], in_=bl, axis=mybir.AxisListType.X),
                 reads=[dbT[hh][0], dbT[hh][1]], writes=[dgl])
            act(glst[:, hh * 257 + 256:hh * 257 + 257], glst[:, hh * 257 + 256:hh * 257 + 257], AF.Exp, [dgl], [dgl])
        dglb, dglg = Dep(), Dep()
        P.dma(SP, "st0", lambda h: h.dma_start(out=glb_d[:, :], in_=glst), reads=[dgl], writes=[dglb])
        coll("cc2", glb_d, glg_d, [dglb], [dglg])
        dbg("kstate", kstate, [128, 8, 512], dks)
        dbg("qdec", qdec, [128, 4, T], [d for r_ in dqd for d in r_])
        dbg("kdec", kdec, [128, 4, T], [d for r_ in dkd for d in r_])
        dbg("glst", glst, [128, 1028], [dgl])
        checkpoint(4)

        P.barrier()
        AR.reset(a_mark)
        glaT = AR.alloc([8, T], BF16)
        c_mark = AR.off
        dgla = [[Dep(), Dep()] for _ in range(8)]
        dmla = [[Dep(), Dep()] for _ in range(8)]
        glgs = AR.alloc([4, 1028], F32)
        XA.reset(b_mark)
        osb = XA.alloc([2, T], F32)
        Sp = XA.alloc([2, 256], F32)
        Sbf = XA.alloc([2, 256], BF16)
        attm = XA.alloc([2, 128], BF16)
        sgt = XA.alloc([2, 512], F32)
        dmt = XA.alloc([1, 8], F32)[:, 0, :]
        sqg = XA.alloc([2, 512], BF16)
        rsg = XA.alloc([2, 512], F32)
        dglgs = Dep()
        ld("ld2", glgs, glg_d.rearrange("(r p) c -> p r c", p=128), [dglgs], reads=[dglg])
        dSp = [Dep(), Dep()]
        dSbf = [Dep(), Dep()]
        datt = [Dep(), Dep()]
        dosb = [[Dep(), Dep()] for _ in range(2)]
        dsg = [Dep(), Dep()]
        dsqg = [Dep(), Dep()]
        drsg = [Dep(), Dep()]
        ddm = Dep()
        c_pool = [0, 1, 2, 3]
        o_pool = [4, 5]
        sbc = 0
        for hh in range(4):
            vmemset(Sp[:, 0, :], 0.0, [dSp[0]])
            for r in range(4):
                Dr = glgs[:, r, hh * 257 + 256:hh * 257 + 257]
                Lr = glgs[:, r, hh * 257:hh * 257 + 256]
                ar_ = AMASK[:, 3 + r:4 + r]
                vts(dmt[:, 0:1], Dr, -1.0, None, ALU.add, None, [dglgs], [ddm])
                vts(dmt[:, 0:1], dmt[:, 0:1], ar_, 1.0, ALU.mult, ALU.add, [ddm, dconst], [ddm])
                vts(Lr, Lr, ar_, None, ALU.mult, None, [dglgs, dconst], [dglgs])
                vstt(Sp[:, 0, :], Sp[:, 0, :], dmt[:, 0:1], Lr, ALU.mult, ALU.add, [dSp[0], ddm, dglgs], [dSp[0]])
            cur = 0
            for tb in range(8):
                tt = tb // 4
                cols = slice(tb * 128, (tb + 1) * 128)
                b = nextps(c_pool)
                mm1(PS[b][:, 0:128], kdec[:, hh, cols], qdec[:, hh, cols], True, True, [dkd[hh][tt], dqd[hh][tt]], [psd[b]])
                ab = tb % 2
                vtt(attm[:, ab, :], PS[b][:, 0:128], U01, ALU.mult, [psd[b], dconst], [datt[ab]])
                sbfs = []
                for j in range(2):
                    n = 2 * tb + j
                    sb_ = sbc % 2
                    sbc += 1
                    act(Sbf[:, sb_, :], Sp[:, cur, :], AF.Copy, [dSp[cur]], [dSbf[sb_]])
                    sbfs.append(sb_)
                    b2 = nextps(c_pool)
                    mm1(PS[b2][:, 0:256], kstate[64 * j:64 * j + 64, tb, hh * 128:(hh + 1) * 128], vtm[64 * j:64 * j + 64, tb, hh * 256:(hh + 1) * 256],
                        True, True, [dks[tb], dvt[tb]], [psd[b2]])
                    nxt = 1 - cur
                    vstt(Sp[:, nxt, :], Sp[:, cur, :], DECAY[:, hh, n:n + 1], PS[b2][:, 0:256], ALU.mult, ALU.add,
                         [dSp[cur], psd[b2], ddec], [dSp[nxt]])
                    cur = nxt
                for half in range(2):
                    ob = o_pool[half]
                    c0 = (tb % 4) * 128

                    def ofn(h, ob=ob, c0=c0, half=half, tb=tb, hh=hh, ab=ab, sbfs=tuple(sbfs), cols=cols):
                        h.matmul(PS[ob][:, c0:c0 + 128], lhsT=vtm[:, tb, hh * 256 + half * 128:hh * 256 + half * 128 + 128], rhs=attm[:, ab, :], start=True, stop=False)
                        h.matmul(PS[ob][:, c0:c0 + 64], lhsT=Sbf[:, sbfs[0], half * 128:(half + 1) * 128], rhs=qdec[:, hh, tb * 128:tb * 128 + 64], start=False, stop=False)
                        return h.matmul(PS[ob][:, c0 + 64:c0 + 128], lhsT=Sbf[:, sbfs[1], half * 128:(half + 1) * 128], rhs=qdec[:, hh, tb * 128 + 64:tb * 128 + 128], start=False, stop=True)
                    P.op(PE, ofn, reads=[dvt[tb], datt[ab], dSbf[0], dSbf[1], dqd[hh][tt]], writes=[psd[ob]])
                if tb % 4 == 3:
                    for half in range(2):
                        act(osb[:, half, tt * 512:(tt + 1) * 512], PS[o_pool[half]][:], AF.Copy, [psd[o_pool[half]]], [dosb[half][tt]])
            s = wload(win_gout_d[hh], 4096)
            for tt in range(2):
                ts_ = slice(tt * 512, (tt + 1) * 512)
                for half in range(2):
                    act(sqg[:, half, :], osb[:, half, ts_], AF.Square, [dosb[half][tt]], [dsqg[half]])
                    mm1(PS[6][:], ON256, sqg[:, half, :], half == 0, half == 1, [dsqg[half], dconst], [psd[6]])
                rstd_from(PS[6][:], rsg[:, tt, :], psd[6], drsg[tt])
                for half in range(2):
                    b = nextps(c_pool)
                    mmg(PS[b][:], [(slots[:, s, k * 256 + half * 128:k * 256 + half * 128 + 128], hb[:, k, ts_]) for k in range(NKC)],
                        [sd[s]] + [hd[k][tt] for k in range(NKC)], [psd[b]])
                    act(sgt[:, half, :], PS[b][:], AF.Silu, [psd[b]], [dsg[half]])
                    vstt(osb[:, half, ts_], osb[:, half, ts_], GT[:, 74 + half:75 + half], rsg[:, tt, :], ALU.mult, ALU.mult,
                         [dosb[half][tt], drsg[tt], dgt], [dosb[half][tt]])
                    vtt(glaT[:, hh * 2 + half, ts_], osb[:, half, ts_], sgt[:, half, :], ALU.mult, [dosb[half][tt], dsg[half]], [dgla[hh * 2 + half][tt]])
        P.barrier()

        dbg("glaT", glaT, [128, 8, T], [d for r_ in dgla for d in r_])
        checkpoint(5)
        XA.reset()
        AR.reset(c_mark)
        mlaT = AR.alloc([8, T], BF16)
        qn = XA.alloc([2, T], BF16)
        qpe = XA.alloc([2, T], BF16)
        qraw = XA.alloc([2, 512], F32)
        t2 = XA.alloc([2, 512], F32)
        sqd = XA.alloc([2, 512], BF16)
        rsd = XA.alloc([2, 512], F32)
        kTb = XA.alloc([2, T], BF16)
        kpb = XA.alloc([2, T], BF16)
        vb = XA.alloc([2, T], BF16)
        PT = XA.alloc([4, 512], BF16)
        rden = XA.alloc([2, 512], F32)
        pacc = XA.alloc([2, 512], F32)
        dpacc = [Dep(), Dep()]
        dqn = [[Dep(), Dep()] for _ in range(2)]
        dqp = [[Dep(), Dep()] for _ in range(2)]
        dqraw, dt2 = [Dep(), Dep()], Dep()
        dsqd, drsd = [Dep(), Dep()], [Dep(), Dep()]
        dkT, dkp, dvb = [Dep(), Dep()], [Dep(), Dep()], [Dep(), Dep()]
        dPT = [Dep() for _ in range(4)]
        drden = [Dep(), Dep()]
        s_pool = [0, 1]
        QB0, QB1 = 6, 7
        SCALE = float(192 ** -0.5)
        dstate = {"ptc": 0, "sq": 0, "wq": None}

        def qproj(hh):
            if hh % 4 == 0:
                dstate["wq"] = wload(wuq_d[hh // 4], 4096)
            wq_slot = dstate["wq"]
            hq = hh % 2
            base = (hh % 4) * 256
            for tt in range(2):
                ts_ = slice(tt * 512, (tt + 1) * 512)
                mmg(PS[QB0][:], [(slots[:, wq_slot, k * 1024 + base:k * 1024 + base + 128], cq[:, k, ts_]) for k in range(4)],
                    [sd[wq_slot]] + [dcq[k][tt] for k in range(4)], [psd[QB0]])
                sb_ = dstate["sq"] % 2
                dstate["sq"] += 1
                act(sqd[:, sb_, :], PS[QB0][:], AF.Square, [psd[QB0]], [dsqd[sb_]])
                mm1(PS[QB1][:], ON128, sqd[:, sb_, :], True, True, [dsqd[sb_], dconst], [psd[QB1]])
                rstd_from(PS[QB1][:], rsd[:, sb_, :], psd[QB1], drsd[sb_])
                vts(rsd[:, sb_, :], rsd[:, sb_, :], SCALE, None, ALU.mult, None, [drsd[sb_]], [drsd[sb_]])
                vstt(qn[:, hq, ts_], PS[QB0][:], GT[:, 72:73], rsd[:, sb_, :], ALU.mult, ALU.mult, [psd[QB0], drsd[sb_], dgt], [dqn[hq][tt]])
                mmg(PS[QB0][0:64, :], [(slots[:, wq_slot, k * 1024 + base + 128:k * 1024 + base + 192], cq[:, k, ts_]) for k in range(4)],
                    [sd[wq_slot]] + [dcq[k][tt] for k in range(4)], [psd[QB0]])
                act(qraw[R, 0, :], PS[QB0][0:64, :], AF.Copy, [psd[QB0]], [dqraw[0]])
                sb_ = dstate["sq"] % 2
                dstate["sq"] += 1
                act(sqd[R, sb_, :], PS[QB0][0:64, :], AF.Square, [psd[QB0]], [dsqd[sb_]])
                mm1(PS[QB1][0:64, :], ON64[R, 0:64], sqd[R, sb_, :], True, True, [dsqd[sb_], dconst], [psd[QB1]])
                rstd_from(PS[QB1][0:64, :], rsd[R, sb_, :], psd[QB1], drsd[sb_])
                vts(rsd[R, sb_, :], rsd[R, sb_, :], SCALE, None, ALU.mult, None, [drsd[sb_]], [drsd[sb_]])
                mmg(PS[QB0][0:64, :], [(slots[:, wq_slot, k * 1024 + base + 192:k * 1024 + base + 256], cq[:, k, ts_]) for k in range(4)],
                    [sd[wq_slot]] + [dcq[k][tt] for k in range(4)], [psd[QB0]])
                vtt(t2[R, 0, :], qraw[R, 0, :], TCq[R, ts_], ALU.mult, [dqraw[0], drope], [dt2])
                vtt(t2[R, 1, :], PS[QB0][0:64, :], TSq[R, ts_], ALU.mult, [psd[QB0], drope], [dt2])
                vtt(t2[R, 0, :], t2[R, 0, :], t2[R, 1, :], ALU.add, [dt2], [dt2])
                vtt(qpe[R, hq, ts_], t2[R, 0, :], rsd[R, sb_, :], ALU.mult, [dt2, drsd[sb_]], [dqp[hq][tt]])

        def kvload(idx):
            hh, slot = idx // 4, idx % 4
            kb_ = idx % 2
            hi, hl = hh // 4, hh % 4
            if slot < 3:
                ksrc, vsrc, psrc, rdeps = kg_k[hi], kg_v[hi], kg_pe, [dkvg]
                kr0, pr0 = slot * 512 + hl * 128, slot * 64
            else:
                ksrc, vsrc, psrc, rdeps = kb_k[hi], kb_v[hi], kb_pe, [dkvb]
                kr0, pr0 = hl * 128, 0
            rk, rp, rv = ([dkg_k[hi]], [dkg_pe], [dkg_v[hi]]) if slot < 3 else (rdeps, rdeps, rdeps)
            P.dma(SP, f"kvk{kb_}", lambda h: h.dma_start(out=kTb[:, kb_, :], in_=ksrc[kr0:kr0 + 128, :]), reads=rk, writes=[dkT[kb_]])
            P.dma(SP, f"kvp{kb_}", lambda h: h.dma_start(out=kpb[R, kb_, :], in_=psrc[pr0:pr0 + 64, :]), reads=rp, writes=[dkp[kb_]])
            P.dma(SP, f"kvv{kb_}", lambda h: h.dma_start(out=vb[:, kb_, :], in_=vsrc[kr0:kr0 + 128, :]), reads=rv, writes=[dvb[kb_]])

        qproj(0)
        kvload(0)
        for hh in range(8):
            hq = hh % 2
            first = [True, True]
            for slot in range(4):
                idx = hh * 4 + slot
                kb_ = idx % 2
                if idx + 1 < 32:
                    kvload(idx + 1)
                if slot == 1 and hh + 1 < 8:
                    qproj(hh + 1)
                items = []
                for tt in range(2):
                    for kb in range(8):
                        q0, diag = 0, False
                        if slot == 3:
                            if kb * 128 >= (tt + 1) * 512:
                                continue
                            if kb * 128 >= tt * 512:
                                q0, diag = kb * 128 - tt * 512, True
                        last = (slot == 3) and (kb == 4 * tt + 3)
                        items.append((tt, kb, q0, diag, last))

                def emit_s(it):
                    tt, kb, q0, diag, last = it
                    sbk = nextps(s_pool)
                    qs = slice(tt * 512 + q0, (tt + 1) * 512)
                    mmg(PS[sbk][:, q0:512],
                        [(kTb[:, kb_, kb * 128:(kb + 1) * 128], qn[:, hq, qs]), (kpb[R, kb_, kb * 128:(kb + 1) * 128], qpe[R, hq, qs])],
                        [dkT[kb_], dkp[kb_], dqn[hq][tt], dqp[hq][tt]], [psd[sbk]])
                    return sbk

                def emit_rest(it, sbk):
                    tt, kb, q0, diag, last = it
                    ob, db = 2 + tt, 4 + tt
                    pb = dstate["ptc"] % 4
                    dstate["ptc"] += 1
                    if slot < 3:
                        act(PT[:, pb, q0:512], PS[sbk][:, q0:512], AF.Exp, [psd[sbk], dconst], [dPT[pb]], bias=AMASK[:, slot:slot + 1])
                    else:
                        act(PT[:, pb, q0:512], PS[sbk][:, q0:512], AF.Exp, [psd[sbk]], [dPT[pb]])
                    if diag:
                        vmemset(PT[64:128, pb, q0:q0 + 64], 0.0, [dPT[pb]])
                    st = first[tt]
                    first[tt] = False
                    mm1(PS[ob][:, q0:512], vb[:, kb_, kb * 128:(kb + 1) * 128], PT[:, pb, q0:512], st, last, [dvb[kb_], dPT[pb]], [psd[ob]])
                    if st:
                        vcopy(pacc[:, tt, :], PT[:, pb, :], [dPT[pb]], [dpacc[tt]], eng=POOL)
                    else:
                        vtt(pacc[:, tt, q0:512], pacc[:, tt, q0:512], PT[:, pb, q0:512], ALU.add, [dPT[pb], dpacc[tt]], [dpacc[tt]], eng=POOL)

                pend = emit_s(items[0])
                for i_, it in enumerate(items):
                    cur = pend
                    if i_ + 1 < len(items):
                        pend = emit_s(items[i_ + 1])
                    emit_rest(it, cur)
            for tt in range(2):
                ts_ = slice(tt * 512, (tt + 1) * 512)
                mm1(PS[4 + tt][:], ONESF, pacc[:, tt, :], True, True, [dpacc[tt], dconst], [psd[4 + tt]])
                vrecip(rden[:, tt, :], PS[4 + tt][:], [psd[4 + tt]], [drden[tt]])
                vtt(mlaT[:, hh, ts_], PS[2 + tt][:], rden[:, tt, :], ALU.mult, [psd[2 + tt], drden[tt]], [dmla[hh][tt]])
        P.barrier()

        dbg("mlaT", mlaT, [128, 8, T], [d for r_ in dmla for d in r_])
        checkpoint(6)
        for kc in range(0, NKC, 4):
            ld("xio", xb[:, kc:kc + 4, :], xs_d.rearrange("(k p) t -> p k t", p=128)[:, kc:kc + 4, :],
               [d for k in range(kc, kc + 4) for d in xd[k]], reads=[dxs])
        AR.reset(0)
        mg = AR.alloc([4, T], BF16)
        sga = AR.alloc([1, 512], F32)[:, 0, :]
        sgb = AR.alloc([1, 512], F32)[:, 0, :]
        ta = AR.alloc([1, 512], F32)[:, 0, :]
        tbb = AR.alloc([1, 512], F32)[:, 0, :]
        assert AR.off <= a_mark
        dmg = [[Dep(), Dep()] for _ in range(4)]
        dsga, dsgb, dta, dtb = Dep(), Dep(), Dep(), Dep()
        e_pool = [0, 1, 2, 3]
        e2_pool = [4, 5, 6, 7]
        for G in range(4):
            for u in range(2):
                spa = wload(wpa_d[G], 4096)
                spb = wload(wpb_d[G], 4096)
                sga_s = wload(win_gab_d[2 * G + u], 4096)
                sgb_s = wload(win_gab_d[8 + 2 * G + u], 4096)
                for c in range(2):
                    ni = 2 * u + c
                    for tt in range(2):
                        ts_ = slice(tt * 512, (tt + 1) * 512)
                        bya, byb, bga, bgb = 0, 1, 2, 3
                        mmg(PS[bya][:], [(slots[:, spa, k * 512 + ni * 128:k * 512 + ni * 128 + 128], mlaT[:, k, ts_]) for k in range(8)],
                            [sd[spa]] + [dmla[k][tt] for k in range(8)], [psd[bya]])
                        mmg(PS[byb][:], [(slots[:, spb, k * 512 + ni * 128:k * 512 + ni * 128 + 128], glaT[:, k, ts_]) for k in range(8)],
                            [sd[spb]] + [dgla[k][tt] for k in range(8)], [psd[byb]])
                        mmg(PS[bga][:], [(slots[:, sga_s, k * 256 + c * 128:k * 256 + c * 128 + 128], hb[:, k, ts_]) for k in range(NKC)],
                            [sd[sga_s]] + [hd[k][tt] for k in range(NKC)], [psd[bga]])
                        mmg(PS[bgb][:], [(slots[:, sgb_s, k * 256 + c * 128:k * 256 + c * 128 + 128], hb[:, k, ts_]) for k in range(NKC)],
                            [sd[sgb_s]] + [hd[k][tt] for k in range(NKC)], [psd[bgb]])
                        act(sga, PS[bga][:], AF.Sigmoid, [psd[bga]], [dsga])
                        act(sgb, PS[bgb][:], AF.Sigmoid, [psd[bgb]], [dsgb])
                        vtt(ta, PS[bya][:], sga, ALU.mult, [psd[bya], dsga], [dta])
                        vtt(tbb, PS[byb][:], sgb, ALU.mult, [psd[byb], dsgb], [dtb])
                        vtt(mg[:, ni, ts_], ta, tbb, ALU.add, [dta, dtb], [dmg[ni][tt]])
            so = [wload(wout_d[2 * G + u], 4096) for u in range(2)]
            for n in range(NKC):
                for tt in range(2):
                    ts_ = slice(tt * 512, (tt + 1) * 512)
                    b = nextps(e2_pool)
                    mmg(PS[b][:], [(slots[:, so[ni // 2], (ni % 2) * 2048 + n * 128:(ni % 2) * 2048 + (n + 1) * 128], mg[:, ni, ts_]) for ni in range(4)],
                        [sd[so[0]], sd[so[1]]] + [dmg[ni][tt] for ni in range(4)], [psd[b]])
                    vstt(xb[:, n, ts_], PS[b][:], GG[:, 1, n:n + 1], xb[:, n, ts_], ALU.mult, ALU.add, [psd[b], xd[n][tt], dmod], [xd[n][tt]])
        P.barrier()

        dbg("x2", xb[:], [128, NKC, T], xall)
        checkpoint(7)
        ffn(1, 2)
        checkpoint(8)

        AR.reset()
        modnorm(lambda kc: GT[:, 48 + kc:49 + kc], None, AR, out_fp32_inplace=True)
        P.stopped = False
        ov = outT_d.rearrange("(k p) t -> p k t", p=128)
        for kc in range(0, NKC, 4):
            tok = P.dma(SP, "xio", lambda h, kc=kc: h.dma_start(out=ov[:, kc:kc + 4, :], in_=xb[:, kc:kc + 4, :]),
                        reads=[d for k in range(kc, kc + 4) for d in xd[k]])
        P._wait(SP, tok)
        P.emit(block)
    _CACHE['used'] = list(DI.keys())
    return nc, dbg_list


def _colunits(W, col_lists, kch):
    out = []
    for cols in col_lists:
        sub = W[:, cols]
        n = sub.shape[1]
        out.append(sub.reshape(kch, 128, n).transpose(1, 0, 2).reshape(128, kch * n))
    return np.ascontiguousarray(np.stack(out)).astype(np.float32, copy=False)


def _rowunits(W, nrow_chunks_per_unit):
    K, N = W.shape
    nu = K // (128 * nrow_chunks_per_unit)
    a = W.reshape(nu, nrow_chunks_per_unit, 128, N).transpose(0, 2, 1, 3).reshape(nu, 128, nrow_chunks_per_unit * N)
    return np.ascontiguousarray(a).astype(np.float32, copy=False)


def _percol(v, n):
    return np.ascontiguousarray(np.asarray(v, np.float32).reshape(n, 128).T)


_CACHE = {}


def kernel(x, c, positions, w_ada, b_ada, g_ffn1, w1_a, w3_a, w2_a, g_mix, w_in,
           g_q_lat, w_uq, g_qn, g_qr, g_kv_lat, w_ukv, g_kn, g_kr, w_gk_up, b_gk, g_gla,
           w_proj_a, w_proj_b, w_out, g_ffn2, w1_b, w3_b, w2_b, g_final):
    f = lambda a: np.asarray(a)
    x, c, positions = f(x), f(c), f(positions)
    w_ada, b_ada = f(w_ada)[0], f(b_ada)[0]
    w_in = f(w_in)[0]
    w_uq, w_ukv = f(w_uq)[0], f(w_ukv)[0]
    ar = np.arange
    shared = {}
    for tag, (w1, w3, w2) in (("a", (w1_a, w3_a, w2_a)), ("b", (w1_b, w3_b, w2_b))):
        w1, w3, w2 = f(w1)[0], f(w3)[0], f(w2)[0]
        u1 = _colunits(w1, [ar(m * 128, (m + 1) * 128) for m in range(NM)], 16)
        u3 = _colunits(w3, [ar(m * 128, (m + 1) * 128) for m in range(NM)], 16)
        shared["w13" + tag] = np.ascontiguousarray(np.concatenate([u1, u3], axis=2))
        shared["w2" + tag] = _rowunits(w2, 2)
    shared["win_lat"] = _colunits(w_in, [ar(i * 256, (i + 1) * 256) for i in range(4)], 16)
    shared["win_kr"] = _colunits(w_in, [np.concatenate([ar(1024, 1088), ar(1056, 1088), ar(1024, 1056)])], 16)
    shared["win_gqk"] = _colunits(w_in, [ar(1088 + i * 256, 1088 + (i + 1) * 256) for i in range(4)], 16)
    shared["win_gv"] = _colunits(w_in, [ar(2112 + i * 256, 2112 + (i + 1) * 256) for i in range(4)], 16)
    shared["win_glr"] = _colunits(w_in, [ar(3136, 3264)], 16)
    shared["win_gout"] = _colunits(w_in, [ar(3152 + i * 256, 3152 + (i + 1) * 256) for i in range(4)], 16)
    shared["win_gab"] = _colunits(w_in, [ar(4176 + i * 256, 4176 + (i + 1) * 256) for i in range(16)], 16)
    uq_cols = []
    for g in range(2):
        cols = []
        for hh in range(4 * g, 4 * g + 4):
            b0 = hh * 192
            cols += [ar(b0, b0 + 128), ar(b0 + 128, b0 + 192), ar(b0 + 160, b0 + 192), ar(b0 + 128, b0 + 160)]
        uq_cols.append(np.concatenate(cols))
    shared["wuq"] = _colunits(w_uq, uq_cols, 4)
    shared["wukv"] = _colunits(w_ukv, [np.concatenate([ar(hh * 256, hh * 256 + 128) for hh in range(8)]),
                                       np.concatenate([ar(hh * 256 + 128, hh * 256 + 256) for hh in range(8)])], 4)
    shared["wpa"] = _colunits(f(w_proj_a)[0], [ar(i * 512, (i + 1) * 512) for i in range(4)], 8)
    shared["wpb"] = _colunits(f(w_proj_b)[0], [ar(i * 512, (i + 1) * 512) for i in range(4)], 8)
    shared["wout"] = _rowunits(f(w_out)[0], 2)
    gains = np.zeros((128, 96), np.float32)
    gains[:, 0:16] = _percol(f(g_ffn1)[0], 16)
    gains[:, 16:32] = _percol(f(g_mix)[0], 16)
    gains[:, 32:48] = _percol(f(g_ffn2)[0], 16)
    gains[:, 48:64] = _percol(f(g_final)[0], 16)
    gains[:, 64:68] = _percol(f(g_q_lat)[0], 4)
    gains[:, 68:72] = _percol(f(g_kv_lat)[0], 4)
    gains[:, 72] = f(g_qn)[0]
    gains[:, 73] = f(g_kn)[0]
    gains[:, 74:76] = _percol(f(g_gla)[0], 2)
    gqr, gkr = f(g_qr)[0], f(g_kr)[0]
    gains[:64, 76] = gqr
    gains[:64, 77] = np.concatenate([gqr[32:], gqr[:32]])
    gains[:64, 78] = gkr
    gains[:64, 79] = np.concatenate([gkr[32:], gkr[:32]])
    gains[:32, 80] = -1.0
    gains[32:64, 80] = 1.0
    inv_freq = (10000.0 ** (-np.arange(0, 64, 2, dtype=np.float32) / 64)).astype(np.float32)
    gains[:64, 81] = np.concatenate([inv_freq, inv_freq])
    shared["gains"] = gains
    shared["bgk"] = np.ascontiguousarray(f(b_gk)[0].reshape(1, 512).astype(np.float32))
    shared["wgk"] = np.ascontiguousarray(f(w_gk_up)[0].astype(np.float32))
    idx = np.arange(128)
    same = (idx[:, None] // 64) == (idx[None, :] // 64)
    shared["umask"] = (same & (idx[:, None] <= idx[None, :])).astype(np.float32)
    shared["m2mask"] = (same & (idx[:, None] > idx[None, :])).astype(np.float32)

    if "nc" not in _CACHE:
        _CACHE["nc"] = build_program()
    nc, dbg_list = _CACHE["nc"]
    in_maps = []
    for core in range(8):
        b, r = core // 4, core % 4
        m = dict(shared)
        m["xT"] = np.ascontiguousarray(x[b, r * T:(r + 1) * T, :].T)
        m["cT"] = _percol(c[b], 16)
        m["pos"] = np.ascontiguousarray(positions[b, r * T:(r + 1) * T].reshape(1, T).astype(np.int32))
        m["bada"] = _percol(b_ada[r * 4608:(r + 1) * 4608], 36)
        m["wada"] = _colunits(w_ada, [ar(r * 4608 + j * 256, r * 4608 + (j + 1) * 256) for j in range(18)], 16)
        am = np.zeros((128, 8), np.float32)
        for s_ in range(3):
            am[:, s_] = 0.0 if s_ < r else -30000.0
        for s_ in range(4):
            am[:, 3 + s_] = 1.0 if s_ < r else 0.0
        m["amask"] = am
        in_maps.append(m)
    used = set(_CACHE['used'])
    in_maps = [{k: v for k, v in m.items() if k in used} for m in in_maps]
    res = run_bass_kernel_spmd(nc, in_maps, core_ids=list(range(8)))
    out = np.empty((2, 4096, D_MODEL), np.float32)
    for core in range(8):
        b, r = core // 4, core % 4
        out[b, r * T:(r + 1) * T, :] = res.results[core]["outT"].T
    if DEBUG:
        _CACHE["dbg"] = [{k: res.results[core][k] for k in dbg_list} for core in range(8)]
    return out
```
